# Optimizing a Trainium2 kernel written in Bass

```python
import math
import jax, jax.numpy as jnp
from jax import lax
import numpy as np

D_MODEL = 2048
BATCH = 1
SEQ = 16384
DEPTH = 2

N_MEM = 256
MIX_W = D_MODEL // 2
GLA_HEADS = 4
GLA_DV = MIX_W // GLA_HEADS
GLA_DK = GLA_DV // 2
GLA_RANK = 16
GLA_TAU = 16.0
GLA_CHUNK = 64
SB_HEADS = 8
SB_DH = MIX_W // SB_HEADS
SB_BLOCK = 128
MEM_HEADS = 4
MEM_DH = MIX_W // MEM_HEADS
N_BRANCH = 3
D_FF = int(math.ceil(8 * D_MODEL / 3 / 256) * 256)
EPS = 1e-6

SPLIT_SIZES = [
    GLA_HEADS * GLA_DK,
    GLA_HEADS * GLA_DK,
    GLA_HEADS * GLA_DV,
    GLA_HEADS * GLA_DV,
    GLA_RANK,
    SB_HEADS * SB_DH,
    SB_HEADS * SB_DH,
    SB_HEADS * SB_DH,
    MEM_HEADS * MEM_DH,
    N_BRANCH * D_MODEL,
]
IN_W = int(sum(SPLIT_SIZES))
SPLIT_POINTS = [int(p) for p in np.cumsum(SPLIT_SIZES)[:-1]]

kernel_name = "hybrid_gla_stickbreaking_memxattn_gated"


def rmsnorm(x, g):
    xf = x.astype(jnp.float32)
    y = xf * lax.rsqrt(jnp.mean(xf * xf, axis=-1, keepdims=True) + EPS)
    return (y * g.astype(jnp.float32)).astype(x.dtype)


def gla_mixer(q, k, v, log_a):
    B, T, H, dk = q.shape
    dv = v.shape[-1]
    N = T // GLA_CHUNK

    def chunk(a):
        return a.astype(jnp.float32).reshape(B, N, GLA_CHUNK, H, a.shape[-1]).transpose(0, 3, 1, 2, 4)

    qc = chunk(q) * (dk ** -0.5)
    kc, vc, gc = chunk(k), chunk(v), chunk(log_a)
    b = jnp.cumsum(gc, axis=3)
    b_last = b[:, :, :, -1:, :]
    q_e = qc * jnp.exp(b)
    k_e = kc * jnp.exp(-b)
    k_d = kc * jnp.exp(b_last - b)
    decay = jnp.exp(b_last[:, :, :, 0, :])

    causal = jnp.tril(jnp.ones((GLA_CHUNK, GLA_CHUNK), dtype=bool))
    s = jnp.where(causal, jnp.einsum('bhncd,bhnsd->bhncs', q_e, k_e), 0.0)
    o_intra = jnp.einsum('bhncs,bhnsv->bhncv', s, vc)

    def step(S, inp):
        qe, kd, vv, dec = inp
        o = jnp.einsum('bhcd,bhdv->bhcv', qe, S)
        S = S * dec[..., :, None] + jnp.einsum('bhcd,bhcv->bhdv', kd, vv)
        return S, o

    xs = (jnp.moveaxis(q_e, 2, 0), jnp.moveaxis(k_d, 2, 0), jnp.moveaxis(vc, 2, 0), jnp.moveaxis(decay, 2, 0))
    S0 = jnp.zeros((B, H, dk, dv), jnp.float32)
    _, o_inter = lax.scan(step, S0, xs)
    o = o_intra + jnp.moveaxis(o_inter, 0, 2)
    return o.transpose(0, 2, 3, 1, 4).reshape(B, T, H, dv)


def stick_breaking_mixer(q, k, v):
    B, T, H, d = q.shape
    NB = T // SB_BLOCK
    scale = d ** -0.5
    qt = q.transpose(0, 2, 1, 3)
    kt = k.transpose(0, 2, 1, 3)
    vt = v.transpose(0, 2, 1, 3)
    idx = jnp.arange(SB_BLOCK)
    later = (idx[:, None] > idx[None, :]).astype(jnp.float32)
    outs = []
    for i in range(NB):
        L = (i + 1) * SB_BLOCK
        qi = qt[:, :, i * SB_BLOCK:L]
        kb = kt[:, :, :L]
        vb = vt[:, :, :L]
        z = jnp.einsum('bhqd,bhkd->bhqk', qi, kb).astype(jnp.float32) * scale
        mask = jnp.arange(L)[None, :] < (i * SB_BLOCK + idx)[:, None]
        l = jnp.where(mask, jax.nn.log_sigmoid(-z), 0.0).reshape(B, H, SB_BLOCK, i + 1, SB_BLOCK)
        within = jnp.einsum('bhqnj,js->bhqns', l, later)
        tot = jnp.sum(l, axis=-1)
        after = lax.cumsum(tot, axis=3, reverse=True) - tot
        rest = (within + after[..., None]).reshape(B, H, SB_BLOCK, L)
        A = jnp.where(mask, jnp.exp(jax.nn.log_sigmoid(z) + rest), 0.0)
        outs.append(jnp.einsum('bhqk,bhkd->bhqd', A.astype(vb.dtype), vb))
    out = jnp.concatenate(outs, axis=2)
    return out.transpose(0, 2, 1, 3).reshape(B, T, H * d)


def memory_mixer(q, mem_k, mem_v):
    B, T, H, d = q.shape
    s = jnp.einsum('bthd,bmhd->bhtm', q, mem_k).astype(jnp.float32) * (d ** -0.5)
    p = jax.nn.softmax(s, axis=-1)
    o = jnp.einsum('bhtm,bmhd->bthd', p.astype(mem_v.dtype), mem_v)
    return o.reshape(B, T, H * d)


def setup_inputs(seed: int = 0) -> dict:
    key = jax.random.key(seed)
    ks = jax.random.split(key, 24)
    L, D = DEPTH, D_MODEL

    def w(k, shape, fan_in):
        return jax.random.normal(k, shape, jnp.float32) * (fan_in ** -0.5)

    def gain(k, shape):
        return 1.0 + 0.02 * jax.random.normal(k, shape, jnp.float32)

    return {
        "x": jax.random.normal(ks[0], (BATCH, SEQ, D), jnp.float32),
        "mem": jax.random.normal(ks[1], (BATCH, N_MEM, D), jnp.float32),
        "attn_norm": gain(ks[2], (L, D)),
        "w_in": w(ks[3], (L, D, IN_W), D),
        "gla_w_a2": w(ks[4], (L, GLA_RANK, GLA_HEADS * GLA_DK), GLA_RANK),
        "gla_b_a": 0.1 * jax.random.normal(ks[5], (L, GLA_HEADS * GLA_DK), jnp.float32),
        "gla_out_norm": gain(ks[6], (L, GLA_DV)),
        "w_br_gla": w(ks[7], (L, GLA_HEADS * GLA_DV, D), GLA_HEADS * GLA_DV),
        "sb_q_norm": gain(ks[8], (L, SB_DH)),
        "sb_k_norm": gain(ks[9], (L, SB_DH)),
        "w_br_sb": w(ks[10], (L, SB_HEADS * SB_DH, D), SB_HEADS * SB_DH),
        "mem_norm": gain(ks[11], (L, D)),
        "w_mem_kv": w(ks[12], (L, D, 2 * MEM_HEADS * MEM_DH), D),
        "mem_q_norm": gain(ks[13], (L, MEM_DH)),
        "mem_k_norm": gain(ks[14], (L, MEM_DH)),
        "w_br_mem": w(ks[15], (L, MEM_HEADS * MEM_DH, D), MEM_HEADS * MEM_DH),
        "w_o": w(ks[16], (L, D, D), D),
        "ffn_norm": gain(ks[17], (L, D)),
        "w_gate_up": w(ks[18], (L, D, 2 * D_FF), D),
        "w_down": w(ks[19], (L, D_FF, D), D_FF),
    }


def reference(x, mem, attn_norm, w_in, gla_w_a2, gla_b_a, gla_out_norm, w_br_gla,
              sb_q_norm, sb_k_norm, w_br_sb, mem_norm, w_mem_kv, mem_q_norm, mem_k_norm,
              w_br_mem, w_o, ffn_norm, w_gate_up, w_down):
    B, T, D = x.shape
    M = mem.shape[1]
    for l in range(DEPTH):
        h = rmsnorm(x, attn_norm[l])
        proj = h @ w_in[l]
        (gq, gk, gv, gr, ga1, sq, sk, sv, mq, gates) = jnp.split(proj, SPLIT_POINTS, axis=-1)

        log_a = jax.nn.log_sigmoid((ga1 @ gla_w_a2[l] + gla_b_a[l]).astype(jnp.float32)) / GLA_TAU
        o_gla = gla_mixer(gq.reshape(B, T, GLA_HEADS, GLA_DK), gk.reshape(B, T, GLA_HEADS, GLA_DK),
                          gv.reshape(B, T, GLA_HEADS, GLA_DV), log_a.reshape(B, T, GLA_HEADS, GLA_DK))
        o_gla = rmsnorm(o_gla, gla_out_norm[l]).reshape(B, T, GLA_HEADS * GLA_DV).astype(x.dtype)
        y_gla = (o_gla * jax.nn.silu(gr)) @ w_br_gla[l]

        q_sb = rmsnorm(sq.reshape(B, T, SB_HEADS, SB_DH), sb_q_norm[l])
        k_sb = rmsnorm(sk.reshape(B, T, SB_HEADS, SB_DH), sb_k_norm[l])
        y_sb = stick_breaking_mixer(q_sb, k_sb, sv.reshape(B, T, SB_HEADS, SB_DH)) @ w_br_sb[l]

        mem_kv = rmsnorm(mem, mem_norm[l]) @ w_mem_kv[l]
        m_k, m_v = jnp.split(mem_kv, 2, axis=-1)
        m_k = rmsnorm(m_k.reshape(B, M, MEM_HEADS, MEM_DH), mem_k_norm[l])
        m_v = m_v.reshape(B, M, MEM_HEADS, MEM_DH)
        q_m = rmsnorm(mq.reshape(B, T, MEM_HEADS, MEM_DH), mem_q_norm[l])
        y_mem = memory_mixer(q_m, m_k, m_v) @ w_br_mem[l]

        g = jax.nn.sigmoid(gates.reshape(B, T, N_BRANCH, D))
        merged = g[:, :, 0] * y_gla + g[:, :, 1] * y_sb + g[:, :, 2] * y_mem
        x = x + merged @ w_o[l]

        h2 = rmsnorm(x, ffn_norm[l])
        gate, up = jnp.split(h2 @ w_gate_up[l], 2, axis=-1)
        x = x + (jax.nn.silu(gate) * up) @ w_down[l]
    return x
```

```python
import contextlib
import numpy as np
import ml_dtypes
import concourse.bass as bass
import concourse.mybir as mybir
from concourse.bass_utils import run_bass_kernel_spmd

F32 = mybir.dt.float32
BF16 = mybir.dt.bfloat16
AF = mybir.ActivationFunctionType
ALU = mybir.AluOpType
AX = mybir.AxisListType
NPBF = ml_dtypes.bfloat16

D = 2048
DFF = 5632
NMEM = 256
EPS = 1e-6
NH1 = 5136
NL = 8192
K1 = 2
K3 = 2
L1P = 2048
TB = 256
PRECAST = True


class Sem:
    def __init__(self, h):
        self.h = h
        self.count = 0


class Buf:
    __slots__ = ("w", "r")

    def __init__(self):
        self.w = None
        self.r = []


class FW:
    ENGS = ("pe", "act", "dve", "pool", "sp")
    M = 8

    def __init__(self, nc, stack):
        self.nc = nc
        self.stack = stack
        self.q = {e: [] for e in self.ENGS}
        self.esem = {e: [self.new_sem(f"e_{e}{j}") for j in range(self.M)] for e in self.ENGS}
        self.eidx = {e: 0 for e in self.ENGS}
        self.seen_e = {e: {s: 0 for s in self.ENGS} for e in self.ENGS}
        self.seen_d = {e: {} for e in self.ENGS}
        self.n = 0

    def new_sem(self, name):
        return Sem(self.stack.enter_context(self.nc.semaphore(name)))

    def sbuf(self, name, shape, dt):
        return self.stack.enter_context(self.nc.sbuf_tensor(name, list(shape), dt))

    def psum(self, name, shape, dt):
        return self.stack.enter_context(self.nc.psum_tensor(name, list(shape), dt))

    def _waits(self, eng, deps):
        out = {}
        for t in deps:
            if t is None:
                continue
            kind, src, val = t
            if kind == "e":
                if src == "pe" and eng == "pe":
                    continue
                if self.seen_e[eng][src] >= val:
                    continue
                key = ("e", src)
                if out.get(key, 0) < val:
                    out[key] = val
            else:
                if self.seen_d[eng].get(src, 0) >= val:
                    continue
                key = ("d", src)
                if out.get(key, 0) < val:
                    out[key] = val
        res = []
        for (kind, src), val in out.items():
            if kind == "e":
                self.seen_e[eng][src] = val
                j = (val - 1) % self.M
                res.append((self.esem[src][j].h, (val - 1) // self.M + 1))
            else:
                self.seen_d[eng][src] = val
                res.append((src.h, val))
        return res

    @staticmethod
    def _deps(reads, writes, extra):
        deps = list(extra)
        for b in reads:
            deps.append(b.w)
        for b in writes:
            deps.append(b.w)
            deps.extend(b.r)
        return deps

    @staticmethod
    def _commit(tok, reads, writes):
        for b in reads:
            b.r.append(tok)
            if len(b.r) > 48:
                del b.r[:-48]
        for b in writes:
            b.w = tok
            b.r = []

    def op(self, eng, fn, reads=(), writes=(), deps=()):
        waits = self._waits(eng, self._deps(reads, writes, deps))
        self.eidx[eng] += 1
        idx = self.eidx[eng]
        tok = ("e", eng, idx)
        s = self.esem[eng][(idx - 1) % self.M]
        self.q[eng].append((waits, fn, (s.h, 1)))
        self.n += 1 + len(waits)
        self._commit(tok, reads, writes)
        return tok

    def dma(self, eng, out, in_, sem, reads=(), writes=(), deps=()):
        waits = self._waits(eng, self._deps(reads, writes, deps))
        sem.count += 16
        tok = ("d", sem, sem.count)
        self.q[eng].append((waits, lambda e: e.dma_start(out=out, in_=in_), (sem.h, 16)))
        self.n += 1 + len(waits)
        self._commit(tok, reads, writes)
        return tok

    def wait(self, eng, deps):
        waits = self._waits(eng, deps)
        if waits:
            self.q[eng].append((waits, None, None))

    def emit(self):
        q = self.q

        def run(e, lst):
            for waits, fn, inc in lst:
                for (h, v) in waits:
                    e.wait_ge(h, v)
                if fn is not None:
                    fn(e).then_inc(inc[0], inc[1])

        with self.nc.Block() as block:
            @block.tensor
            def _(e):
                run(e, q["pe"])

            @block.scalar
            def _(e):
                run(e, q["act"])

            @block.vector
            def _(e):
                run(e, q["dve"])

            @block.gpsimd
            def _(e):
                run(e, q["pool"])

            @block.sync
            def _(e):
                run(e, q["sp"])


class Rot:
    def __init__(self, fw, name, shape, dt, n, psum=False, sem=False):
        self.t = [(fw.psum if psum else fw.sbuf)(f"{name}{i}", shape, dt) for i in range(n)]
        self.b = [Buf() for _ in range(n)]
        self.s = [fw.new_sem(f"s_{name}{i}") for i in range(n)] if sem else None
        self.i = -1
        self.n = n

    def next(self):
        self.i = (self.i + 1) % self.n
        if self.s:
            return self.t[self.i], self.b[self.i], self.s[self.i]
        return self.t[self.i], self.b[self.i]


class Common:
    def __init__(self, fw, nc):
        self.fw = fw
        self.nc = nc
        self.ident = fw.sbuf("ident", [128, 128], BF16)
        self.b_ident = Buf()
        self.s_const = fw.new_sem("s_const")
        self.xr = Rot(fw, "xr", [128, D], F32, 2, sem=True)
        self.hb = Rot(fw, "hb", [128, D], BF16, 2)
        self.junk = fw.sbuf("junk", [128, D], BF16)
        self.b_junk = Buf()
        self.st = Rot(fw, "st", [128, 8], F32, 4)
        self.ps = Rot(fw, "ps", [128, 512], F32, 6, psum=True)
        self.pt = Rot(fw, "pt", [128, 1024], BF16, 2, psum=True)
        self.wb = Rot(fw, "wb", [128, 16, 512], BF16, 2, sem=True)

    def load_const(self, dst, src, b):
        self.fw.dma("sp", dst, src, self.s_const, writes=[b])

    def rstd(self, ssq_ap, b_ssq, n, scale, bias):
        fw = self.fw
        st, b_st = self.st.next()
        fw.op("act", lambda e: e.activation(out=st[:, 0:n], in_=ssq_ap, func=AF.Sqrt, scale=scale, bias=bias),
              reads=[b_ssq], writes=[b_st])
        fw.op("dve", lambda e: e.reciprocal(out=st[:, 0:n], in_=st[:, 0:n]), reads=[b_st], writes=[b_st])
        return st, b_st

    def norm_T(self, src, b_src, gain_fm, b_gain, dstT, b_dst, col0):
        fw = self.fw
        ss, b_ss = self.st.next()
        fw.op("act", lambda e: e.activation(out=self.junk[:], in_=src, func=AF.Square, accum_out=ss[:, 0:1]),
              reads=[b_src], writes=[self.b_junk, b_ss])
        rs, b_rs = self.rstd(ss[:, 0:1], b_ss, 1, 1.0 / D, EPS)
        hb, b_hb = self.hb.next()
        fw.op("dve", lambda e: e.tensor_scalar(out=hb[:], in0=src, scalar1=rs[:, 0:1], scalar2=None, op0=ALU.mult),
              reads=[b_src, b_rs], writes=[b_hb])
        self.transposes(hb, b_hb, D // 128, lambda c: dstT[:, c, col0:col0 + 128], b_dst,
                        lambda c: gain_fm[:, c:c + 1], b_gain)

    def transposes(self, src, b_src, nch, dst_fn, b_dst, scale_fn=None, b_scale=None):
        fw = self.fw
        for c0 in range(0, nch, 8):
            pt, b_pt = self.pt.next()
            m = min(8, nch - c0)
            for j in range(m):
                c = c0 + j
                fw.op("pe", lambda e, c=c, j=j, pt=pt: e.transpose(out=pt[:, j * 128:(j + 1) * 128],
                                                                 in_=src[:, c * 128:(c + 1) * 128],
                                                                 identity=self.ident[:]),
                      reads=[b_src, self.b_ident], writes=[b_pt])
            for j in range(m):
                c = c0 + j
                eng = "dve" if j % 2 == 0 else "act"
                rd = [b_pt] + ([b_scale] if b_scale is not None else [])
                if scale_fn is None:
                    if eng == "dve":
                        fw.op("dve", lambda e, c=c, j=j, pt=pt: e.tensor_copy(out=dst_fn(c), in_=pt[:, j * 128:(j + 1) * 128]),
                              reads=rd, writes=[b_dst])
                    else:
                        fw.op("act", lambda e, c=c, j=j, pt=pt: e.activation(out=dst_fn(c), in_=pt[:, j * 128:(j + 1) * 128], func=AF.Copy),
                              reads=rd, writes=[b_dst])
                else:
                    if eng == "dve":
                        fw.op("dve", lambda e, c=c, j=j, pt=pt: e.tensor_scalar(out=dst_fn(c), in0=pt[:, j * 128:(j + 1) * 128],
                                                                               scalar1=scale_fn(c), scalar2=None, op0=ALU.mult),
                              reads=rd, writes=[b_dst])
                    else:
                        fw.op("act", lambda e, c=c, j=j, pt=pt: e.activation(out=dst_fn(c), in_=pt[:, j * 128:(j + 1) * 128],
                                                                            func=AF.Copy, scale=scale_fn(c)),
                              reads=rd, writes=[b_dst])

    def load_w(self, w_ap, k0, nk, n0, ncols):
        rd = []
        if isinstance(w_ap, tuple):
            w_ap, b_w = w_ap
            rd = [b_w]
        wb, b_wb, s_wb = self.wb.next()
        src = w_ap[k0 * 128:(k0 + nk) * 128, n0:n0 + ncols].rearrange("(c p) n -> p c n", p=128)
        self.fw.dma("pool", wb[:, 0:nk, 0:ncols], src, s_wb, reads=rd, writes=[b_wb])
        return wb, b_wb

    def precast(self, name, w_ap, K, N):
        t = self.nc.dram_tensor(name + "_bf", [K, N], BF16)
        b = Buf()
        sem = self.fw.new_sem("s_pc_" + name)
        tok = None
        for r0 in range(0, K, 128):
            tok = self.fw.dma("pool", t[r0:r0 + 128, :], w_ap[r0:r0 + 128, :], sem)
        b.w = tok
        return (t, b)


def build_l1(TOK):
    PT = min(TOK, L1P)
    NTP = PT // 128
    nc = bass.Bass("TRN2", target_bir_lowering=False)
    x = nc.dram_tensor("x", [TOK, D], F32, kind="ExternalInput").ap()
    wh = nc.dram_tensor("wh", [D, NH1], F32, kind="ExternalInput").ap()
    gn = nc.dram_tensor("gn", [128, 16], F32, kind="ExternalInput").ap()
    qg = nc.dram_tensor("qg", [128, 1024], F32, kind="ExternalInput").ap()
    kg = nc.dram_tensor("kg", [128, 1024], F32, kind="ExternalInput").ap()
    idn = nc.dram_tensor("idn", [128, 128], BF16, kind="ExternalInput").ap()
    oh = nc.dram_tensor("oh", [TOK, 5120], BF16, kind="ExternalOutput").ap()
    oa = nc.dram_tensor("oa", [TOK, 16], F32, kind="ExternalOutput").ap()
    with contextlib.ExitStack() as st:
        fw = FW(nc, st)
        cm = Common(fw, nc)
        gn_s = fw.sbuf("gn_s", [128, 16], F32); b_gn = Buf()
        qg_s = fw.sbuf("qg_s", [128, 1024], F32); b_qg = Buf()
        kg_s = fw.sbuf("kg_s", [128, 1024], F32); b_kg = Buf()
        hT = fw.sbuf("hT", [128, 16, PT], BF16); b_hT = Buf()
        sqs = fw.sbuf("sqs", [128, 512], F32); b_sqs = Buf()
        ob = Rot(fw, "ob", [128, 512], BF16, 3, sem=True)
        oab = Rot(fw, "oab", [128, 16], F32, 2, sem=True)
        cm.load_const(cm.ident[:], idn, cm.b_ident)
        cm.load_const(gn_s[:], gn, b_gn)
        cm.load_const(qg_s[:], qg, b_qg)
        cm.load_const(kg_s[:], kg, b_kg)
        outs = []
        nblocks = [(i * 512, 512) for i in range(10)] + [(5120, 16)]
        def l1_body(tb):
            for bi, (n0, ncols) in enumerate(nblocks):
                wb, b_wb = cm.load_w(wh, 0, 16, n0, ncols)
                for t in range(NTP):
                    ps, b_ps = cm.ps.next()
                    for kc in range(16):
                        fw.op("pe", lambda e, kc=kc, t=t, ps=ps, wb=wb, ncols=ncols: e.matmul(
                            ps[:, 0:ncols], lhsT=hT[:, kc, t * 128:(t + 1) * 128], rhs=wb[:, kc, 0:ncols],
                            start=(kc == 0), stop=(kc == 15)), reads=[b_hT, b_wb], writes=[b_ps])
                    if bi < 4:
                        isq = bi < 2
                        g_s, b_g = (qg_s, b_qg) if isq else (kg_s, b_kg)
                        gofs = (bi % 2) * 512
                        fw.op("act", lambda e, ps=ps: e.activation(out=sqs[:], in_=ps[:], func=AF.Square),
                              reads=[b_ps], writes=[b_sqs])
                        ss, b_ss = cm.st.next()
                        fw.op("dve", lambda e, ss=ss: e.tensor_reduce(out=ss[:, 0:4], in_=sqs[:].rearrange("p (h d) -> p h d", h=4),
                                                                     axis=AX.X, op=ALU.add), reads=[b_sqs], writes=[b_ss])
                        if isq:
                            rs, b_rs = cm.rstd(ss[:, 0:4], b_ss, 4, 1.0, 128 * EPS)
                        else:
                            rs, b_rs = cm.rstd(ss[:, 0:4], b_ss, 4, 1.0 / 128, EPS)
                        o, b_o, s_o = ob.next()
                        for h in range(4):
                            fw.op("dve", lambda e, h=h, ps=ps, rs=rs, o=o, g_s=g_s, gofs=gofs: e.scalar_tensor_tensor(
                                out=o[:, h * 128:(h + 1) * 128], in0=ps[:, h * 128:(h + 1) * 128], scalar=rs[:, h:h + 1],
                                in1=g_s[:, gofs + h * 128:gofs + (h + 1) * 128], op0=ALU.mult, op1=ALU.mult),
                                reads=[b_ps, b_rs, b_g], writes=[b_o])
                        outs.append(fw.dma("sp", oh[(tb + t) * 128:(tb + t + 1) * 128, n0:n0 + 512], o[:], s_o, reads=[b_o]))
                    elif bi < 10:
                        o, b_o, s_o = ob.next()
                        if t % 2 == 0:
                            fw.op("act", lambda e, ps=ps, o=o: e.activation(out=o[:], in_=ps[:], func=AF.Copy),
                                  reads=[b_ps], writes=[b_o])
                        else:
                            fw.op("dve", lambda e, ps=ps, o=o: e.tensor_copy(out=o[:], in_=ps[:]), reads=[b_ps], writes=[b_o])
                        outs.append(fw.dma("sp", oh[(tb + t) * 128:(tb + t + 1) * 128, n0:n0 + 512], o[:], s_o, reads=[b_o]))
                    else:
                        o, b_o, s_o = oab.next()
                        fw.op("dve", lambda e, ps=ps, o=o: e.tensor_copy(out=o[:], in_=ps[:, 0:16]), reads=[b_ps], writes=[b_o])
                        outs.append(fw.dma("sp", oa[(tb + t) * 128:(tb + t + 1) * 128, :], o[:], s_o, reads=[b_o]))
        for p0 in range(0, TOK, PT):
          tb = p0 // 128
          for t in range(NTP):
            xt, b_xt, s_xt = cm.xr.next()
            fw.dma("sp", xt[:], x[(tb + t) * 128:(tb + t + 1) * 128, :], s_xt, writes=[b_xt])
            cm.norm_T(xt[:], b_xt, gn_s, b_gn, hT, b_hT, t * 128)
          l1_body(tb)
        fw.wait("sp", outs[-8:])
        fw.emit()
    return nc


def l2_consts():
    i = np.arange(128)
    c = {}
    c["idn"] = np.eye(128, dtype=np.float32).astype(NPBF)
    c["mU"] = (i[:, None] <= i[None, :]).astype(np.float32).astype(NPBF)
    c["Uc"] = ((i[:, None] <= i[None, :]) * (-1.0 / 16)).astype(np.float32)
    c["U2"] = ((i[:, None] > i[None, :]) * (-1.0 / 16)).astype(np.float32)
    c["Lpn"] = (-(i[:, None] >= i[None, :]).astype(np.float32)).astype(NPBF)
    c["On"] = (-np.ones((128, 128), np.float32)).astype(NPBF)
    m4 = np.zeros((4, 128, 512), np.float32)
    tri = (i[:, None] < i[None, :]).astype(np.float32)
    for r in range(4):
        for qb in range(4):
            if qb == r:
                m4[r][:, qb * 128:(qb + 1) * 128] = tri
            elif qb > r:
                m4[r][:, qb * 128:(qb + 1) * 128] = 1.0
    c["M4"] = np.ascontiguousarray(m4.transpose(1, 0, 2)).astype(NPBF)
    return c


def build_l2(T):
    NCH = T // 128
    NG = T // 512
    GC = min(8, NCH)
    nc = bass.Bass("TRN2", target_bir_lowering=False)
    di = lambda n, s, dt: nc.dram_tensor(n, s, dt, kind="ExternalInput").ap()
    sqT = di("sqT", [128, T], BF16); skT = di("skT", [128, T], BF16); sv = di("sv", [T, 128], BF16)
    gqT = di("gqT", [128, T], BF16); gkT = di("gkT", [128, T], BF16)
    gk = di("gk", [T, 128], BF16); gv = di("gv", [T, 128], BF16)
    ga = di("ga", [17, T], F32); wa = di("wa", [17, 128], F32)
    idn = di("idn", [128, 128], BF16); mU = di("mU", [128, 128], BF16)
    Uc = di("Uc", [128, 128], F32); U2 = di("U2", [128, 128], F32)
    Lpn = di("Lpn", [128, 128], BF16); On = di("On", [128, 128], BF16); M4 = di("M4", [128, 4, 512], BF16)
    sbT = nc.dram_tensor("sbT", [128, T], BF16, kind="ExternalOutput").ap()
    go = nc.dram_tensor("go", [T, 128], BF16, kind="ExternalOutput").ap()
    with contextlib.ExitStack() as st:
        fw = FW(nc, st)
        s_const = fw.new_sem("s_const")

        def const(name, src, shape, dt):
            t = fw.sbuf(name, shape, dt); b = Buf()
            fw.dma("sp", t[:], src, s_const, writes=[b])
            return t, b
        mU_s, b_mU = const("mU_s", mU, [128, 128], BF16)
        Uc_s, b_Uc = const("Uc_s", Uc, [128, 128], F32)
        U2_s, b_U2 = const("U2_s", U2, [128, 128], F32)
        Lpn_s, b_Lpn = const("Lpn_s", Lpn, [128, 128], BF16)
        On_s, b_On = const("On_s", On, [128, 128], BF16)
        M4_s, b_M4 = const("M4_s", M4, [128, 4, 512], BF16)
        wa_s, b_wa = const("wa_s", wa, [17, 128], F32)
        sqT_s, b_sqT = const("sqT_s", sqT, [128, T], BF16)
        skT_s, b_skT = const("skT_s", skT, [128, T], BF16)
        sv_s, b_sv = const("sv_s", sv.rearrange("(c p) d -> p c d", p=128), [128, NCH, 128], BF16)
        ps = Rot(fw, "ps", [128, 512], F32, 6, psum=True)
        pso = Rot(fw, "pso", [128, 512], F32, 2, psum=True)

        gq_r = Rot(fw, "gq_r", [128, GC * 128], BF16, 2, sem=True)
        gkT_r = Rot(fw, "gkT_r", [128, GC * 128], BF16, 2, sem=True)
        gk_r = Rot(fw, "gk_r", [128, GC, 128], BF16, 2, sem=True)
        gv_r = Rot(fw, "gv_r", [128, GC, 128], BF16, 2, sem=True)
        ga_r = Rot(fw, "ga_r", [17, GC * 128], F32, 2, sem=True)
        f32t = Rot(fw, "f32t", [128, 128], F32, 6)
        bft = Rot(fw, "bft", [128, 128], BF16, 12)
        gob = Rot(fw, "gob", [128, 128], BF16, 3, sem=True)
        dec_r = Rot(fw, "dec_r", [128, 1], F32, 3)
        S32 = fw.sbuf("S32", [128, 128], F32); b_S32 = Buf()
        Sbf = Rot(fw, "Sbf", [128, 128], BF16, 2)
        fw.op("dve", lambda e: e.memset(S32[:], 0.0), writes=[b_S32])
        sb_cur, b_sb_cur = Sbf.next()
        fw.op("dve", lambda e, t=sb_cur: e.memset(t[:], 0.0), writes=[b_sb_cur])
        outs = []
        grp = None
        for c in range(NCH):
            if c % GC == 0:
                g0 = c * 128
                tq, bq, sq_ = gq_r.next(); fw.dma("sp", tq[:], gqT[:, g0:g0 + GC * 128], sq_, writes=[bq])
                tk, bk, sk_ = gkT_r.next(); fw.dma("sp", tk[:], gkT[:, g0:g0 + GC * 128], sk_, writes=[bk])
                tkk, bkk, skk = gk_r.next(); fw.dma("sp", tkk[:], gk[g0:g0 + GC * 128, :].rearrange("(c p) d -> p c d", p=128), skk, writes=[bkk])
                tv, bv, sv_ = gv_r.next(); fw.dma("sp", tv[:], gv[g0:g0 + GC * 128, :].rearrange("(c p) d -> p c d", p=128), sv_, writes=[bv])
                ta, ba, sa_ = ga_r.next(); fw.dma("sp", ta[:], ga[:, g0:g0 + GC * 128], sa_, writes=[ba])
                grp = (tq, bq, tk, bk, tkk, bkk, tv, bv, ta, ba)
            tq, bq, tk, bk, tkk, bkk, tv, bv, ta, ba = grp
            j = c % GC
            cs = slice(j * 128, (j + 1) * 128)
            pu, b_pu = ps.next()
            fw.op("pe", lambda e, pu=pu, ta=ta, cs=cs: e.matmul(pu[:, 0:128], lhsT=ta[:, cs], rhs=wa_s[:], start=True, stop=True),
                  reads=[ba, b_wa], writes=[b_pu])
            ex, b_ex = f32t.next()
            fw.op("act", lambda e, pu=pu, ex=ex: e.activation(out=ex[:], in_=pu[:, 0:128], func=AF.Exp, scale=-1.0),
                  reads=[b_pu], writes=[b_ex])
            spt, b_spt = f32t.next()
            fw.op("act", lambda e, ex=ex, spt=spt: e.activation(out=spt[:], in_=ex[:], func=AF.Ln, bias=1.0),
                  reads=[b_ex], writes=[b_spt])
            pb, b_pb = ps.next()
            fw.op("pe", lambda e, pb=pb, spt=spt: e.matmul(pb[:, 0:128], lhsT=spt[:], rhs=Uc_s[:], start=True, stop=True),
                  reads=[b_spt, b_Uc], writes=[b_pb])
            fw.op("pe", lambda e, pb=pb, spt=spt: e.matmul(pb[:, 128:256], lhsT=U2_s[:], rhs=spt[:], start=True, stop=True),
                  reads=[b_spt, b_U2], writes=[b_pb])
            e1, b_e1 = f32t.next()
            fw.op("act", lambda e, pb=pb, e1=e1: e.activation(out=e1[:], in_=pb[:, 0:128], func=AF.Exp),
                  reads=[b_pb], writes=[b_e1])
            e2, b_e2 = f32t.next()
            fw.op("act", lambda e, pb=pb, e2=e2: e.activation(out=e2[:], in_=pb[:, 0:128], func=AF.Exp, scale=-1.0),
                  reads=[b_pb], writes=[b_e2])
            e3, b_e3 = f32t.next()
            fw.op("act", lambda e, pb=pb, e3=e3: e.activation(out=e3[:], in_=pb[:, 128:256], func=AF.Exp),
                  reads=[b_pb], writes=[b_e3])
            dec, b_dec = dec_r.next()
            fw.op("dve", lambda e, dec=dec, e1=e1: e.tensor_copy(out=dec[:], in_=e1[:, 127:128]), reads=[b_e1], writes=[b_dec])
            qe, b_qe = bft.next()
            fw.op("dve", lambda e, qe=qe, tq=tq, cs=cs, e1=e1: e.scalar_tensor_tensor(
                out=qe[:], in0=tq[:, cs], scalar=float(128 ** -0.5), in1=e1[:], op0=ALU.mult, op1=ALU.mult),
                reads=[bq, b_e1], writes=[b_qe])
            ke, b_ke = bft.next()
            fw.op("dve", lambda e, ke=ke, tk=tk, cs=cs, e2=e2: e.tensor_tensor(out=ke[:], in0=tk[:, cs], in1=e2[:], op=ALU.mult),
                  reads=[bk, b_e2], writes=[b_ke])
            kd, b_kd = bft.next()
            fw.op("pool", lambda e, kd=kd, tkk=tkk, j=j, e3=e3: e.tensor_tensor(out=kd[:], in0=tkk[:, j, :], in1=e3[:], op=ALU.mult),
                  reads=[bkk, b_e3], writes=[b_kd])
            pS, b_pS = ps.next()
            fw.op("pe", lambda e, pS=pS, ke=ke, qe=qe: e.matmul(pS[:, 0:128], lhsT=ke[:], rhs=qe[:], start=True, stop=True),
                  reads=[b_ke, b_qe], writes=[b_pS])
            sTm, b_sTm = bft.next()
            fw.op("dve", lambda e, sTm=sTm, pS=pS: e.tensor_tensor(out=sTm[:], in0=pS[:, 0:128], in1=mU_s[:], op=ALU.mult),
                  reads=[b_pS, b_mU], writes=[b_sTm])
            pO, b_pO = ps.next()
            fw.op("pe", lambda e, pO=pO, sTm=sTm, tv=tv, j=j: e.matmul(pO[:, 0:128], lhsT=sTm[:], rhs=tv[:, j, :], start=True, stop=False),
                  reads=[b_sTm, bv], writes=[b_pO])
            fw.op("pe", lambda e, pO=pO, qe=qe, sbc=sb_cur: e.matmul(pO[:, 0:128], lhsT=qe[:], rhs=sbc[:], start=False, stop=True),
                  reads=[b_qe, b_sb_cur], writes=[b_pO])
            ot, b_ot, s_ot = gob.next()
            fw.op("act", lambda e, ot=ot, pO=pO: e.activation(out=ot[:], in_=pO[:, 0:128], func=AF.Copy), reads=[b_pO], writes=[b_ot])
            outs.append(fw.dma("sp", go[c * 128:(c + 1) * 128, :], ot[:], s_ot, reads=[b_ot]))
            pK, b_pK = ps.next()
            fw.op("pe", lambda e, pK=pK, kd=kd, tv=tv, j=j: e.matmul(pK[:, 0:128], lhsT=kd[:], rhs=tv[:, j, :], start=True, stop=True),
                  reads=[b_kd, bv], writes=[b_pK])
            fw.op("dve", lambda e, pK=pK, dec=dec: e.scalar_tensor_tensor(out=S32[:], in0=S32[:], scalar=dec[:, 0:1], in1=pK[:, 0:128],
                                                                       op0=ALU.mult, op1=ALU.add),
                  reads=[b_pK, b_dec, b_S32], writes=[b_S32])
            sb_cur, b_sb_cur = Sbf.next()
            fw.op("dve", lambda e, t=sb_cur: e.tensor_copy(out=t[:], in_=S32[:]), reads=[b_S32], writes=[b_sb_cur])

        e32 = Rot(fw, "e32", [128, 512], F32, 2)
        spb = Rot(fw, "spb", [128, 512], BF16, 3)
        Ab = Rot(fw, "Ab", [128, 512], BF16, 3)
        SS = fw.sbuf("SS", [128, 512], F32); b_SS = Buf()
        SSb = Rot(fw, "SSb", [128, 512], BF16, 3)
        osb = Rot(fw, "osb", [128, 512], BF16, 2, sem=True)
        for g in range(NG):
            qs = slice(g * 512, (g + 1) * 512)
            kbs = list(range(4 * g + 3, -1, -1))
            pOa, b_pOa = pso.next()
            nkb = len(kbs)
            stA = {}

            def stageA(i, kb):
                ks = slice(kb * 128, (kb + 1) * 128)
                pz, b_pz = ps.next()
                fw.op("pe", lambda e, pz=pz, ks=ks, qs=qs: e.matmul(pz[:], lhsT=skT_s[:, ks], rhs=sqT_s[:, qs], start=True, stop=True),
                      reads=[b_skT, b_sqT], writes=[b_pz])
                ee, b_ee = e32.next()
                fw.op("act", lambda e, pz=pz, ee=ee: e.activation(out=ee[:], in_=pz[:], func=AF.Exp), reads=[b_pz], writes=[b_ee])
                sp, b_sp = spb.next()
                fw.op("act", lambda e, ee=ee, sp=sp: e.activation(out=sp[:], in_=ee[:], func=AF.Ln, bias=1.0), reads=[b_ee], writes=[b_sp])
                r = kb - 4 * g
                if r >= 0:
                    fw.op("dve", lambda e, sp=sp, r=r: e.tensor_tensor(out=sp[:], in0=sp[:], in1=M4_s[:, r, :], op=ALU.mult),
                          reads=[b_sp, b_M4], writes=[b_sp])
                if i == 0:
                    sprev = None
                    fw.op("pool", lambda e, sp=sp: e.tensor_copy(out=SS[:], in_=sp[:]), reads=[b_sp], writes=[b_SS])
                else:
                    sprev = stA[i - 1]["snext"]
                    fw.op("pool", lambda e, sp=sp: e.tensor_tensor(out=SS[:], in0=SS[:], in1=sp[:], op=ALU.add),
                          reads=[b_sp, b_SS], writes=[b_SS])
                sn, b_sn = SSb.next()
                fw.op("dve", lambda e, sn=sn: e.tensor_copy(out=sn[:], in_=SS[:]), reads=[b_SS], writes=[b_sn])
                stA[i] = dict(ks=ks, sp=(sp, b_sp), sprev=sprev, snext=(sn, b_sn), r=r, kb=kb)

            def stageB(i):
                d = stA[i]
                ks = d["ks"]; sp, b_sp = d["sp"]
                pl, b_pl = ps.next()
                last = d["sprev"] is None
                fw.op("pe", lambda e, pl=pl, ks=ks, qs=qs: e.matmul(pl[:], lhsT=skT_s[:, ks], rhs=sqT_s[:, qs], start=True, stop=False),
                      reads=[b_skT, b_sqT], writes=[b_pl])
                fw.op("pe", lambda e, pl=pl, sp=sp, last=last: e.matmul(pl[:], lhsT=Lpn_s[:], rhs=sp[:], start=False, stop=last),
                      reads=[b_Lpn, b_sp], writes=[b_pl])
                if not last:
                    sv_, b_sv_ = d["sprev"]
                    fw.op("pe", lambda e, pl=pl, sv_=sv_: e.matmul(pl[:], lhsT=On_s[:], rhs=sv_[:], start=False, stop=True),
                          reads=[b_On, b_sv_], writes=[b_pl])
                A, b_A = Ab.next()
                fw.op("act", lambda e, pl=pl, A=A: e.activation(out=A[:], in_=pl[:], func=AF.Exp), reads=[b_pl], writes=[b_A])
                if d["r"] >= 0:
                    fw.op("dve", lambda e, A=A, r=d["r"]: e.tensor_tensor(out=A[:], in0=A[:], in1=M4_s[:, r, :], op=ALU.mult),
                          reads=[b_A, b_M4], writes=[b_A])
                kb = d["kb"]
                fw.op("pe", lambda e, A=A, kb=kb, i=i, pOa=pOa, nkb=nkb: e.matmul(pOa[:], lhsT=sv_s[:, kb, :], rhs=A[:], start=(i == 0), stop=(i == nkb - 1)),
                      reads=[b_sv, b_A], writes=[b_pOa])
                del stA[i]["sp"]

            stageA(0, kbs[0])
            for i in range(len(kbs)):
                if i + 1 < len(kbs):
                    stageA(i + 1, kbs[i + 1])
                stageB(i)
            o, b_o, s_o = osb.next()
            fw.op("act", lambda e, o=o, pOa=pOa: e.activation(out=o[:], in_=pOa[:], func=AF.Copy), reads=[b_pOa], writes=[b_o])
            outs.append(fw.dma("sp", sbT[:, qs], o[:], s_o, reads=[b_o]))
        fw.wait("sp", outs[-6:])
        fw.emit()
    return nc


def build_l3(TOK):
    NB = TOK // TB
    NTB = TB // 128
    nc = bass.Bass("TRN2", target_bir_lowering=False)
    di = lambda n, s, dt: nc.dram_tensor(n, s, dt, kind="ExternalInput").ap()
    x = di("x", [TOK, D], F32)
    sbT = di("sbT", [1024, TOK], BF16)
    go = di("go", [TOK, 1024], BF16)
    mem = di("mem", [NMEM, D], F32)
    wl = di("wl", [D, NL], F32)
    wbr = [di(n, [1024, D], F32) for n in ("wbg", "wbs", "wbm")]
    wkv = di("wkv", [D, 2048], F32)
    wo = di("wo", [D, D], F32)
    wgu = di("wgu", [D, 2 * DFF], F32)
    wd = di("wd", [DFF, D], F32)
    an = di("an", [128, 16], F32); fn = di("fn", [128, 16], F32); mn = di("mn", [128, 16], F32)
    gon = di("gon", [128, 1024], F32)
    mqn = di("mqn", [128, 2], F32); mkn = di("mkn", [128, 2], F32)
    idn = di("idn", [128, 128], BF16)
    y = nc.dram_tensor("y", [TOK, D], F32, kind="ExternalOutput").ap()
    with contextlib.ExitStack() as st:
        fw = FW(nc, st)
        cm = Common(fw, nc)

        def const(name, src, shape, dt):
            t = fw.sbuf(name, shape, dt); b = Buf()
            cm.load_const(t[:], src, b)
            return t, b
        cm.load_const(cm.ident[:], idn, cm.b_ident)
        an_s, b_an = const("an_s", an, [128, 16], F32)
        fn_s, b_fn = const("fn_s", fn, [128, 16], F32)
        mn_s, b_mn = const("mn_s", mn, [128, 16], F32)
        gon_s, b_gon = const("gon_s", gon, [128, 1024], F32)
        mqn_s, b_mqn = const("mqn_s", mqn, [128, 2], F32)
        mkn_s, b_mkn = const("mkn_s", mkn, [128, 2], F32)

        if PRECAST:
            wl = cm.precast("wl", wl, D, NL)
            wbr = [cm.precast(n, w, 1024, D) for n, w in zip(("wbg", "wbs", "wbm"), wbr)]
            wo = cm.precast("wo", wo, D, D)
            wgu = cm.precast("wgu", wgu, D, 2 * DFF)
            wd = cm.precast("wd", wd, DFF, D)
        big = fw.sbuf("big", [128, 48, TB], BF16)
        b_big = [Buf() for _ in range(48)]
        hT = fw.sbuf("hT", [128, 16, TB], BF16); b_hT = Buf()
        mT = fw.sbuf("mT", [128, 16, TB], BF16); b_mT = [Buf() for _ in range(16)]
        ogT = fw.sbuf("ogT", [128, 8, TB], BF16); b_ogT = Buf()
        omT = fw.sbuf("omT", [128, 8, TB], BF16); b_omT = Buf()
        sbTs = fw.sbuf("sbTs", [128, 8, TB], BF16); b_sbTs = Buf(); s_sbTs = fw.new_sem("s_sbTs")
        sgr = fw.sbuf("sgr", [128, NTB, 1024], BF16); b_sgr = [Buf() for _ in range(NTB)]
        qm = fw.sbuf("qm", [128, NTB, 1024], BF16); b_qm = [Buf() for _ in range(NTB)]
        x1s = fw.sbuf("x1s", [128, NTB, D], F32); b_x1s = [Buf() for _ in range(NTB)]
        mkT = fw.sbuf("mkT", [128, 4, 2, NMEM], BF16); b_mkT = Buf()
        mv = fw.sbuf("mv", [128, 2, 1024], BF16); b_mv = Buf()
        mktm = fw.sbuf("mktm", [128, 2, 1024], BF16); b_mktm = Buf()
        sq32 = fw.sbuf("sq32", [128, 1024], F32); b_sq32 = Buf()
        t32 = Rot(fw, "t32", [128, 512], F32, 4)
        gor = Rot(fw, "gor", [128, 1024], BF16, 2, sem=True)
        ogb = Rot(fw, "ogb", [128, 1024], BF16, 2)
        og32 = fw.sbuf("og32", [128, 1024], F32); b_og32 = Buf()
        qmT = fw.sbuf("qmT", [128, 8, 128], BF16); b_qmT = Buf()
        pbf = Rot(fw, "pbf", [128, 256], BF16, 2)
        pTs = fw.sbuf("pTs", [128, 2, 128], BF16); b_pTs = Buf()
        ys = Rot(fw, "ys", [128, 512], F32, 3, sem=True)
        mg32 = fw.sbuf("mg32", [128, 16, TB], F32); b_mg = [Buf() for _ in range(16)]

        def mm_tm(lhs_fn, b_lhs, nk, wb, b_wb, ncols, ps, b_ps, first=True, last=True):
            for kc in range(nk):
                fw.op("pe", lambda e, kc=kc: e.matmul(ps[:, 0:ncols], lhsT=lhs_fn(kc), rhs=wb[:, kc, 0:ncols],
                                                      start=(first and kc == 0), stop=(last and kc == nk - 1)),
                      reads=list(b_lhs) + [b_wb], writes=[b_ps])

        def mm_fm(wb, b_wb, c0, nk, rhs_fn, b_rhs, ps, b_ps):
            for kc in range(nk):
                fw.op("pe", lambda e, kc=kc: e.matmul(ps[:, 0:TB], lhsT=wb[:, kc, c0:c0 + 128], rhs=rhs_fn(kc),
                                                      start=(kc == 0), stop=(kc == nk - 1)),
                      reads=list(b_rhs) + [b_wb], writes=[b_ps])

        def head_rstd(src_ap, b_src, nh, dh, scale, bias):
            fw.op("act", lambda e: e.activation(out=sq32[:, 0:nh * dh], in_=src_ap, func=AF.Square),
                  reads=[b_src], writes=[b_sq32])
            ss, b_ss = cm.st.next()
            fw.op("dve", lambda e: e.tensor_reduce(out=ss[:, 0:nh], in_=sq32[:, 0:nh * dh].rearrange("p (h d) -> p h d", h=nh),
                                                   axis=AX.X, op=ALU.add), reads=[b_sq32], writes=[b_ss])
            return cm.rstd(ss[:, 0:nh], b_ss, nh, scale, bias)

        memT = big
        for mt in range(2):
            xt, b_xt, s_xt = cm.xr.next()
            fw.dma("sp", xt[:], mem[mt * 128:(mt + 1) * 128, :], s_xt, writes=[b_xt])
            cm.norm_T(xt[:], b_xt, mn_s, b_mn, memT, b_big[0], mt * 128)
        for nb in range(4):
            wb, b_wb = cm.load_w(wkv, 0, 16, nb * 512, 512)
            for mt in range(2):
                ps, b_ps = cm.ps.next()
                mm_tm(lambda kc, mt=mt: memT[:, kc, mt * 128:(mt + 1) * 128], [b_big[0]], 16, wb, b_wb, 512, ps, b_ps)
                if nb < 2:
                    rs, b_rs = head_rstd(ps[:], b_ps, 2, 256, 1.0 / 256, EPS)
                    for h in range(2):
                        fw.op("dve", lambda e, h=h, ps=ps, rs=rs, mt=mt, nb=nb: e.tensor_scalar(
                            out=mktm[:, mt, nb * 512 + h * 256: nb * 512 + (h + 1) * 256], in0=ps[:, h * 256:(h + 1) * 256],
                            scalar1=rs[:, h:h + 1], scalar2=None, op0=ALU.mult), reads=[b_ps, b_rs], writes=[b_mktm])
                else:
                    fw.op("act", lambda e, ps=ps, mt=mt, nb=nb: e.activation(out=mv[:, mt, (nb - 2) * 512:(nb - 1) * 512], in_=ps[:], func=AF.Copy),
                          reads=[b_ps], writes=[b_mv])
        for mt in range(2):
            cm.transposes(mktm[:, mt, :], b_mktm, 8,
                          lambda c, mt=mt: mkT[:, c // 2, c % 2, mt * 128:(mt + 1) * 128], b_mkT,
                          lambda c: mkn_s[:, (c % 2):(c % 2) + 1], b_mkn)

        outs = []
        for blk in range(NB):
            t0 = blk * TB
            for i in range(NTB):
                xt, b_xt, s_xt = cm.xr.next()
                fw.dma("sp", xt[:], x[t0 + i * 128: t0 + (i + 1) * 128, :], s_xt, writes=[b_xt])
                cm.norm_T(xt[:], b_xt, an_s, b_an, hT, b_hT, i * 128)
            fw.dma("sp", sbTs[:], sbT[:, t0:t0 + TB].rearrange("(c p) t -> p c t", p=128), s_sbTs, writes=[b_sbTs])
            for nb in range(4):
                wb, b_wb = cm.load_w(wl, 0, 16, nb * 512, 512)
                for i in range(NTB):
                    ps, b_ps = cm.ps.next()
                    mm_tm(lambda kc, i=i: hT[:, kc, i * 128:(i + 1) * 128], [b_hT], 16, wb, b_wb, 512, ps, b_ps)
                    if nb < 2:
                        fw.op("act", lambda e, ps=ps, i=i, nb=nb: e.activation(out=sgr[:, i, nb * 512:(nb + 1) * 512], in_=ps[:], func=AF.Silu),
                              reads=[b_ps], writes=[b_sgr[i]])
                    else:
                        rs, b_rs = head_rstd(ps[:], b_ps, 2, 256, 1.0, 256 * EPS)
                        for h in range(2):
                            fw.op("dve", lambda e, h=h, ps=ps, rs=rs, i=i, nb=nb: e.tensor_scalar(
                                out=qm[:, i, (nb - 2) * 512 + h * 256:(nb - 2) * 512 + (h + 1) * 256], in0=ps[:, h * 256:(h + 1) * 256],
                                scalar1=rs[:, h:h + 1], scalar2=None, op0=ALU.mult), reads=[b_ps, b_rs], writes=[b_qm[i]])
            for nb in range(12):
                wb, b_wb = cm.load_w(wl, 0, 16, 2048 + nb * 512, 512)
                for s in range(4):
                    ch = nb * 4 + s
                    ps, b_ps = cm.ps.next()
                    mm_fm(wb, b_wb, s * 128, 16, lambda kc: hT[:, kc, :], [b_hT], ps, b_ps)
                    fw.op("act", lambda e, ps=ps, ch=ch: e.activation(out=big[:, ch, :], in_=ps[:, 0:TB], func=AF.Sigmoid),
                          reads=[b_ps], writes=[b_big[ch]])
            for i in range(NTB):
                gt, b_gt, s_gt = gor.next()
                fw.dma("sp", gt[:], go[t0 + i * 128:t0 + (i + 1) * 128, :], s_gt, writes=[b_gt])
                rs, b_rs = head_rstd(gt[:], b_gt, 4, 256, 1.0 / 256, EPS)
                for h in range(4):
                    hs = slice(h * 256, (h + 1) * 256)
                    fw.op("dve", lambda e, h=h, hs=hs, gt=gt, rs=rs: e.scalar_tensor_tensor(
                        out=og32[:, hs], in0=gt[:, hs], scalar=rs[:, h:h + 1], in1=gon_s[:, hs], op0=ALU.mult, op1=ALU.mult),
                        reads=[b_gt, b_rs, b_gon], writes=[b_og32])
                ob_, b_ob = ogb.next()
                fw.op("dve", lambda e, ob_=ob_, i=i: e.tensor_tensor(out=ob_[:], in0=og32[:], in1=sgr[:, i, :], op=ALU.mult),
                      reads=[b_og32, b_sgr[i]], writes=[b_ob])
                cm.transposes(ob_, b_ob, 8, lambda c, i=i: ogT[:, c, i * 128:(i + 1) * 128], b_ogT)
            for i in range(NTB):
                cm.transposes(qm[:, i, :], b_qm[i], 8, lambda c: qmT[:, c, :], b_qmT,
                              lambda c: mqn_s[:, (c % 2):(c % 2) + 1], b_mqn)
                for h in range(4):
                    ps, b_ps = cm.ps.next()
                    for c2 in range(2):
                        fw.op("pe", lambda e, ps=ps, h=h, c2=c2: e.matmul(ps[:, 0:NMEM], lhsT=qmT[:, h * 2 + c2, :], rhs=mkT[:, h, c2, :],
                                                                         start=(c2 == 0), stop=(c2 == 1)),
                              reads=[b_qmT, b_mkT], writes=[b_ps])
                    mx, b_mx = cm.st.next()
                    fw.op("dve", lambda e, ps=ps, mx=mx: e.tensor_reduce(out=mx[:, 0:1], in_=ps[:, 0:NMEM], axis=AX.X, op=ALU.max),
                          reads=[b_ps], writes=[b_mx])
                    fw.op("dve", lambda e, mx=mx: e.tensor_scalar(out=mx[:, 1:2], in0=mx[:, 0:1], scalar1=-1.0, scalar2=None, op0=ALU.mult),
                          reads=[b_mx], writes=[b_mx])
                    pe_, b_pe = t32.next()
                    fw.op("act", lambda e, ps=ps, mx=mx, pe_=pe_: e.activation(out=pe_[:, 0:NMEM], in_=ps[:, 0:NMEM], func=AF.Exp,
                                                                             bias=mx[:, 1:2], accum_out=mx[:, 2:3]),
                          reads=[b_ps, b_mx], writes=[b_pe, b_mx])
                    fw.op("dve", lambda e, mx=mx: e.reciprocal(out=mx[:, 3:4], in_=mx[:, 2:3]), reads=[b_mx], writes=[b_mx])
                    pb_, b_pb = pbf.next()
                    fw.op("dve", lambda e, pb_=pb_, pe_=pe_, mx=mx: e.tensor_scalar(out=pb_[:], in0=pe_[:, 0:NMEM], scalar1=mx[:, 3:4],
                                                                                  scalar2=None, op0=ALU.mult),
                          reads=[b_pe, b_mx], writes=[b_pb])
                    cm.transposes(pb_, b_pb, 2, lambda c: pTs[:, c, :], b_pTs)
                    for c2 in range(2):
                        ps2, b_ps2 = cm.ps.next()
                        for mc in range(2):
                            fw.op("pe", lambda e, ps2=ps2, h=h, c2=c2, mc=mc: e.matmul(
                                ps2[:, 0:128], lhsT=mv[:, mc, h * 256 + c2 * 128: h * 256 + (c2 + 1) * 128], rhs=pTs[:, mc, :],
                                start=(mc == 0), stop=(mc == 1)), reads=[b_mv, b_pTs], writes=[b_ps2])
                        fw.op("act", lambda e, ps2=ps2, h=h, c2=c2, i=i: e.activation(out=omT[:, h * 2 + c2, i * 128:(i + 1) * 128],
                                                                                    in_=ps2[:, 0:128], func=AF.Copy),
                              reads=[b_ps2], writes=[b_omT])
            srcs = [(ogT, b_ogT), (sbTs, b_sbTs), (omT, b_omT)]
            for br in range(3):
                src, b_src = srcs[br]
                for nb in range(4):
                    wb, b_wb = cm.load_w(wbr[br], 0, 8, nb * 512, 512)
                    for s in range(4):
                        ncn = nb * 4 + s
                        gch = br * 16 + ncn
                        ps, b_ps = cm.ps.next()
                        mm_fm(wb, b_wb, s * 128, 8, lambda kc, src=src: src[:, kc, :], [b_src], ps, b_ps)
                        if br == 0:
                            fw.op("dve", lambda e, ps=ps, ncn=ncn, gch=gch: e.tensor_tensor(out=mg32[:, ncn, :], in0=ps[:, 0:TB], in1=big[:, gch, :], op=ALU.mult),
                                  reads=[b_ps, b_big[gch]], writes=[b_mg[ncn]])
                        else:
                            tt, b_tt = t32.next()
                            fw.op("dve", lambda e, ps=ps, tt=tt, gch=gch: e.tensor_tensor(out=tt[:, 0:TB], in0=ps[:, 0:TB], in1=big[:, gch, :], op=ALU.mult),
                                  reads=[b_ps, b_big[gch]], writes=[b_tt])
                            if br == 1:
                                fw.op("pool", lambda e, tt=tt, ncn=ncn: e.tensor_tensor(out=mg32[:, ncn, :], in0=mg32[:, ncn, :], in1=tt[:, 0:TB], op=ALU.add),
                                      reads=[b_tt, b_mg[ncn]], writes=[b_mg[ncn]])
                            else:
                                fw.op("pool", lambda e, tt=tt, ncn=ncn: e.tensor_tensor(out=mT[:, ncn, :], in0=mg32[:, ncn, :], in1=tt[:, 0:TB], op=ALU.add),
                                      reads=[b_tt, b_mg[ncn]], writes=[b_mT[ncn]])
            for nb in range(4):
                wb, b_wb = cm.load_w(wo, 0, 16, nb * 512, 512)
                for i in range(NTB):
                    ps, b_ps = cm.ps.next()
                    mm_tm(lambda kc, i=i: mT[:, kc, i * 128:(i + 1) * 128], b_mT, 16, wb, b_wb, 512, ps, b_ps)
                    xt, b_xt, s_xt = ys.next()
                    fw.dma("sp", xt[:], x[t0 + i * 128:t0 + (i + 1) * 128, nb * 512:(nb + 1) * 512], s_xt, writes=[b_xt])
                    fw.op("dve", lambda e, ps=ps, xt=xt, i=i, nb=nb: e.tensor_tensor(out=x1s[:, i, nb * 512:(nb + 1) * 512], in0=ps[:], in1=xt[:], op=ALU.add),
                          reads=[b_ps, b_xt], writes=[b_x1s[i]])
            for i in range(NTB):
                cm.norm_T(x1s[:, i, :], b_x1s[i], fn_s, b_fn, hT, b_hT, i * 128)
            for nb in range(11):
                wg_, b_wg = cm.load_w(wgu, 0, 16, nb * 512, 512)
                sgs = []
                for s in range(4):
                    ps, b_ps = cm.ps.next()
                    mm_fm(wg_, b_wg, s * 128, 16, lambda kc: hT[:, kc, :], [b_hT], ps, b_ps)
                    sg, b_sg = t32.next()
                    fw.op("act", lambda e, ps=ps, sg=sg: e.activation(out=sg[:, 0:TB], in_=ps[:, 0:TB], func=AF.Silu), reads=[b_ps], writes=[b_sg])
                    sgs.append((sg, b_sg))
                wu_, b_wu = cm.load_w(wgu, 0, 16, DFF + nb * 512, 512)
                for s in range(4):
                    ch = nb * 4 + s
                    ps, b_ps = cm.ps.next()
                    mm_fm(wu_, b_wu, s * 128, 16, lambda kc: hT[:, kc, :], [b_hT], ps, b_ps)
                    sg, b_sg = sgs[s]
                    fw.op("dve", lambda e, ps=ps, sg=sg, ch=ch: e.tensor_tensor(out=big[:, ch, :], in0=ps[:, 0:TB], in1=sg[:, 0:TB], op=ALU.mult),
                          reads=[b_ps, b_sg], writes=[b_big[ch]])
            kgs = [(0, 16), (16, 16), (32, 12)]
            for nb in range(4):
                pss = [cm.ps.next() for _ in range(NTB)]
                for gi, (k0, nk) in enumerate(kgs):
                    wb, b_wb = cm.load_w(wd, k0, nk, nb * 512, 512)
                    for i in range(NTB):
                        ps, b_ps = pss[i]
                        mm_tm(lambda kc, i=i, k0=k0: big[:, k0 + kc, i * 128:(i + 1) * 128], b_big[k0:k0 + nk], nk, wb, b_wb, 512, ps, b_ps,
                              first=(gi == 0), last=(gi == 2))
                for i in range(NTB):
                    ps, b_ps = pss[i]
                    yt, b_yt, s_yt = ys.next()
                    fw.op("dve", lambda e, ps=ps, yt=yt, i=i, nb=nb: e.tensor_tensor(out=yt[:], in0=ps[:], in1=x1s[:, i, nb * 512:(nb + 1) * 512], op=ALU.add),
                          reads=[b_ps, b_x1s[i]], writes=[b_yt])
                    outs.append(fw.dma("sp", y[t0 + i * 128:t0 + (i + 1) * 128, nb * 512:(nb + 1) * 512], yt[:], s_yt, reads=[b_yt]))
        fw.wait("sp", outs[-6:])
        fw.emit()
    return nc


_CACHE = {}


def _get(name, fn, *a):
    k = (name,) + a
    if k not in _CACHE:
        _CACHE[k] = fn(*a)
    return _CACHE[k]


def _fm(g):
    return np.ascontiguousarray(np.asarray(g, np.float32).reshape(-1, 128).T)


def _bc(g, rep):
    return np.ascontiguousarray(np.broadcast_to(np.tile(np.asarray(g, np.float32), rep)[None, :], (128, g.shape[0] * rep)))


def run_l1(x2, wh, attn_norm, sbq, sbk):
    T = x2.shape[0]
    TOK = T // K1
    nc = _get("l1", build_l1, TOK)
    idn = np.eye(128, dtype=np.float32).astype(NPBF)
    ims = [dict(x=x2[c * TOK:(c + 1) * TOK], wh=wh, gn=_fm(attn_norm), qg=_bc(sbq, 8), kg=_bc(sbk, 8), idn=idn) for c in range(K1)]
    res = run_bass_kernel_spmd(nc, ims, core_ids=list(range(K1))).results
    oh = np.concatenate([r["oh"] for r in res], 0)
    oa = np.concatenate([r["oa"] for r in res], 0)
    return oh, oa


def run_l2(oh, oa, w_a2, b_a):
    T = oh.shape[0]
    nc = _get("l2", build_l2, T)
    cs = l2_consts()
    sq, sk, sv = oh[:, 0:1024], oh[:, 1024:2048], oh[:, 2048:3072]
    gq, gk, gv = oh[:, 3072:3584], oh[:, 3584:4096], oh[:, 4096:5120]
    ga = np.ascontiguousarray(np.concatenate([oa.T, np.ones((1, T), np.float32)], 0))
    ims = []
    for c in range(8):
        hg, half = c // 2, c % 2
        hs = slice(c * 128, (c + 1) * 128)
        gs = slice(hg * 128, (hg + 1) * 128)
        vs = slice(hg * 256 + half * 128, hg * 256 + (half + 1) * 128)
        wa = np.ascontiguousarray(np.concatenate([w_a2[:, gs], b_a[None, gs]], 0).astype(np.float32))
        ims.append(dict(sqT=np.ascontiguousarray(sq[:, hs].T), skT=np.ascontiguousarray(sk[:, hs].T), sv=np.ascontiguousarray(sv[:, hs]),
                        gqT=np.ascontiguousarray(gq[:, gs].T), gkT=np.ascontiguousarray(gk[:, gs].T),
                        gk=np.ascontiguousarray(gk[:, gs]), gv=np.ascontiguousarray(gv[:, vs]), ga=ga, wa=wa, **cs))
    res = run_bass_kernel_spmd(nc, ims, core_ids=list(range(8))).results
    sbT = np.concatenate([r["sbT"] for r in res], 0)
    go = np.concatenate([r["go"] for r in res], 1)
    return sbT, go


def run_l3(x2, sbT, go, mem2, p):
    T = x2.shape[0]
    TOK = T // K3
    nc = _get("l3", build_l3, TOK)
    idn = np.eye(128, dtype=np.float32).astype(NPBF)
    ims = []
    for c in range(K3):
        ts = slice(c * TOK, (c + 1) * TOK)
        ims.append(dict(x=x2[ts], sbT=np.ascontiguousarray(sbT[:, ts]), go=np.ascontiguousarray(go[ts]), mem=mem2,
                        wl=p["wl"], wbg=p["wbg"], wbs=p["wbs"], wbm=p["wbm"], wkv=p["wkv"], wo=p["wo"], wgu=p["wgu"], wd=p["wd"],
                        an=_fm(p["attn_norm"]), fn=_fm(p["ffn_norm"]), mn=_fm(p["mem_norm"]), gon=_bc(p["gla_out_norm"], 4),
                        mqn=np.ascontiguousarray(p["mem_q_norm"].reshape(2, 128).T), mkn=np.ascontiguousarray(p["mem_k_norm"].reshape(2, 128).T),
                        idn=idn))
    res = run_bass_kernel_spmd(nc, ims, core_ids=list(range(K3))).results
    return np.concatenate([r["y"] for r in res], 0)


def split_w_in(w):
    gq, gk, gv, gr = w[:, 0:512], w[:, 512:1024], w[:, 1024:2048], w[:, 2048:3072]
    ga1 = w[:, 3072:3088]
    sq, sk, sv = w[:, 3088:4112], w[:, 4112:5136], w[:, 5136:6160]
    mq, gates = w[:, 6160:7184], w[:, 7184:13328]
    wh = np.ascontiguousarray(np.concatenate([sq, sk, sv, gq, gk, gv, ga1], 1))
    wl = np.ascontiguousarray(np.concatenate([gr, mq, gates], 1))
    return wh, wl


def kernel(x, mem, attn_norm, w_in, gla_w_a2, gla_b_a, gla_out_norm, w_br_gla,
           sb_q_norm, sb_k_norm, w_br_sb, mem_norm, w_mem_kv, mem_q_norm, mem_k_norm,
           w_br_mem, w_o, ffn_norm, w_gate_up, w_down):
    f = lambda a: np.asarray(a, np.float32)
    x2 = np.ascontiguousarray(f(x)[0])
    mem2 = np.ascontiguousarray(f(mem)[0])
    for l in range(w_in.shape[0]):
        wh, wl = split_w_in(f(w_in[l]))
        oh, oa = run_l1(x2, wh, f(attn_norm[l]), f(sb_q_norm[l]), f(sb_k_norm[l]))
        sbT, go = run_l2(oh, oa, f(gla_w_a2[l]), f(gla_b_a[l]))
        p = dict(wl=wl, wbg=f(w_br_gla[l]), wbs=f(w_br_sb[l]), wbm=f(w_br_mem[l]), wkv=f(w_mem_kv[l]), wo=f(w_o[l]),
                 wgu=f(w_gate_up[l]), wd=f(w_down[l]), attn_norm=f(attn_norm[l]), ffn_norm=f(ffn_norm[l]),
                 mem_norm=f(mem_norm[l]), gla_out_norm=f(gla_out_norm[l]), mem_q_norm=f(mem_q_norm[l]),
                 mem_k_norm=f(mem_k_norm[l]))
        x2 = run_l3(x2, sbT, go, mem2, p)
    return x2[None].astype(np.float32)
```

```python
import contextlib
import numpy as np
import ml_dtypes
import concourse.bass as bass
import concourse.mybir as mybir
from concourse.bass_utils import run_bass_kernel_spmd

F32 = mybir.dt.float32
BF16 = mybir.dt.bfloat16
AF = mybir.ActivationFunctionType
ALU = mybir.AluOpType
AX = mybir.AxisListType
NPBF = ml_dtypes.bfloat16

D = 2048
DFF = 5632
NMEM = 256
EPS = 1e-6
NH1 = 5136
NL = 8192
K1 = 4
K3 = 4
L1P = 2048
TB = 256
PRECAST = True


class Sem:
    def __init__(self, h):
        self.h = h
        self.count = 0


class Buf:
    __slots__ = ("w", "r")

    def __init__(self):
        self.w = None
        self.r = []


class FW:
    ENGS = ("pe", "act", "dve", "pool", "sp")
    M = 8

    def __init__(self, nc, stack):
        self.nc = nc
        self.stack = stack
        self.q = {e: [] for e in self.ENGS}
        self.esem = {e: [self.new_sem(f"e_{e}{j}") for j in range(self.M)] for e in self.ENGS}
        self.eidx = {e: 0 for e in self.ENGS}
        self.seen_e = {e: {s: 0 for s in self.ENGS} for e in self.ENGS}
        self.seen_d = {e: {} for e in self.ENGS}
        self.n = 0

    def new_sem(self, name):
        return Sem(self.stack.enter_context(self.nc.semaphore(name)))

    def sbuf(self, name, shape, dt):
        return self.stack.enter_context(self.nc.sbuf_tensor(name, list(shape), dt))

    def psum(self, name, shape, dt):
        return self.stack.enter_context(self.nc.psum_tensor(name, list(shape), dt))

    def _waits(self, eng, deps):
        out = {}
        for t in deps:
            if t is None:
                continue
            kind, src, val = t
            if kind == "e":
                if src == "pe" and eng == "pe":
                    continue
                if self.seen_e[eng][src] >= val:
                    continue
                key = ("e", src)
                if out.get(key, 0) < val:
                    out[key] = val
            else:
                if self.seen_d[eng].get(src, 0) >= val:
                    continue
                key = ("d", src)
                if out.get(key, 0) < val:
                    out[key] = val
        res = []
        for (kind, src), val in out.items():
            if kind == "e":
                self.seen_e[eng][src] = val
                j = (val - 1) % self.M
                res.append((self.esem[src][j].h, (val - 1) // self.M + 1))
            else:
                self.seen_d[eng][src] = val
                res.append((src.h, val))
        return res

    @staticmethod
    def _deps(reads, writes, extra):
        deps = list(extra)
        for b in reads:
            deps.append(b.w)
        for b in writes:
            deps.append(b.w)
            deps.extend(b.r)
        return deps

    @staticmethod
    def _commit(tok, reads, writes):
        for b in reads:
            b.r.append(tok)
            if len(b.r) > 48:
                del b.r[:-48]
        for b in writes:
            b.w = tok
            b.r = []

    def op(self, eng, fn, reads=(), writes=(), deps=()):
        waits = self._waits(eng, self._deps(reads, writes, deps))
        self.eidx[eng] += 1
        idx = self.eidx[eng]
        tok = ("e", eng, idx)
        s = self.esem[eng][(idx - 1) % self.M]
        self.q[eng].append((waits, fn, (s.h, 1)))
        self.n += 1 + len(waits)
        self._commit(tok, reads, writes)
        return tok

    def dma(self, eng, out, in_, sem, reads=(), writes=(), deps=()):
        waits = self._waits(eng, self._deps(reads, writes, deps))
        sem.count += 16
        tok = ("d", sem, sem.count)
        self.q[eng].append((waits, lambda e: e.dma_start(out=out, in_=in_), (sem.h, 16)))
        self.n += 1 + len(waits)
        self._commit(tok, reads, writes)
        return tok

    def wait(self, eng, deps):
        waits = self._waits(eng, deps)
        if waits:
            self.q[eng].append((waits, None, None))

    def emit(self):
        q = self.q

        def run(e, lst):
            for waits, fn, inc in lst:
                for (h, v) in waits:
                    e.wait_ge(h, v)
                if fn is not None:
                    fn(e).then_inc(inc[0], inc[1])

        with self.nc.Block() as block:
            @block.tensor
            def _(e):
                run(e, q["pe"])

            @block.scalar
            def _(e):
                run(e, q["act"])

            @block.vector
            def _(e):
                run(e, q["dve"])

            @block.gpsimd
            def _(e):
                run(e, q["pool"])

            @block.sync
            def _(e):
                run(e, q["sp"])


class Rot:
    def __init__(self, fw, name, shape, dt, n, psum=False, sem=False):
        self.t = [(fw.psum if psum else fw.sbuf)(f"{name}{i}", shape, dt) for i in range(n)]
        self.b = [Buf() for _ in range(n)]
        self.s = [fw.new_sem(f"s_{name}{i}") for i in range(n)] if sem else None
        self.i = -1
        self.n = n

    def next(self):
        self.i = (self.i + 1) % self.n
        if self.s:
            return self.t[self.i], self.b[self.i], self.s[self.i]
        return self.t[self.i], self.b[self.i]


class Common:
    def __init__(self, fw, nc):
        self.fw = fw
        self.nc = nc
        self.ident = fw.sbuf("ident", [128, 128], BF16)
        self.b_ident = Buf()
        self.s_const = fw.new_sem("s_const")
        self.xr = Rot(fw, "xr", [128, D], F32, 2, sem=True)
        self.hb = Rot(fw, "hb", [128, D], BF16, 2)
        self.junk = fw.sbuf("junk", [128, D], BF16)
        self.b_junk = Buf()
        self.st = Rot(fw, "st", [128, 8], F32, 4)
        self.ps = Rot(fw, "ps", [128, 512], F32, 6, psum=True)
        self.pt = Rot(fw, "pt", [128, 1024], BF16, 2, psum=True)
        self.wb = Rot(fw, "wb", [128, 16, 512], BF16, 2, sem=True)

    def load_const(self, dst, src, b):
        self.fw.dma("sp", dst, src, self.s_const, writes=[b])

    def rstd(self, ssq_ap, b_ssq, n, scale, bias):
        fw = self.fw
        st, b_st = self.st.next()
        fw.op("act", lambda e: e.activation(out=st[:, 0:n], in_=ssq_ap, func=AF.Sqrt, scale=scale, bias=bias),
              reads=[b_ssq], writes=[b_st])
        fw.op("dve", lambda e: e.reciprocal(out=st[:, 0:n], in_=st[:, 0:n]), reads=[b_st], writes=[b_st])
        return st, b_st

    def norm_T(self, src, b_src, gain_fm, b_gain, dstT, b_dst, col0):
        fw = self.fw
        ss, b_ss = self.st.next()
        fw.op("act", lambda e: e.activation(out=self.junk[:], in_=src, func=AF.Square, accum_out=ss[:, 0:1]),
              reads=[b_src], writes=[self.b_junk, b_ss])
        rs, b_rs = self.rstd(ss[:, 0:1], b_ss, 1, 1.0 / D, EPS)
        hb, b_hb = self.hb.next()
        fw.op("dve", lambda e: e.tensor_scalar(out=hb[:], in0=src, scalar1=rs[:, 0:1], scalar2=None, op0=ALU.mult),
              reads=[b_src, b_rs], writes=[b_hb])
        self.transposes(hb, b_hb, D // 128, lambda c: dstT[:, c, col0:col0 + 128], b_dst,
                        lambda c: gain_fm[:, c:c + 1], b_gain)

    def transposes(self, src, b_src, nch, dst_fn, b_dst, scale_fn=None, b_scale=None):
        fw = self.fw
        for c0 in range(0, nch, 8):
            pt, b_pt = self.pt.next()
            m = min(8, nch - c0)
            for j in range(m):
                c = c0 + j
                fw.op("pe", lambda e, c=c, j=j, pt=pt: e.transpose(out=pt[:, j * 128:(j + 1) * 128],
                                                                 in_=src[:, c * 128:(c + 1) * 128],
                                                                 identity=self.ident[:]),
                      reads=[b_src, self.b_ident], writes=[b_pt])
            for j in range(m):
                c = c0 + j
                eng = "dve" if j % 2 == 0 else "act"
                rd = [b_pt] + ([b_scale] if b_scale is not None else [])
                if scale_fn is None:
                    if eng == "dve":
                        fw.op("dve", lambda e, c=c, j=j, pt=pt: e.tensor_copy(out=dst_fn(c), in_=pt[:, j * 128:(j + 1) * 128]),
                              reads=rd, writes=[b_dst])
                    else:
                        fw.op("act", lambda e, c=c, j=j, pt=pt: e.activation(out=dst_fn(c), in_=pt[:, j * 128:(j + 1) * 128], func=AF.Copy),
                              reads=rd, writes=[b_dst])
                else:
                    if eng == "dve":
                        fw.op("dve", lambda e, c=c, j=j, pt=pt: e.tensor_scalar(out=dst_fn(c), in0=pt[:, j * 128:(j + 1) * 128],
                                                                               scalar1=scale_fn(c), scalar2=None, op0=ALU.mult),
                              reads=rd, writes=[b_dst])
                    else:
                        fw.op("act", lambda e, c=c, j=j, pt=pt: e.activation(out=dst_fn(c), in_=pt[:, j * 128:(j + 1) * 128],
                                                                            func=AF.Copy, scale=scale_fn(c)),
                              reads=rd, writes=[b_dst])

    def load_w(self, w_ap, k0, nk, n0, ncols):
        wb, b_wb, s_wb = self.wb.next()
        if isinstance(w_ap, tuple):
            t, b_w = w_ap
            assert n0 % 512 == 0 and ncols == 512
            src = t[n0 // 512][:, k0:k0 + nk, :]
            self.fw.dma("pool", wb[:, 0:nk, 0:ncols], src, s_wb, reads=[b_w], writes=[b_wb])
        else:
            src = w_ap[k0 * 128:(k0 + nk) * 128, n0:n0 + ncols].rearrange("(c p) n -> p c n", p=128)
            self.fw.dma("pool", wb[:, 0:nk, 0:ncols], src, s_wb, writes=[b_wb])
        return wb, b_wb

    def precast(self, name, w_ap, K, N):
        KC, NBK = K // 128, N // 512
        t = self.nc.dram_tensor(name + "_bf", [NBK, 128, KC, 512], BF16)
        b = Buf()
        sem = self.fw.new_sem("s_pc_" + name)
        tok = None
        for nb in range(NBK):
            src = w_ap[:, nb * 512:(nb + 1) * 512].rearrange("(c p) n -> p c n", p=128)
            tok = self.fw.dma("pool", t[nb], src, sem)
        b.w = tok
        return (t, b)


def build_l1(TOK):
    PT = min(TOK, L1P)
    NTP = PT // 128
    nc = bass.Bass("TRN2", target_bir_lowering=False)
    x = nc.dram_tensor("x", [TOK, D], F32, kind="ExternalInput").ap()
    wh = nc.dram_tensor("wh", [D, NH1], F32, kind="ExternalInput").ap()
    gn = nc.dram_tensor("gn", [128, 16], F32, kind="ExternalInput").ap()
    qg = nc.dram_tensor("qg", [128, 1024], F32, kind="ExternalInput").ap()
    kg = nc.dram_tensor("kg", [128, 1024], F32, kind="ExternalInput").ap()
    idn = nc.dram_tensor("idn", [128, 128], BF16, kind="ExternalInput").ap()
    oh = nc.dram_tensor("oh", [TOK, 5120], BF16, kind="ExternalOutput").ap()
    oa = nc.dram_tensor("oa", [TOK, 16], F32, kind="ExternalOutput").ap()
    with contextlib.ExitStack() as st:
        fw = FW(nc, st)
        cm = Common(fw, nc)
        gn_s = fw.sbuf("gn_s", [128, 16], F32); b_gn = Buf()
        qg_s = fw.sbuf("qg_s", [128, 1024], F32); b_qg = Buf()
        kg_s = fw.sbuf("kg_s", [128, 1024], F32); b_kg = Buf()
        hT = fw.sbuf("hT", [128, 16, PT], BF16); b_hT = Buf()
        sqs = fw.sbuf("sqs", [128, 512], F32); b_sqs = Buf()
        ob = Rot(fw, "ob", [128, 512], BF16, 3, sem=True)
        oab = Rot(fw, "oab", [128, 16], F32, 2, sem=True)
        cm.load_const(cm.ident[:], idn, cm.b_ident)
        cm.load_const(gn_s[:], gn, b_gn)
        cm.load_const(qg_s[:], qg, b_qg)
        cm.load_const(kg_s[:], kg, b_kg)
        outs = []
        nblocks = [(i * 512, 512) for i in range(10)] + [(5120, 16)]
        def l1_body(tb):
            for bi, (n0, ncols) in enumerate(nblocks):
                wb, b_wb = cm.load_w(wh, 0, 16, n0, ncols)
                for t in range(NTP):
                    ps, b_ps = cm.ps.next()
                    for kc in range(16):
                        fw.op("pe", lambda e, kc=kc, t=t, ps=ps, wb=wb, ncols=ncols: e.matmul(
                            ps[:, 0:ncols], lhsT=hT[:, kc, t * 128:(t + 1) * 128], rhs=wb[:, kc, 0:ncols],
                            start=(kc == 0), stop=(kc == 15)), reads=[b_hT, b_wb], writes=[b_ps])
                    if bi < 4:
                        isq = bi < 2
                        g_s, b_g = (qg_s, b_qg) if isq else (kg_s, b_kg)
                        gofs = (bi % 2) * 512
                        fw.op("act", lambda e, ps=ps: e.activation(out=sqs[:], in_=ps[:], func=AF.Square),
                              reads=[b_ps], writes=[b_sqs])
                        ss, b_ss = cm.st.next()
                        fw.op("dve", lambda e, ss=ss: e.tensor_reduce(out=ss[:, 0:4], in_=sqs[:].rearrange("p (h d) -> p h d", h=4),
                                                                     axis=AX.X, op=ALU.add), reads=[b_sqs], writes=[b_ss])
                        if isq:
                            rs, b_rs = cm.rstd(ss[:, 0:4], b_ss, 4, 1.0, 128 * EPS)
                        else:
                            rs, b_rs = cm.rstd(ss[:, 0:4], b_ss, 4, 1.0 / 128, EPS)
                        o, b_o, s_o = ob.next()
                        for h in range(4):
                            fw.op("dve", lambda e, h=h, ps=ps, rs=rs, o=o, g_s=g_s, gofs=gofs: e.scalar_tensor_tensor(
                                out=o[:, h * 128:(h + 1) * 128], in0=ps[:, h * 128:(h + 1) * 128], scalar=rs[:, h:h + 1],
                                in1=g_s[:, gofs + h * 128:gofs + (h + 1) * 128], op0=ALU.mult, op1=ALU.mult),
                                reads=[b_ps, b_rs, b_g], writes=[b_o])
                        outs.append(fw.dma("sp", oh[(tb + t) * 128:(tb + t + 1) * 128, n0:n0 + 512], o[:], s_o, reads=[b_o]))
                    elif bi < 10:
                        o, b_o, s_o = ob.next()
                        if t % 2 == 0:
                            fw.op("act", lambda e, ps=ps, o=o: e.activation(out=o[:], in_=ps[:], func=AF.Copy),
                                  reads=[b_ps], writes=[b_o])
                        else:
                            fw.op("dve", lambda e, ps=ps, o=o: e.tensor_copy(out=o[:], in_=ps[:]), reads=[b_ps], writes=[b_o])
                        outs.append(fw.dma("sp", oh[(tb + t) * 128:(tb + t + 1) * 128, n0:n0 + 512], o[:], s_o, reads=[b_o]))
                    else:
                        o, b_o, s_o = oab.next()
                        fw.op("dve", lambda e, ps=ps, o=o: e.tensor_copy(out=o[:], in_=ps[:, 0:16]), reads=[b_ps], writes=[b_o])
                        outs.append(fw.dma("sp", oa[(tb + t) * 128:(tb + t + 1) * 128, :], o[:], s_o, reads=[b_o]))
        for p0 in range(0, TOK, PT):
          tb = p0 // 128
          for t in range(NTP):
            xt, b_xt, s_xt = cm.xr.next()
            fw.dma("sp", xt[:], x[(tb + t) * 128:(tb + t + 1) * 128, :], s_xt, writes=[b_xt])
            cm.norm_T(xt[:], b_xt, gn_s, b_gn, hT, b_hT, t * 128)
          l1_body(tb)
        fw.wait("sp", outs[-8:])
        fw.emit()
    return nc


def l2_consts():
    i = np.arange(128)
    c = {}
    c["idn"] = np.eye(128, dtype=np.float32).astype(NPBF)
    c["mU"] = (i[:, None] <= i[None, :]).astype(np.float32).astype(NPBF)
    c["Uc"] = ((i[:, None] <= i[None, :]) * (-1.0 / 16)).astype(np.float32)
    c["U2"] = ((i[:, None] > i[None, :]) * (-1.0 / 16)).astype(np.float32)
    c["Lpn"] = (-(i[:, None] >= i[None, :]).astype(np.float32)).astype(NPBF)
    c["On"] = (-np.ones((128, 128), np.float32)).astype(NPBF)
    m4 = np.zeros((4, 128, 512), np.float32)
    tri = (i[:, None] < i[None, :]).astype(np.float32)
    for r in range(4):
        for qb in range(4):
            if qb == r:
                m4[r][:, qb * 128:(qb + 1) * 128] = tri
            elif qb > r:
                m4[r][:, qb * 128:(qb + 1) * 128] = 1.0
    c["M4"] = np.ascontiguousarray(m4.transpose(1, 0, 2)).astype(NPBF)
    return c


def build_l2(T):
    NCH = T // 128
    NG = T // 512
    GC = min(8, NCH)
    GLA_EVERY = max(1, (NG * (2 * NG + 2)) // NCH)
    nc = bass.Bass("TRN2", target_bir_lowering=False)
    di = lambda n, s, dt: nc.dram_tensor(n, s, dt, kind="ExternalInput").ap()
    sqT = di("sqT", [128, T], BF16); skT = di("skT", [128, T], BF16); sv = di("sv", [T, 128], BF16)
    gqT = di("gqT", [128, T], BF16); gkT = di("gkT", [128, T], BF16)
    gk = di("gk", [T, 128], BF16); gv = di("gv", [T, 128], BF16)
    ga = di("ga", [17, T], F32); wa = di("wa", [17, 128], F32)
    idn = di("idn", [128, 128], BF16); mU = di("mU", [128, 128], BF16)
    Uc = di("Uc", [128, 128], F32); U2 = di("U2", [128, 128], F32)
    Lpn = di("Lpn", [128, 128], BF16); On = di("On", [128, 128], BF16); M4 = di("M4", [128, 4, 512], BF16)
    sbT = nc.dram_tensor("sbT", [128, T], BF16, kind="ExternalOutput").ap()
    go = nc.dram_tensor("go", [T, 128], BF16, kind="ExternalOutput").ap()
    with contextlib.ExitStack() as st:
        fw = FW(nc, st)
        s_const = fw.new_sem("s_const")

        def const(name, src, shape, dt):
            t = fw.sbuf(name, shape, dt); b = Buf()
            fw.dma("sp", t[:], src, s_const, writes=[b])
            return t, b
        mU_s, b_mU = const("mU_s", mU, [128, 128], BF16)
        Uc_s, b_Uc = const("Uc_s", Uc, [128, 128], F32)
        U2_s, b_U2 = const("U2_s", U2, [128, 128], F32)
        Lpn_s, b_Lpn = const("Lpn_s", Lpn, [128, 128], BF16)
        On_s, b_On = const("On_s", On, [128, 128], BF16)
        M4_s, b_M4 = const("M4_s", M4, [128, 4, 512], BF16)
        wa_s, b_wa = const("wa_s", wa, [17, 128], F32)
        sqT_s, b_sqT = const("sqT_s", sqT, [128, T], BF16)
        skT_s, b_skT = const("skT_s", skT, [128, T], BF16)
        sv_s, b_sv = const("sv_s", sv.rearrange("(c p) d -> p c d", p=128), [128, NCH, 128], BF16)
        ps = Rot(fw, "ps", [128, 512], F32, 6, psum=True)
        pso = Rot(fw, "pso", [128, 512], F32, 2, psum=True)

        gq_r = Rot(fw, "gq_r", [128, GC * 128], BF16, 2, sem=True)
        gkT_r = Rot(fw, "gkT_r", [128, GC * 128], BF16, 2, sem=True)
        gk_r = Rot(fw, "gk_r", [128, GC, 128], BF16, 2, sem=True)
        gv_r = Rot(fw, "gv_r", [128, GC, 128], BF16, 2, sem=True)
        ga_r = Rot(fw, "ga_r", [17, GC * 128], F32, 2, sem=True)
        f32t = Rot(fw, "f32t", [128, 128], F32, 6)
        bft = Rot(fw, "bft", [128, 128], BF16, 12)
        gob = Rot(fw, "gob", [128, 128], BF16, 3, sem=True)
        dec_r = Rot(fw, "dec_r", [128, 1], F32, 3)
        S32 = fw.sbuf("S32", [128, 128], F32); b_S32 = Buf()
        Sbf = Rot(fw, "Sbf", [128, 128], BF16, 2)
        fw.op("dve", lambda e: e.memset(S32[:], 0.0), writes=[b_S32])
        sb_cur, b_sb_cur = Sbf.next()
        fw.op("dve", lambda e, t=sb_cur: e.memset(t[:], 0.0), writes=[b_sb_cur])
        outs = []
        gst = dict(grp=None, sb_cur=sb_cur, b_sb_cur=b_sb_cur, next=0, pairs=0)

        def gla_chunk(c):
            grp = gst["grp"]; sb_cur = gst["sb_cur"]; b_sb_cur = gst["b_sb_cur"]
            if c % GC == 0:
                g0 = c * 128
                tq, bq, sq_ = gq_r.next(); fw.dma("sp", tq[:], gqT[:, g0:g0 + GC * 128], sq_, writes=[bq])
                tk, bk, sk_ = gkT_r.next(); fw.dma("sp", tk[:], gkT[:, g0:g0 + GC * 128], sk_, writes=[bk])
                tkk, bkk, skk = gk_r.next(); fw.dma("sp", tkk[:], gk[g0:g0 + GC * 128, :].rearrange("(c p) d -> p c d", p=128), skk, writes=[bkk])
                tv, bv, sv_ = gv_r.next(); fw.dma("sp", tv[:], gv[g0:g0 + GC * 128, :].rearrange("(c p) d -> p c d", p=128), sv_, writes=[bv])
                ta, ba, sa_ = ga_r.next(); fw.dma("sp", ta[:], ga[:, g0:g0 + GC * 128], sa_, writes=[ba])
                grp = (tq, bq, tk, bk, tkk, bkk, tv, bv, ta, ba)
            tq, bq, tk, bk, tkk, bkk, tv, bv, ta, ba = grp
            j = c % GC
            cs = slice(j * 128, (j + 1) * 128)
            pu, b_pu = ps.next()
            fw.op("pe", lambda e, pu=pu, ta=ta, cs=cs: e.matmul(pu[:, 0:128], lhsT=ta[:, cs], rhs=wa_s[:], start=True, stop=True),
                  reads=[ba, b_wa], writes=[b_pu])
            ex, b_ex = f32t.next()
            fw.op("act", lambda e, pu=pu, ex=ex: e.activation(out=ex[:], in_=pu[:, 0:128], func=AF.Exp, scale=-1.0),
                  reads=[b_pu], writes=[b_ex])
            spt, b_spt = f32t.next()
            fw.op("act", lambda e, ex=ex, spt=spt: e.activation(out=spt[:], in_=ex[:], func=AF.Ln, bias=1.0),
                  reads=[b_ex], writes=[b_spt])
            pb, b_pb = ps.next()
            fw.op("pe", lambda e, pb=pb, spt=spt: e.matmul(pb[:, 0:128], lhsT=spt[:], rhs=Uc_s[:], start=True, stop=True),
                  reads=[b_spt, b_Uc], writes=[b_pb])
            fw.op("pe", lambda e, pb=pb, spt=spt: e.matmul(pb[:, 128:256], lhsT=U2_s[:], rhs=spt[:], start=True, stop=True),
                  reads=[b_spt, b_U2], writes=[b_pb])
            e1, b_e1 = f32t.next()
            fw.op("act", lambda e, pb=pb, e1=e1: e.activation(out=e1[:], in_=pb[:, 0:128], func=AF.Exp),
                  reads=[b_pb], writes=[b_e1])
            e2, b_e2 = f32t.next()
            fw.op("act", lambda e, pb=pb, e2=e2: e.activation(out=e2[:], in_=pb[:, 0:128], func=AF.Exp, scale=-1.0),
                  reads=[b_pb], writes=[b_e2])
            e3, b_e3 = f32t.next()
            fw.op("act", lambda e, pb=pb, e3=e3: e.activation(out=e3[:], in_=pb[:, 128:256], func=AF.Exp),
                  reads=[b_pb], writes=[b_e3])
            dec, b_dec = dec_r.next()
            fw.op("dve", lambda e, dec=dec, e1=e1: e.tensor_copy(out=dec[:], in_=e1[:, 127:128]), reads=[b_e1], writes=[b_dec])
            qe, b_qe = bft.next()
            fw.op("dve", lambda e, qe=qe, tq=tq, cs=cs, e1=e1: e.scalar_tensor_tensor(
                out=qe[:], in0=tq[:, cs], scalar=float(128 ** -0.5), in1=e1[:], op0=ALU.mult, op1=ALU.mult),
                reads=[bq, b_e1], writes=[b_qe])
            ke, b_ke = bft.next()
            fw.op("dve", lambda e, ke=ke, tk=tk, cs=cs, e2=e2: e.tensor_tensor(out=ke[:], in0=tk[:, cs], in1=e2[:], op=ALU.mult),
                  reads=[bk, b_e2], writes=[b_ke])
            kd, b_kd = bft.next()
            fw.op("pool", lambda e, kd=kd, tkk=tkk, j=j, e3=e3: e.tensor_tensor(out=kd[:], in0=tkk[:, j, :], in1=e3[:], op=ALU.mult),
                  reads=[bkk, b_e3], writes=[b_kd])
            pS, b_pS = ps.next()
            fw.op("pe", lambda e, pS=pS, ke=ke, qe=qe: e.matmul(pS[:, 0:128], lhsT=ke[:], rhs=qe[:], start=True, stop=True),
                  reads=[b_ke, b_qe], writes=[b_pS])
            sTm, b_sTm = bft.next()
            fw.op("dve", lambda e, sTm=sTm, pS=pS: e.tensor_tensor(out=sTm[:], in0=pS[:, 0:128], in1=mU_s[:], op=ALU.mult),
                  reads=[b_pS, b_mU], writes=[b_sTm])
            pO, b_pO = ps.next()
            fw.op("pe", lambda e, pO=pO, sTm=sTm, tv=tv, j=j: e.matmul(pO[:, 0:128], lhsT=sTm[:], rhs=tv[:, j, :], start=True, stop=False),
                  reads=[b_sTm, bv], writes=[b_pO])
            fw.op("pe", lambda e, pO=pO, qe=qe, sbc=sb_cur: e.matmul(pO[:, 0:128], lhsT=qe[:], rhs=sbc[:], start=False, stop=True),
                  reads=[b_qe, b_sb_cur], writes=[b_pO])
            ot, b_ot, s_ot = gob.next()
            fw.op("act", lambda e, ot=ot, pO=pO: e.activation(out=ot[:], in_=pO[:, 0:128], func=AF.Copy), reads=[b_pO], writes=[b_ot])
            outs.append(fw.dma("sp", go[c * 128:(c + 1) * 128, :], ot[:], s_ot, reads=[b_ot]))
            pK, b_pK = ps.next()
            fw.op("pe", lambda e, pK=pK, kd=kd, tv=tv, j=j: e.matmul(pK[:, 0:128], lhsT=kd[:], rhs=tv[:, j, :], start=True, stop=True),
                  reads=[b_kd, bv], writes=[b_pK])
            fw.op("dve", lambda e, pK=pK, dec=dec: e.scalar_tensor_tensor(out=S32[:], in0=S32[:], scalar=dec[:, 0:1], in1=pK[:, 0:128],
                                                                       op0=ALU.mult, op1=ALU.add),
                  reads=[b_pK, b_dec, b_S32], writes=[b_S32])
            sb_cur, b_sb_cur = Sbf.next()
            fw.op("dve", lambda e, t=sb_cur: e.tensor_copy(out=t[:], in_=S32[:]), reads=[b_S32], writes=[b_sb_cur])

            gst["grp"] = grp; gst["sb_cur"] = sb_cur; gst["b_sb_cur"] = b_sb_cur

        def gla_step():
            if gst["next"] < NCH:
                gla_chunk(gst["next"])
                gst["next"] += 1

        e32 = Rot(fw, "e32", [128, 512], F32, 2)
        spb = Rot(fw, "spb", [128, 512], BF16, 3)
        Ab = Rot(fw, "Ab", [128, 512], BF16, 3)
        SS = fw.sbuf("SS", [128, 512], F32); b_SS = Buf()
        SSb = Rot(fw, "SSb", [128, 512], BF16, 3)
        osb = Rot(fw, "osb", [128, 512], BF16, 2, sem=True)
        for g in range(NG):
            qs = slice(g * 512, (g + 1) * 512)
            kbs = list(range(4 * g + 3, -1, -1))
            pOa, b_pOa = pso.next()
            nkb = len(kbs)
            stA = {}

            def stageA(i, kb):
                ks = slice(kb * 128, (kb + 1) * 128)
                pz, b_pz = ps.next()
                fw.op("pe", lambda e, pz=pz, ks=ks, qs=qs: e.matmul(pz[:], lhsT=skT_s[:, ks], rhs=sqT_s[:, qs], start=True, stop=True),
                      reads=[b_skT, b_sqT], writes=[b_pz])
                ee, b_ee = e32.next()
                fw.op("act", lambda e, pz=pz, ee=ee: e.activation(out=ee[:], in_=pz[:], func=AF.Exp), reads=[b_pz], writes=[b_ee])
                sp, b_sp = spb.next()
                fw.op("act", lambda e, ee=ee, sp=sp: e.activation(out=sp[:], in_=ee[:], func=AF.Ln, bias=1.0), reads=[b_ee], writes=[b_sp])
                r = kb - 4 * g
                if r >= 0:
                    fw.op("dve", lambda e, sp=sp, r=r: e.tensor_tensor(out=sp[:], in0=sp[:], in1=M4_s[:, r, :], op=ALU.mult),
                          reads=[b_sp, b_M4], writes=[b_sp])
                if i == 0:
                    sprev = None
                    fw.op("pool", lambda e, sp=sp: e.tensor_copy(out=SS[:], in_=sp[:]), reads=[b_sp], writes=[b_SS])
                else:
                    sprev = stA[i - 1]["snext"]
                    fw.op("pool", lambda e, sp=sp: e.tensor_tensor(out=SS[:], in0=SS[:], in1=sp[:], op=ALU.add),
                          reads=[b_sp, b_SS], writes=[b_SS])
                sn, b_sn = SSb.next()
                fw.op("dve", lambda e, sn=sn: e.tensor_copy(out=sn[:], in_=SS[:]), reads=[b_SS], writes=[b_sn])
                stA[i] = dict(ks=ks, sp=(sp, b_sp), sprev=sprev, snext=(sn, b_sn), r=r, kb=kb)

            def stageB(i):
                d = stA[i]
                ks = d["ks"]; sp, b_sp = d["sp"]
                pl, b_pl = ps.next()
                last = d["sprev"] is None
                fw.op("pe", lambda e, pl=pl, ks=ks, qs=qs: e.matmul(pl[:], lhsT=skT_s[:, ks], rhs=sqT_s[:, qs], start=True, stop=False),
                      reads=[b_skT, b_sqT], writes=[b_pl])
                fw.op("pe", lambda e, pl=pl, sp=sp, last=last: e.matmul(pl[:], lhsT=Lpn_s[:], rhs=sp[:], start=False, stop=last),
                      reads=[b_Lpn, b_sp], writes=[b_pl])
                if not last:
                    sv_, b_sv_ = d["sprev"]
                    fw.op("pe", lambda e, pl=pl, sv_=sv_: e.matmul(pl[:], lhsT=On_s[:], rhs=sv_[:], start=False, stop=True),
                          reads=[b_On, b_sv_], writes=[b_pl])
                A, b_A = Ab.next()
                fw.op("act", lambda e, pl=pl, A=A: e.activation(out=A[:], in_=pl[:], func=AF.Exp), reads=[b_pl], writes=[b_A])
                if d["r"] >= 0:
                    fw.op("dve", lambda e, A=A, r=d["r"]: e.tensor_tensor(out=A[:], in0=A[:], in1=M4_s[:, r, :], op=ALU.mult),
                          reads=[b_A, b_M4], writes=[b_A])
                kb = d["kb"]
                fw.op("pe", lambda e, A=A, kb=kb, i=i, pOa=pOa, nkb=nkb: e.matmul(pOa[:], lhsT=sv_s[:, kb, :], rhs=A[:], start=(i == 0), stop=(i == nkb - 1)),
                      reads=[b_sv, b_A], writes=[b_pOa])
                del stA[i]["sp"]

            stageA(0, kbs[0])
            for i in range(len(kbs)):
                if i + 1 < len(kbs):
                    stageA(i + 1, kbs[i + 1])
                if gst["pairs"] % GLA_EVERY == 0:
                    gla_step()
                gst["pairs"] += 1
                stageB(i)
            o, b_o, s_o = osb.next()
            fw.op("act", lambda e, o=o, pOa=pOa: e.activation(out=o[:], in_=pOa[:], func=AF.Copy), reads=[b_pOa], writes=[b_o])
            outs.append(fw.dma("sp", sbT[:, qs], o[:], s_o, reads=[b_o]))
        while gst["next"] < NCH:
            gla_step()
        fw.wait("sp", outs[-6:])
        fw.emit()
    return nc


def build_l3(TOK):
    NB = TOK // TB
    NTB = TB // 128
    nc = bass.Bass("TRN2", target_bir_lowering=False)
    di = lambda n, s, dt: nc.dram_tensor(n, s, dt, kind="ExternalInput").ap()
    x = di("x", [TOK, D], F32)
    sbT = di("sbT", [1024, TOK], BF16)
    go = di("go", [TOK, 1024], BF16)
    mem = di("mem", [NMEM, D], F32)
    wl = di("wl", [D, NL], F32)
    wbr = [di(n, [1024, D], F32) for n in ("wbg", "wbs", "wbm")]
    wkv = di("wkv", [D, 2048], F32)
    wo = di("wo", [D, D], F32)
    wgu = di("wgu", [D, 2 * DFF], F32)
    wd = di("wd", [DFF, D], F32)
    an = di("an", [128, 16], F32); fn = di("fn", [128, 16], F32); mn = di("mn", [128, 16], F32)
    gon = di("gon", [128, 1024], F32)
    mqn = di("mqn", [128, 2], F32); mkn = di("mkn", [128, 2], F32)
    idn = di("idn", [128, 128], BF16)
    y = nc.dram_tensor("y", [TOK, D], F32, kind="ExternalOutput").ap()
    with contextlib.ExitStack() as st:
        fw = FW(nc, st)
        cm = Common(fw, nc)

        def const(name, src, shape, dt):
            t = fw.sbuf(name, shape, dt); b = Buf()
            cm.load_const(t[:], src, b)
            return t, b
        cm.load_const(cm.ident[:], idn, cm.b_ident)
        an_s, b_an = const("an_s", an, [128, 16], F32)
        fn_s, b_fn = const("fn_s", fn, [128, 16], F32)
        mn_s, b_mn = const("mn_s", mn, [128, 16], F32)
        gon_s, b_gon = const("gon_s", gon, [128, 1024], F32)
        mqn_s, b_mqn = const("mqn_s", mqn, [128, 2], F32)
        mkn_s, b_mkn = const("mkn_s", mkn, [128, 2], F32)

        if PRECAST:
            wl = cm.precast("wl", wl, D, NL)
            wbr = [cm.precast(n, w, 1024, D) for n, w in zip(("wbg", "wbs", "wbm"), wbr)]
            wo = cm.precast("wo", wo, D, D)
            wgu = cm.precast("wgu", wgu, D, 2 * DFF)
            wd = cm.precast("wd", wd, DFF, D)
        big = fw.sbuf("big", [128, 48, TB], BF16)
        b_big = [Buf() for _ in range(48)]
        hT = fw.sbuf("hT", [128, 16, TB], BF16); b_hT = Buf()
        mT = fw.sbuf("mT", [128, 16, TB], BF16); b_mT = [Buf() for _ in range(16)]
        ogT = fw.sbuf("ogT", [128, 8, TB], BF16); b_ogT = Buf()
        omT = fw.sbuf("omT", [128, 8, TB], BF16); b_omT = Buf()
        sbTs = fw.sbuf("sbTs", [128, 8, TB], BF16); b_sbTs = Buf(); s_sbTs = fw.new_sem("s_sbTs")
        sgr = fw.sbuf("sgr", [128, NTB, 1024], BF16); b_sgr = [Buf() for _ in range(NTB)]
        qm = fw.sbuf("qm", [128, NTB, 1024], BF16); b_qm = [Buf() for _ in range(NTB)]
        x1s = fw.sbuf("x1s", [128, NTB, D], F32); b_x1s = [Buf() for _ in range(NTB)]
        mkT = fw.sbuf("mkT", [128, 4, 2, NMEM], BF16); b_mkT = Buf()
        mv = fw.sbuf("mv", [128, 2, 1024], BF16); b_mv = Buf()
        mktm = fw.sbuf("mktm", [128, 2, 1024], BF16); b_mktm = Buf()
        sq32 = fw.sbuf("sq32", [128, 1024], F32); b_sq32 = Buf()
        t32 = Rot(fw, "t32", [128, 512], F32, 4)
        gor = Rot(fw, "gor", [128, 1024], BF16, 2, sem=True)
        ogb = Rot(fw, "ogb", [128, 1024], BF16, 2)
        og32 = fw.sbuf("og32", [128, 1024], F32); b_og32 = Buf()
        qmT = fw.sbuf("qmT", [128, 8, 128], BF16); b_qmT = Buf()
        pbf = Rot(fw, "pbf", [128, 256], BF16, 2)
        pTs = fw.sbuf("pTs", [128, 2, 128], BF16); b_pTs = Buf()
        ys = Rot(fw, "ys", [128, 512], F32, 3, sem=True)
        mg32 = fw.sbuf("mg32", [128, 16, TB], F32); b_mg = [Buf() for _ in range(16)]

        def mm_tm(lhs_fn, b_lhs, nk, wb, b_wb, ncols, ps, b_ps, first=True, last=True):
            for kc in range(nk):
                fw.op("pe", lambda e, kc=kc: e.matmul(ps[:, 0:ncols], lhsT=lhs_fn(kc), rhs=wb[:, kc, 0:ncols],
                                                      start=(first and kc == 0), stop=(last and kc == nk - 1)),
                      reads=list(b_lhs) + [b_wb], writes=[b_ps])

        def mm_fm(wb, b_wb, c0, nk, rhs_fn, b_rhs, ps, b_ps):
            for kc in range(nk):
                fw.op("pe", lambda e, kc=kc: e.matmul(ps[:, 0:TB], lhsT=wb[:, kc, c0:c0 + 128], rhs=rhs_fn(kc),
                                                      start=(kc == 0), stop=(kc == nk - 1)),
                      reads=list(b_rhs) + [b_wb], writes=[b_ps])

        def head_rstd(src_ap, b_src, nh, dh, scale, bias):
            fw.op("act", lambda e: e.activation(out=sq32[:, 0:nh * dh], in_=src_ap, func=AF.Square),
                  reads=[b_src], writes=[b_sq32])
            ss, b_ss = cm.st.next()
            fw.op("dve", lambda e: e.tensor_reduce(out=ss[:, 0:nh], in_=sq32[:, 0:nh * dh].rearrange("p (h d) -> p h d", h=nh),
                                                   axis=AX.X, op=ALU.add), reads=[b_sq32], writes=[b_ss])
            return cm.rstd(ss[:, 0:nh], b_ss, nh, scale, bias)

        memT = big
        for mt in range(2):
            xt, b_xt, s_xt = cm.xr.next()
            fw.dma("sp", xt[:], mem[mt * 128:(mt + 1) * 128, :], s_xt, writes=[b_xt])
            cm.norm_T(xt[:], b_xt, mn_s, b_mn, memT, b_big[0], mt * 128)
        for nb in range(4):
            wb, b_wb = cm.load_w(wkv, 0, 16, nb * 512, 512)
            for mt in range(2):
                ps, b_ps = cm.ps.next()
                mm_tm(lambda kc, mt=mt: memT[:, kc, mt * 128:(mt + 1) * 128], [b_big[0]], 16, wb, b_wb, 512, ps, b_ps)
                if nb < 2:
                    rs, b_rs = head_rstd(ps[:], b_ps, 2, 256, 1.0 / 256, EPS)
                    for h in range(2):
                        fw.op("dve", lambda e, h=h, ps=ps, rs=rs, mt=mt, nb=nb: e.tensor_scalar(
                            out=mktm[:, mt, nb * 512 + h * 256: nb * 512 + (h + 1) * 256], in0=ps[:, h * 256:(h + 1) * 256],
                            scalar1=rs[:, h:h + 1], scalar2=None, op0=ALU.mult), reads=[b_ps, b_rs], writes=[b_mktm])
                else:
                    fw.op("act", lambda e, ps=ps, mt=mt, nb=nb: e.activation(out=mv[:, mt, (nb - 2) * 512:(nb - 1) * 512], in_=ps[:], func=AF.Copy),
                          reads=[b_ps], writes=[b_mv])
        for mt in range(2):
            cm.transposes(mktm[:, mt, :], b_mktm, 8,
                          lambda c, mt=mt: mkT[:, c // 2, c % 2, mt * 128:(mt + 1) * 128], b_mkT,
                          lambda c: mkn_s[:, (c % 2):(c % 2) + 1], b_mkn)

        outs = []
        for blk in range(NB):
            t0 = blk * TB
            for i in range(NTB):
                xt, b_xt, s_xt = cm.xr.next()
                fw.dma("sp", xt[:], x[t0 + i * 128: t0 + (i + 1) * 128, :], s_xt, writes=[b_xt])
                cm.norm_T(xt[:], b_xt, an_s, b_an, hT, b_hT, i * 128)
            fw.dma("sp", sbTs[:], sbT[:, t0:t0 + TB].rearrange("(c p) t -> p c t", p=128), s_sbTs, writes=[b_sbTs])
            for nb in range(4):
                wb, b_wb = cm.load_w(wl, 0, 16, nb * 512, 512)
                for i in range(NTB):
                    ps, b_ps = cm.ps.next()
                    mm_tm(lambda kc, i=i: hT[:, kc, i * 128:(i + 1) * 128], [b_hT], 16, wb, b_wb, 512, ps, b_ps)
                    if nb < 2:
                        fw.op("act", lambda e, ps=ps, i=i, nb=nb: e.activation(out=sgr[:, i, nb * 512:(nb + 1) * 512], in_=ps[:], func=AF.Silu),
                              reads=[b_ps], writes=[b_sgr[i]])
                    else:
                        rs, b_rs = head_rstd(ps[:], b_ps, 2, 256, 1.0, 256 * EPS)
                        for h in range(2):
                            fw.op("dve", lambda e, h=h, ps=ps, rs=rs, i=i, nb=nb: e.tensor_scalar(
                                out=qm[:, i, (nb - 2) * 512 + h * 256:(nb - 2) * 512 + (h + 1) * 256], in0=ps[:, h * 256:(h + 1) * 256],
                                scalar1=rs[:, h:h + 1], scalar2=None, op0=ALU.mult), reads=[b_ps, b_rs], writes=[b_qm[i]])
            for nb in range(12):
                wb, b_wb = cm.load_w(wl, 0, 16, 2048 + nb * 512, 512)
                for s in range(4):
                    ch = nb * 4 + s
                    ps, b_ps = cm.ps.next()
                    mm_fm(wb, b_wb, s * 128, 16, lambda kc: hT[:, kc, :], [b_hT], ps, b_ps)
                    fw.op("act", lambda e, ps=ps, ch=ch: e.activation(out=big[:, ch, :], in_=ps[:, 0:TB], func=AF.Sigmoid),
                          reads=[b_ps], writes=[b_big[ch]])
            for i in range(NTB):
                gt, b_gt, s_gt = gor.next()
                fw.dma("sp", gt[:], go[t0 + i * 128:t0 + (i + 1) * 128, :], s_gt, writes=[b_gt])
                rs, b_rs = head_rstd(gt[:], b_gt, 4, 256, 1.0 / 256, EPS)
                for h in range(4):
                    hs = slice(h * 256, (h + 1) * 256)
                    fw.op("dve", lambda e, h=h, hs=hs, gt=gt, rs=rs: e.scalar_tensor_tensor(
                        out=og32[:, hs], in0=gt[:, hs], scalar=rs[:, h:h + 1], in1=gon_s[:, hs], op0=ALU.mult, op1=ALU.mult),
                        reads=[b_gt, b_rs, b_gon], writes=[b_og32])
                ob_, b_ob = ogb.next()
                fw.op("dve", lambda e, ob_=ob_, i=i: e.tensor_tensor(out=ob_[:], in0=og32[:], in1=sgr[:, i, :], op=ALU.mult),
                      reads=[b_og32, b_sgr[i]], writes=[b_ob])
                cm.transposes(ob_, b_ob, 8, lambda c, i=i: ogT[:, c, i * 128:(i + 1) * 128], b_ogT)
            for i in range(NTB):
                cm.transposes(qm[:, i, :], b_qm[i], 8, lambda c: qmT[:, c, :], b_qmT,
                              lambda c: mqn_s[:, (c % 2):(c % 2) + 1], b_mqn)
                for h in range(4):
                    ps, b_ps = cm.ps.next()
                    for c2 in range(2):
                        fw.op("pe", lambda e, ps=ps, h=h, c2=c2: e.matmul(ps[:, 0:NMEM], lhsT=qmT[:, h * 2 + c2, :], rhs=mkT[:, h, c2, :],
                                                                         start=(c2 == 0), stop=(c2 == 1)),
                              reads=[b_qmT, b_mkT], writes=[b_ps])
                    mx, b_mx = cm.st.next()
                    fw.op("dve", lambda e, ps=ps, mx=mx: e.tensor_reduce(out=mx[:, 0:1], in_=ps[:, 0:NMEM], axis=AX.X, op=ALU.max),
                          reads=[b_ps], writes=[b_mx])
                    fw.op("dve", lambda e, mx=mx: e.tensor_scalar(out=mx[:, 1:2], in0=mx[:, 0:1], scalar1=-1.0, scalar2=None, op0=ALU.mult),
                          reads=[b_mx], writes=[b_mx])
                    pe_, b_pe = t32.next()
                    fw.op("act", lambda e, ps=ps, mx=mx, pe_=pe_: e.activation(out=pe_[:, 0:NMEM], in_=ps[:, 0:NMEM], func=AF.Exp,
                                                                             bias=mx[:, 1:2], accum_out=mx[:, 2:3]),
                          reads=[b_ps, b_mx], writes=[b_pe, b_mx])
                    fw.op("dve", lambda e, mx=mx: e.reciprocal(out=mx[:, 3:4], in_=mx[:, 2:3]), reads=[b_mx], writes=[b_mx])
                    pb_, b_pb = pbf.next()
                    fw.op("dve", lambda e, pb_=pb_, pe_=pe_, mx=mx: e.tensor_scalar(out=pb_[:], in0=pe_[:, 0:NMEM], scalar1=mx[:, 3:4],
                                                                                  scalar2=None, op0=ALU.mult),
                          reads=[b_pe, b_mx], writes=[b_pb])
                    cm.transposes(pb_, b_pb, 2, lambda c: pTs[:, c, :], b_pTs)
                    for c2 in range(2):
                        ps2, b_ps2 = cm.ps.next()
                        for mc in range(2):
                            fw.op("pe", lambda e, ps2=ps2, h=h, c2=c2, mc=mc: e.matmul(
                                ps2[:, 0:128], lhsT=mv[:, mc, h * 256 + c2 * 128: h * 256 + (c2 + 1) * 128], rhs=pTs[:, mc, :],
                                start=(mc == 0), stop=(mc == 1)), reads=[b_mv, b_pTs], writes=[b_ps2])
                        fw.op("act", lambda e, ps2=ps2, h=h, c2=c2, i=i: e.activation(out=omT[:, h * 2 + c2, i * 128:(i + 1) * 128],
                                                                                    in_=ps2[:, 0:128], func=AF.Copy),
                              reads=[b_ps2], writes=[b_omT])
            srcs = [(ogT, b_ogT), (sbTs, b_sbTs), (omT, b_omT)]
            for br in range(3):
                src, b_src = srcs[br]
                for nb in range(4):
                    wb, b_wb = cm.load_w(wbr[br], 0, 8, nb * 512, 512)
                    for s in range(4):
                        ncn = nb * 4 + s
                        gch = br * 16 + ncn
                        ps, b_ps = cm.ps.next()
                        mm_fm(wb, b_wb, s * 128, 8, lambda kc, src=src: src[:, kc, :], [b_src], ps, b_ps)
                        if br == 0:
                            fw.op("dve", lambda e, ps=ps, ncn=ncn, gch=gch: e.tensor_tensor(out=mg32[:, ncn, :], in0=ps[:, 0:TB], in1=big[:, gch, :], op=ALU.mult),
                                  reads=[b_ps, b_big[gch]], writes=[b_mg[ncn]])
                        else:
                            tt, b_tt = t32.next()
                            fw.op("dve", lambda e, ps=ps, tt=tt, gch=gch: e.tensor_tensor(out=tt[:, 0:TB], in0=ps[:, 0:TB], in1=big[:, gch, :], op=ALU.mult),
                                  reads=[b_ps, b_big[gch]], writes=[b_tt])
                            if br == 1:
                                fw.op("pool", lambda e, tt=tt, ncn=ncn: e.tensor_tensor(out=mg32[:, ncn, :], in0=mg32[:, ncn, :], in1=tt[:, 0:TB], op=ALU.add),
                                      reads=[b_tt, b_mg[ncn]], writes=[b_mg[ncn]])
                            else:
                                fw.op("pool", lambda e, tt=tt, ncn=ncn: e.tensor_tensor(out=mT[:, ncn, :], in0=mg32[:, ncn, :], in1=tt[:, 0:TB], op=ALU.add),
                                      reads=[b_tt, b_mg[ncn]], writes=[b_mT[ncn]])
            for nb in range(4):
                wb, b_wb = cm.load_w(wo, 0, 16, nb * 512, 512)
                for i in range(NTB):
                    ps, b_ps = cm.ps.next()
                    mm_tm(lambda kc, i=i: mT[:, kc, i * 128:(i + 1) * 128], b_mT, 16, wb, b_wb, 512, ps, b_ps)
                    xt, b_xt, s_xt = ys.next()
                    fw.dma("sp", xt[:], x[t0 + i * 128:t0 + (i + 1) * 128, nb * 512:(nb + 1) * 512], s_xt, writes=[b_xt])
                    fw.op("dve", lambda e, ps=ps, xt=xt, i=i, nb=nb: e.tensor_tensor(out=x1s[:, i, nb * 512:(nb + 1) * 512], in0=ps[:], in1=xt[:], op=ALU.add),
                          reads=[b_ps, b_xt], writes=[b_x1s[i]])
            for i in range(NTB):
                cm.norm_T(x1s[:, i, :], b_x1s[i], fn_s, b_fn, hT, b_hT, i * 128)
            for nb in range(11):
                wg_, b_wg = cm.load_w(wgu, 0, 16, nb * 512, 512)
                sgs = []
                for s in range(4):
                    ps, b_ps = cm.ps.next()
                    mm_fm(wg_, b_wg, s * 128, 16, lambda kc: hT[:, kc, :], [b_hT], ps, b_ps)
                    sg, b_sg = t32.next()
                    fw.op("act", lambda e, ps=ps, sg=sg: e.activation(out=sg[:, 0:TB], in_=ps[:, 0:TB], func=AF.Silu), reads=[b_ps], writes=[b_sg])
                    sgs.append((sg, b_sg))
                wu_, b_wu = cm.load_w(wgu, 0, 16, DFF + nb * 512, 512)
                for s in range(4):
                    ch = nb * 4 + s
                    ps, b_ps = cm.ps.next()
                    mm_fm(wu_, b_wu, s * 128, 16, lambda kc: hT[:, kc, :], [b_hT], ps, b_ps)
                    sg, b_sg = sgs[s]
                    fw.op("dve", lambda e, ps=ps, sg=sg, ch=ch: e.tensor_tensor(out=big[:, ch, :], in0=ps[:, 0:TB], in1=sg[:, 0:TB], op=ALU.mult),
                          reads=[b_ps, b_sg], writes=[b_big[ch]])
            kgs = [(0, 16), (16, 16), (32, 12)]
            for nb in range(4):
                pss = [cm.ps.next() for _ in range(NTB)]
                for gi, (k0, nk) in enumerate(kgs):
                    wb, b_wb = cm.load_w(wd, k0, nk, nb * 512, 512)
                    for i in range(NTB):
                        ps, b_ps = pss[i]
                        mm_tm(lambda kc, i=i, k0=k0: big[:, k0 + kc, i * 128:(i + 1) * 128], b_big[k0:k0 + nk], nk, wb, b_wb, 512, ps, b_ps,
                              first=(gi == 0), last=(gi == 2))
                for i in range(NTB):
                    ps, b_ps = pss[i]
                    yt, b_yt, s_yt = ys.next()
                    fw.op("dve", lambda e, ps=ps, yt=yt, i=i, nb=nb: e.tensor_tensor(out=yt[:], in0=ps[:], in1=x1s[:, i, nb * 512:(nb + 1) * 512], op=ALU.add),
                          reads=[b_ps, b_x1s[i]], writes=[b_yt])
                    outs.append(fw.dma("sp", y[t0 + i * 128:t0 + (i + 1) * 128, nb * 512:(nb + 1) * 512], yt[:], s_yt, reads=[b_yt]))
        fw.wait("sp", outs[-6:])
        fw.emit()
    return nc


_CACHE = {}


def _get(name, fn, *a):
    k = (name,) + a
    if k not in _CACHE:
        _CACHE[k] = fn(*a)
    return _CACHE[k]


def _fm(g):
    return np.ascontiguousarray(np.asarray(g, np.float32).reshape(-1, 128).T)


def _bc(g, rep):
    return np.ascontiguousarray(np.broadcast_to(np.tile(np.asarray(g, np.float32), rep)[None, :], (128, g.shape[0] * rep)))


def run_l1(x2, wh, attn_norm, sbq, sbk):
    T = x2.shape[0]
    TOK = T // K1
    nc = _get("l1", build_l1, TOK)
    idn = np.eye(128, dtype=np.float32).astype(NPBF)
    ims = [dict(x=x2[c * TOK:(c + 1) * TOK], wh=wh, gn=_fm(attn_norm), qg=_bc(sbq, 8), kg=_bc(sbk, 8), idn=idn) for c in range(K1)]
    res = run_bass_kernel_spmd(nc, ims, core_ids=list(range(K1))).results
    oh = np.concatenate([r["oh"] for r in res], 0)
    oa = np.concatenate([r["oa"] for r in res], 0)
    return oh, oa


def run_l2(oh, oa, w_a2, b_a):
    T = oh.shape[0]
    nc = _get("l2", build_l2, T)
    cs = l2_consts()
    sq, sk, sv = oh[:, 0:1024], oh[:, 1024:2048], oh[:, 2048:3072]
    gq, gk, gv = oh[:, 3072:3584], oh[:, 3584:4096], oh[:, 4096:5120]
    ga = np.ascontiguousarray(np.concatenate([oa.T, np.ones((1, T), np.float32)], 0))
    ims = []
    for c in range(8):
        hg, half = c // 2, c % 2
        hs = slice(c * 128, (c + 1) * 128)
        gs = slice(hg * 128, (hg + 1) * 128)
        vs = slice(hg * 256 + half * 128, hg * 256 + (half + 1) * 128)
        wa = np.ascontiguousarray(np.concatenate([w_a2[:, gs], b_a[None, gs]], 0).astype(np.float32))
        ims.append(dict(sqT=np.ascontiguousarray(sq[:, hs].T), skT=np.ascontiguousarray(sk[:, hs].T), sv=np.ascontiguousarray(sv[:, hs]),
                        gqT=np.ascontiguousarray(gq[:, gs].T), gkT=np.ascontiguousarray(gk[:, gs].T),
                        gk=np.ascontiguousarray(gk[:, gs]), gv=np.ascontiguousarray(gv[:, vs]), ga=ga, wa=wa, **cs))
    res = run_bass_kernel_spmd(nc, ims, core_ids=list(range(8))).results
    sbT = np.concatenate([r["sbT"] for r in res], 0)
    go = np.concatenate([r["go"] for r in res], 1)
    return sbT, go


def run_l3(x2, sbT, go, mem2, p):
    T = x2.shape[0]
    TOK = T // K3
    nc = _get("l3", build_l3, TOK)
    idn = np.eye(128, dtype=np.float32).astype(NPBF)
    ims = []
    for c in range(K3):
        ts = slice(c * TOK, (c + 1) * TOK)
        ims.append(dict(x=x2[ts], sbT=np.ascontiguousarray(sbT[:, ts]), go=np.ascontiguousarray(go[ts]), mem=mem2,
                        wl=p["wl"], wbg=p["wbg"], wbs=p["wbs"], wbm=p["wbm"], wkv=p["wkv"], wo=p["wo"], wgu=p["wgu"], wd=p["wd"],
                        an=_fm(p["attn_norm"]), fn=_fm(p["ffn_norm"]), mn=_fm(p["mem_norm"]), gon=_bc(p["gla_out_norm"], 4),
                        mqn=np.ascontiguousarray(p["mem_q_norm"].reshape(2, 128).T), mkn=np.ascontiguousarray(p["mem_k_norm"].reshape(2, 128).T),
                        idn=idn))
    res = run_bass_kernel_spmd(nc, ims, core_ids=list(range(K3))).results
    return np.concatenate([r["y"] for r in res], 0)


def split_w_in(w):
    gq, gk, gv, gr = w[:, 0:512], w[:, 512:1024], w[:, 1024:2048], w[:, 2048:3072]
    ga1 = w[:, 3072:3088]
    sq, sk, sv = w[:, 3088:4112], w[:, 4112:5136], w[:, 5136:6160]
    mq, gates = w[:, 6160:7184], w[:, 7184:13328]
    wh = np.ascontiguousarray(np.concatenate([sq, sk, sv, gq, gk, gv, ga1], 1))
    wl = np.ascontiguousarray(np.concatenate([gr, mq, gates], 1))
    return wh, wl


def kernel(x, mem, attn_norm, w_in, gla_w_a2, gla_b_a, gla_out_norm, w_br_gla,
           sb_q_norm, sb_k_norm, w_br_sb, mem_norm, w_mem_kv, mem_q_norm, mem_k_norm,
           w_br_mem, w_o, ffn_norm, w_gate_up, w_down):
    f = lambda a: np.asarray(a, np.float32)
    x2 = np.ascontiguousarray(f(x)[0])
    mem2 = np.ascontiguousarray(f(mem)[0])
    for l in range(w_in.shape[0]):
        wh, wl = split_w_in(f(w_in[l]))
        oh, oa = run_l1(x2, wh, f(attn_norm[l]), f(sb_q_norm[l]), f(sb_k_norm[l]))
        sbT, go = run_l2(oh, oa, f(gla_w_a2[l]), f(gla_b_a[l]))
        p = dict(wl=wl, wbg=f(w_br_gla[l]), wbs=f(w_br_sb[l]), wbm=f(w_br_mem[l]), wkv=f(w_mem_kv[l]), wo=f(w_o[l]),
                 wgu=f(w_gate_up[l]), wd=f(w_down[l]), attn_norm=f(attn_norm[l]), ffn_norm=f(ffn_norm[l]),
                 mem_norm=f(mem_norm[l]), gla_out_norm=f(gla_out_norm[l]), mem_q_norm=f(mem_q_norm[l]),
                 mem_k_norm=f(mem_k_norm[l]))
        x2 = run_l3(x2, sbT, go, mem2, p)
    return x2[None].astype(np.float32)
```

```python
import contextlib
import numpy as np
import ml_dtypes
import concourse.bass as bass
import concourse.mybir as mybir
from concourse.bass_utils import run_bass_kernel_spmd

F32 = mybir.dt.float32
BF16 = mybir.dt.bfloat16
AF = mybir.ActivationFunctionType
ALU = mybir.AluOpType
AX = mybir.AxisListType
NPBF = ml_dtypes.bfloat16

D = 2048
DFF = 5632
NMEM = 256
EPS = 1e-6
NH1 = 5136
NL = 8192
K1 = 8
K3 = 8
L1P = 2048
TB = 256
PRECAST = True


class Sem:
    def __init__(self, h):
        self.h = h
        self.count = 0


class Buf:
    __slots__ = ("w", "r")

    def __init__(self):
        self.w = None
        self.r = []


class FW:
    ENGS = ("pe", "act", "dve", "pool", "sp")
    M = 8

    def __init__(self, nc, stack):
        self.nc = nc
        self.stack = stack
        self.q = {e: [] for e in self.ENGS}
        self.esem = {e: [self.new_sem(f"e_{e}{j}") for j in range(self.M)] for e in self.ENGS}
        self.eidx = {e: 0 for e in self.ENGS}
        self.seen_e = {e: {s: 0 for s in self.ENGS} for e in self.ENGS}
        self.seen_d = {e: {} for e in self.ENGS}
        self.n = 0

    def new_sem(self, name):
        return Sem(self.stack.enter_context(self.nc.semaphore(name)))

    def sbuf(self, name, shape, dt):
        return self.stack.enter_context(self.nc.sbuf_tensor(name, list(shape), dt))

    def psum(self, name, shape, dt):
        return self.stack.enter_context(self.nc.psum_tensor(name, list(shape), dt))

    def _waits(self, eng, deps):
        out = {}
        for t in deps:
            if t is None:
                continue
            kind, src, val = t
            if kind == "e":
                if src == "pe" and eng == "pe":
                    continue
                if self.seen_e[eng][src] >= val:
                    continue
                key = ("e", src)
                if out.get(key, 0) < val:
                    out[key] = val
            else:
                if self.seen_d[eng].get(src, 0) >= val:
                    continue
                key = ("d", src)
                if out.get(key, 0) < val:
                    out[key] = val
        res = []
        for (kind, src), val in out.items():
            if kind == "e":
                self.seen_e[eng][src] = val
                j = (val - 1) % self.M
                res.append((self.esem[src][j].h, (val - 1) // self.M + 1))
            else:
                self.seen_d[eng][src] = val
                res.append((src.h, val))
        return res

    @staticmethod
    def _deps(reads, writes, extra):
        deps = list(extra)
        for b in reads:
            deps.append(b.w)
        for b in writes:
            deps.append(b.w)
            deps.extend(b.r)
        return deps

    @staticmethod
    def _commit(tok, reads, writes):
        for b in reads:
            b.r.append(tok)
            if len(b.r) > 48:
                del b.r[:-48]
        for b in writes:
            b.w = tok
            b.r = []

    def op(self, eng, fn, reads=(), writes=(), deps=()):
        waits = self._waits(eng, self._deps(reads, writes, deps))
        self.eidx[eng] += 1
        idx = self.eidx[eng]
        tok = ("e", eng, idx)
        s = self.esem[eng][(idx - 1) % self.M]
        self.q[eng].append((waits, fn, (s.h, 1)))
        self.n += 1 + len(waits)
        self._commit(tok, reads, writes)
        return tok

    def dma(self, eng, out, in_, sem, reads=(), writes=(), deps=()):
        waits = self._waits(eng, self._deps(reads, writes, deps))
        sem.count += 16
        tok = ("d", sem, sem.count)
        self.q[eng].append((waits, lambda e: e.dma_start(out=out, in_=in_), (sem.h, 16)))
        self.n += 1 + len(waits)
        self._commit(tok, reads, writes)
        return tok

    def wait(self, eng, deps):
        waits = self._waits(eng, deps)
        if waits:
            self.q[eng].append((waits, None, None))

    def emit(self):
        q = self.q

        def run(e, lst):
            for waits, fn, inc in lst:
                for (h, v) in waits:
                    e.wait_ge(h, v)
                if fn is not None:
                    fn(e).then_inc(inc[0], inc[1])

        with self.nc.Block() as block:
            @block.tensor
            def _(e):
                run(e, q["pe"])

            @block.scalar
            def _(e):
                run(e, q["act"])

            @block.vector
            def _(e):
                run(e, q["dve"])

            @block.gpsimd
            def _(e):
                run(e, q["pool"])

            @block.sync
            def _(e):
                run(e, q["sp"])


class Rot:
    def __init__(self, fw, name, shape, dt, n, psum=False, sem=False):
        self.t = [(fw.psum if psum else fw.sbuf)(f"{name}{i}", shape, dt) for i in range(n)]
        self.b = [Buf() for _ in range(n)]
        self.s = [fw.new_sem(f"s_{name}{i}") for i in range(n)] if sem else None
        self.i = -1
        self.n = n

    def next(self):
        self.i = (self.i + 1) % self.n
        if self.s:
            return self.t[self.i], self.b[self.i], self.s[self.i]
        return self.t[self.i], self.b[self.i]


class Common:
    def __init__(self, fw, nc):
        self.fw = fw
        self.nc = nc
        self.ident = fw.sbuf("ident", [128, 128], BF16)
        self.b_ident = Buf()
        self.s_const = fw.new_sem("s_const")
        self.xr = Rot(fw, "xr", [128, D], F32, 2, sem=True)
        self.hb = Rot(fw, "hb", [128, D], BF16, 2)
        self.junk = fw.sbuf("junk", [128, D], BF16)
        self.b_junk = Buf()
        self.st = Rot(fw, "st", [128, 8], F32, 4)
        self.ps = Rot(fw, "ps", [128, 512], F32, 6, psum=True)
        self.pt = Rot(fw, "pt", [128, 1024], BF16, 2, psum=True)
        self.wb = Rot(fw, "wb", [128, 16, 512], BF16, 2, sem=True)

    def load_const(self, dst, src, b):
        self.fw.dma("sp", dst, src, self.s_const, writes=[b])

    def rstd(self, ssq_ap, b_ssq, n, scale, bias):
        fw = self.fw
        st, b_st = self.st.next()
        fw.op("act", lambda e: e.activation(out=st[:, 0:n], in_=ssq_ap, func=AF.Sqrt, scale=scale, bias=bias),
              reads=[b_ssq], writes=[b_st])
        fw.op("dve", lambda e: e.reciprocal(out=st[:, 0:n], in_=st[:, 0:n]), reads=[b_st], writes=[b_st])
        return st, b_st

    def norm_T(self, src, b_src, gain_fm, b_gain, dstT, b_dst, col0):
        fw = self.fw
        ss, b_ss = self.st.next()
        fw.op("act", lambda e: e.activation(out=self.junk[:], in_=src, func=AF.Square, accum_out=ss[:, 0:1]),
              reads=[b_src], writes=[self.b_junk, b_ss])
        rs, b_rs = self.rstd(ss[:, 0:1], b_ss, 1, 1.0 / D, EPS)
        hb, b_hb = self.hb.next()
        fw.op("dve", lambda e: e.tensor_scalar(out=hb[:], in0=src, scalar1=rs[:, 0:1], scalar2=None, op0=ALU.mult),
              reads=[b_src, b_rs], writes=[b_hb])
        self.transposes(hb, b_hb, D // 128, lambda c: dstT[:, c, col0:col0 + 128], b_dst,
                        lambda c: gain_fm[:, c:c + 1], b_gain)

    def transposes(self, src, b_src, nch, dst_fn, b_dst, scale_fn=None, b_scale=None):
        fw = self.fw
        for c0 in range(0, nch, 8):
            pt, b_pt = self.pt.next()
            m = min(8, nch - c0)
            for j in range(m):
                c = c0 + j
                fw.op("pe", lambda e, c=c, j=j, pt=pt: e.transpose(out=pt[:, j * 128:(j + 1) * 128],
                                                                 in_=src[:, c * 128:(c + 1) * 128],
                                                                 identity=self.ident[:]),
                      reads=[b_src, self.b_ident], writes=[b_pt])
            for j in range(m):
                c = c0 + j
                eng = "dve" if j % 2 == 0 else "act"
                rd = [b_pt] + ([b_scale] if b_scale is not None else [])
                if scale_fn is None:
                    if eng == "dve":
                        fw.op("dve", lambda e, c=c, j=j, pt=pt: e.tensor_copy(out=dst_fn(c), in_=pt[:, j * 128:(j + 1) * 128]),
                              reads=rd, writes=[b_dst])
                    else:
                        fw.op("act", lambda e, c=c, j=j, pt=pt: e.activation(out=dst_fn(c), in_=pt[:, j * 128:(j + 1) * 128], func=AF.Copy),
                              reads=rd, writes=[b_dst])
                else:
                    if eng == "dve":
                        fw.op("dve", lambda e, c=c, j=j, pt=pt: e.tensor_scalar(out=dst_fn(c), in0=pt[:, j * 128:(j + 1) * 128],
                                                                               scalar1=scale_fn(c), scalar2=None, op0=ALU.mult),
                              reads=rd, writes=[b_dst])
                    else:
                        fw.op("act", lambda e, c=c, j=j, pt=pt: e.activation(out=dst_fn(c), in_=pt[:, j * 128:(j + 1) * 128],
                                                                            func=AF.Copy, scale=scale_fn(c)),
                              reads=rd, writes=[b_dst])

    def load_w(self, w_ap, k0, nk, n0, ncols):
        wb, b_wb, s_wb = self.wb.next()
        if isinstance(w_ap, tuple):
            t, b_w = w_ap
            assert n0 % 512 == 0 and ncols == 512
            src = t[n0 // 512][:, k0:k0 + nk, :]
            self.fw.dma("pool", wb[:, 0:nk, 0:ncols], src, s_wb, reads=[b_w], writes=[b_wb])
        else:
            src = w_ap[k0 * 128:(k0 + nk) * 128, n0:n0 + ncols].rearrange("(c p) n -> p c n", p=128)
            self.fw.dma("pool", wb[:, 0:nk, 0:ncols], src, s_wb, writes=[b_wb])
        return wb, b_wb

    def precast(self, name, w_ap, K, N):
        KC, NBK = K // 128, N // 512
        t = self.nc.dram_tensor(name + "_bf", [NBK, 128, KC, 512], BF16)
        b = Buf()
        sem = self.fw.new_sem("s_pc_" + name)
        tok = None
        for nb in range(NBK):
            src = w_ap[:, nb * 512:(nb + 1) * 512].rearrange("(c p) n -> p c n", p=128)
            tok = self.fw.dma("pool", t[nb], src, sem)
        b.w = tok
        return (t, b)


def build_l1(TOK):
    PT = min(TOK, L1P)
    NTP = PT // 128
    nc = bass.Bass("TRN2", target_bir_lowering=False)
    x = nc.dram_tensor("x", [TOK, D], F32, kind="ExternalInput").ap()
    wh = nc.dram_tensor("wh", [D, NH1], F32, kind="ExternalInput").ap()
    gn = nc.dram_tensor("gn", [128, 16], F32, kind="ExternalInput").ap()
    qg = nc.dram_tensor("qg", [128, 1024], F32, kind="ExternalInput").ap()
    kg = nc.dram_tensor("kg", [128, 1024], F32, kind="ExternalInput").ap()
    idn = nc.dram_tensor("idn", [128, 128], BF16, kind="ExternalInput").ap()
    oh = nc.dram_tensor("oh", [TOK, 5120], BF16, kind="ExternalOutput").ap()
    oa = nc.dram_tensor("oa", [TOK, 16], F32, kind="ExternalOutput").ap()
    with contextlib.ExitStack() as st:
        fw = FW(nc, st)
        cm = Common(fw, nc)
        gn_s = fw.sbuf("gn_s", [128, 16], F32); b_gn = Buf()
        qg_s = fw.sbuf("qg_s", [128, 1024], F32); b_qg = Buf()
        kg_s = fw.sbuf("kg_s", [128, 1024], F32); b_kg = Buf()
        hT = fw.sbuf("hT", [128, 16, PT], BF16); b_hT = Buf()
        sqs = fw.sbuf("sqs", [128, 512], F32); b_sqs = Buf()
        ob = Rot(fw, "ob", [128, 512], BF16, 3, sem=True)
        oab = Rot(fw, "oab", [128, 16], F32, 2, sem=True)
        cm.load_const(cm.ident[:], idn, cm.b_ident)
        cm.load_const(gn_s[:], gn, b_gn)
        cm.load_const(qg_s[:], qg, b_qg)
        cm.load_const(kg_s[:], kg, b_kg)
        outs = []
        nblocks = [(i * 512, 512) for i in range(10)] + [(5120, 16)]
        def l1_body(tb):
            for bi, (n0, ncols) in enumerate(nblocks):
                wb, b_wb = cm.load_w(wh, 0, 16, n0, ncols)
                for t in range(NTP):
                    ps, b_ps = cm.ps.next()
                    for kc in range(16):
                        fw.op("pe", lambda e, kc=kc, t=t, ps=ps, wb=wb, ncols=ncols: e.matmul(
                            ps[:, 0:ncols], lhsT=hT[:, kc, t * 128:(t + 1) * 128], rhs=wb[:, kc, 0:ncols],
                            start=(kc == 0), stop=(kc == 15)), reads=[b_hT, b_wb], writes=[b_ps])
                    if bi < 4:
                        isq = bi < 2
                        g_s, b_g = (qg_s, b_qg) if isq else (kg_s, b_kg)
                        gofs = (bi % 2) * 512
                        fw.op("act", lambda e, ps=ps: e.activation(out=sqs[:], in_=ps[:], func=AF.Square),
                              reads=[b_ps], writes=[b_sqs])
                        ss, b_ss = cm.st.next()
                        fw.op("dve", lambda e, ss=ss: e.tensor_reduce(out=ss[:, 0:4], in_=sqs[:].rearrange("p (h d) -> p h d", h=4),
                                                                     axis=AX.X, op=ALU.add), reads=[b_sqs], writes=[b_ss])
                        if isq:
                            rs, b_rs = cm.rstd(ss[:, 0:4], b_ss, 4, 1.0, 128 * EPS)
                        else:
                            rs, b_rs = cm.rstd(ss[:, 0:4], b_ss, 4, 1.0 / 128, EPS)
                        o, b_o, s_o = ob.next()
                        for h in range(4):
                            fw.op("dve", lambda e, h=h, ps=ps, rs=rs, o=o, g_s=g_s, gofs=gofs: e.scalar_tensor_tensor(
                                out=o[:, h * 128:(h + 1) * 128], in0=ps[:, h * 128:(h + 1) * 128], scalar=rs[:, h:h + 1],
                                in1=g_s[:, gofs + h * 128:gofs + (h + 1) * 128], op0=ALU.mult, op1=ALU.mult),
                                reads=[b_ps, b_rs, b_g], writes=[b_o])
                        outs.append(fw.dma("sp", oh[(tb + t) * 128:(tb + t + 1) * 128, n0:n0 + 512], o[:], s_o, reads=[b_o]))
                    elif bi < 10:
                        o, b_o, s_o = ob.next()
                        if t % 2 == 0:
                            fw.op("act", lambda e, ps=ps, o=o: e.activation(out=o[:], in_=ps[:], func=AF.Copy),
                                  reads=[b_ps], writes=[b_o])
                        else:
                            fw.op("dve", lambda e, ps=ps, o=o: e.tensor_copy(out=o[:], in_=ps[:]), reads=[b_ps], writes=[b_o])
                        outs.append(fw.dma("sp", oh[(tb + t) * 128:(tb + t + 1) * 128, n0:n0 + 512], o[:], s_o, reads=[b_o]))
                    else:
                        o, b_o, s_o = oab.next()
                        fw.op("dve", lambda e, ps=ps, o=o: e.tensor_copy(out=o[:], in_=ps[:, 0:16]), reads=[b_ps], writes=[b_o])
                        outs.append(fw.dma("sp", oa[(tb + t) * 128:(tb + t + 1) * 128, :], o[:], s_o, reads=[b_o]))
        for p0 in range(0, TOK, PT):
          tb = p0 // 128
          for t in range(NTP):
            xt, b_xt, s_xt = cm.xr.next()
            fw.dma("sp", xt[:], x[(tb + t) * 128:(tb + t + 1) * 128, :], s_xt, writes=[b_xt])
            cm.norm_T(xt[:], b_xt, gn_s, b_gn, hT, b_hT, t * 128)
          l1_body(tb)
        fw.wait("sp", outs[-8:])
        fw.emit()
    return nc


def l2_consts():
    i = np.arange(128)
    c = {}
    c["idn"] = np.eye(128, dtype=np.float32).astype(NPBF)
    c["mU"] = (i[:, None] <= i[None, :]).astype(np.float32).astype(NPBF)
    c["Uc"] = ((i[:, None] <= i[None, :]) * (-1.0 / 16)).astype(np.float32)
    c["U2"] = ((i[:, None] > i[None, :]) * (-1.0 / 16)).astype(np.float32)
    c["Lpn"] = (-(i[:, None] >= i[None, :]).astype(np.float32)).astype(NPBF)
    c["On"] = (-np.ones((128, 128), np.float32)).astype(NPBF)
    m4 = np.zeros((4, 128, 512), np.float32)
    tri = (i[:, None] < i[None, :]).astype(np.float32)
    for r in range(4):
        for qb in range(4):
            if qb == r:
                m4[r][:, qb * 128:(qb + 1) * 128] = tri
            elif qb > r:
                m4[r][:, qb * 128:(qb + 1) * 128] = 1.0
    c["M4"] = np.ascontiguousarray(m4.transpose(1, 0, 2)).astype(NPBF)
    return c


def build_l2(T):
    NCH = T // 128
    NG = T // 512
    GC = min(8, NCH)
    GLA_EVERY = max(1, (NG * (2 * NG + 2)) // NCH)
    nc = bass.Bass("TRN2", target_bir_lowering=False)
    di = lambda n, s, dt: nc.dram_tensor(n, s, dt, kind="ExternalInput").ap()
    sqT = di("sqT", [128, T], BF16); skT = di("skT", [128, T], BF16); sv = di("sv", [T, 128], BF16)
    gqT = di("gqT", [128, T], BF16); gkT = di("gkT", [128, T], BF16)
    gk = di("gk", [T, 128], BF16); gv = di("gv", [T, 128], BF16)
    ga = di("ga", [17, T], F32); wa = di("wa", [17, 128], F32)
    idn = di("idn", [128, 128], BF16); mU = di("mU", [128, 128], BF16)
    Uc = di("Uc", [128, 128], F32); U2 = di("U2", [128, 128], F32)
    Lpn = di("Lpn", [128, 128], BF16); On = di("On", [128, 128], BF16); M4 = di("M4", [128, 4, 512], BF16)
    sbT = nc.dram_tensor("sbT", [128, T], BF16, kind="ExternalOutput").ap()
    go = nc.dram_tensor("go", [T, 128], BF16, kind="ExternalOutput").ap()
    with contextlib.ExitStack() as st:
        fw = FW(nc, st)
        s_const = fw.new_sem("s_const")

        def const(name, src, shape, dt):
            t = fw.sbuf(name, shape, dt); b = Buf()
            fw.dma("sp", t[:], src, s_const, writes=[b])
            return t, b
        mU_s, b_mU = const("mU_s", mU, [128, 128], BF16)
        Uc_s, b_Uc = const("Uc_s", Uc, [128, 128], F32)
        U2_s, b_U2 = const("U2_s", U2, [128, 128], F32)
        Lpn_s, b_Lpn = const("Lpn_s", Lpn, [128, 128], BF16)
        On_s, b_On = const("On_s", On, [128, 128], BF16)
        M4_s, b_M4 = const("M4_s", M4, [128, 4, 512], BF16)
        wa_s, b_wa = const("wa_s", wa, [17, 128], F32)
        sqT_s, b_sqT = const("sqT_s", sqT, [128, T], BF16)
        skT_s, b_skT = const("skT_s", skT, [128, T], BF16)
        sv_s, b_sv = const("sv_s", sv.rearrange("(c p) d -> p c d", p=128), [128, NCH, 128], BF16)
        ps = Rot(fw, "ps", [128, 512], F32, 6, psum=True)
        pso = Rot(fw, "pso", [128, 512], F32, 2, psum=True)

        gq_r = Rot(fw, "gq_r", [128, GC * 128], BF16, 2, sem=True)
        gkT_r = Rot(fw, "gkT_r", [128, GC * 128], BF16, 2, sem=True)
        gk_r = Rot(fw, "gk_r", [128, GC, 128], BF16, 2, sem=True)
        gv_r = Rot(fw, "gv_r", [128, GC, 128], BF16, 2, sem=True)
        ga_r = Rot(fw, "ga_r", [17, GC * 128], F32, 2, sem=True)
        f32t = Rot(fw, "f32t", [128, 128], F32, 6)
        bft = Rot(fw, "bft", [128, 128], BF16, 12)
        gob = Rot(fw, "gob", [128, 128], BF16, 3, sem=True)
        dec_r = Rot(fw, "dec_r", [128, 1], F32, 3)
        S32 = fw.sbuf("S32", [128, 128], F32); b_S32 = Buf()
        Sbf = Rot(fw, "Sbf", [128, 128], BF16, 2)
        fw.op("dve", lambda e: e.memset(S32[:], 0.0), writes=[b_S32])
        sb_cur, b_sb_cur = Sbf.next()
        fw.op("dve", lambda e, t=sb_cur: e.memset(t[:], 0.0), writes=[b_sb_cur])
        outs = []
        gst = dict(grp=None, sb_cur=sb_cur, b_sb_cur=b_sb_cur, next=0, pairs=0)

        def gla_chunk(c):
            grp = gst["grp"]; sb_cur = gst["sb_cur"]; b_sb_cur = gst["b_sb_cur"]
            if c % GC == 0:
                g0 = c * 128
                tq, bq, sq_ = gq_r.next(); fw.dma("sp", tq[:], gqT[:, g0:g0 + GC * 128], sq_, writes=[bq])
                tk, bk, sk_ = gkT_r.next(); fw.dma("sp", tk[:], gkT[:, g0:g0 + GC * 128], sk_, writes=[bk])
                tkk, bkk, skk = gk_r.next(); fw.dma("sp", tkk[:], gk[g0:g0 + GC * 128, :].rearrange("(c p) d -> p c d", p=128), skk, writes=[bkk])
                tv, bv, sv_ = gv_r.next(); fw.dma("sp", tv[:], gv[g0:g0 + GC * 128, :].rearrange("(c p) d -> p c d", p=128), sv_, writes=[bv])
                ta, ba, sa_ = ga_r.next(); fw.dma("sp", ta[:], ga[:, g0:g0 + GC * 128], sa_, writes=[ba])
                grp = (tq, bq, tk, bk, tkk, bkk, tv, bv, ta, ba)
            tq, bq, tk, bk, tkk, bkk, tv, bv, ta, ba = grp
            j = c % GC
            cs = slice(j * 128, (j + 1) * 128)
            pu, b_pu = ps.next()
            fw.op("pe", lambda e, pu=pu, ta=ta, cs=cs: e.matmul(pu[:, 0:128], lhsT=ta[:, cs], rhs=wa_s[:], start=True, stop=True),
                  reads=[ba, b_wa], writes=[b_pu])
            ex, b_ex = f32t.next()
            fw.op("act", lambda e, pu=pu, ex=ex: e.activation(out=ex[:], in_=pu[:, 0:128], func=AF.Exp, scale=-1.0),
                  reads=[b_pu], writes=[b_ex])
            spt, b_spt = f32t.next()
            fw.op("act", lambda e, ex=ex, spt=spt: e.activation(out=spt[:], in_=ex[:], func=AF.Ln, bias=1.0),
                  reads=[b_ex], writes=[b_spt])
            pb, b_pb = ps.next()
            fw.op("pe", lambda e, pb=pb, spt=spt: e.matmul(pb[:, 0:128], lhsT=spt[:], rhs=Uc_s[:], start=True, stop=True),
                  reads=[b_spt, b_Uc], writes=[b_pb])
            fw.op("pe", lambda e, pb=pb, spt=spt: e.matmul(pb[:, 128:256], lhsT=U2_s[:], rhs=spt[:], start=True, stop=True),
                  reads=[b_spt, b_U2], writes=[b_pb])
            e1, b_e1 = f32t.next()
            fw.op("act", lambda e, pb=pb, e1=e1: e.activation(out=e1[:], in_=pb[:, 0:128], func=AF.Exp),
                  reads=[b_pb], writes=[b_e1])
            e2, b_e2 = f32t.next()
            fw.op("act", lambda e, pb=pb, e2=e2: e.activation(out=e2[:], in_=pb[:, 0:128], func=AF.Exp, scale=-1.0),
                  reads=[b_pb], writes=[b_e2])
            e3, b_e3 = f32t.next()
            fw.op("act", lambda e, pb=pb, e3=e3: e.activation(out=e3[:], in_=pb[:, 128:256], func=AF.Exp),
                  reads=[b_pb], writes=[b_e3])
            dec, b_dec = dec_r.next()
            fw.op("dve", lambda e, dec=dec, e1=e1: e.tensor_copy(out=dec[:], in_=e1[:, 127:128]), reads=[b_e1], writes=[b_dec])
            qe, b_qe = bft.next()
            fw.op("dve", lambda e, qe=qe, tq=tq, cs=cs, e1=e1: e.scalar_tensor_tensor(
                out=qe[:], in0=tq[:, cs], scalar=float(128 ** -0.5), in1=e1[:], op0=ALU.mult, op1=ALU.mult),
                reads=[bq, b_e1], writes=[b_qe])
            ke, b_ke = bft.next()
            fw.op("dve", lambda e, ke=ke, tk=tk, cs=cs, e2=e2: e.tensor_tensor(out=ke[:], in0=tk[:, cs], in1=e2[:], op=ALU.mult),
                  reads=[bk, b_e2], writes=[b_ke])
            kd, b_kd = bft.next()
            fw.op("pool", lambda e, kd=kd, tkk=tkk, j=j, e3=e3: e.tensor_tensor(out=kd[:], in0=tkk[:, j, :], in1=e3[:], op=ALU.mult),
                  reads=[bkk, b_e3], writes=[b_kd])
            pS, b_pS = ps.next()
            fw.op("pe", lambda e, pS=pS, ke=ke, qe=qe: e.matmul(pS[:, 0:128], lhsT=ke[:], rhs=qe[:], start=True, stop=True),
                  reads=[b_ke, b_qe], writes=[b_pS])
            sTm, b_sTm = bft.next()
            fw.op("dve", lambda e, sTm=sTm, pS=pS: e.tensor_tensor(out=sTm[:], in0=pS[:, 0:128], in1=mU_s[:], op=ALU.mult),
                  reads=[b_pS, b_mU], writes=[b_sTm])
            pO, b_pO = ps.next()
            fw.op("pe", lambda e, pO=pO, sTm=sTm, tv=tv, j=j: e.matmul(pO[:, 0:128], lhsT=sTm[:], rhs=tv[:, j, :], start=True, stop=False),
                  reads=[b_sTm, bv], writes=[b_pO])
            fw.op("pe", lambda e, pO=pO, qe=qe, sbc=sb_cur: e.matmul(pO[:, 0:128], lhsT=qe[:], rhs=sbc[:], start=False, stop=True),
                  reads=[b_qe, b_sb_cur], writes=[b_pO])
            ot, b_ot, s_ot = gob.next()
            fw.op("act", lambda e, ot=ot, pO=pO: e.activation(out=ot[:], in_=pO[:, 0:128], func=AF.Copy), reads=[b_pO], writes=[b_ot])
            outs.append(fw.dma("sp", go[c * 128:(c + 1) * 128, :], ot[:], s_ot, reads=[b_ot]))
            pK, b_pK = ps.next()
            fw.op("pe", lambda e, pK=pK, kd=kd, tv=tv, j=j: e.matmul(pK[:, 0:128], lhsT=kd[:], rhs=tv[:, j, :], start=True, stop=True),
                  reads=[b_kd, bv], writes=[b_pK])
            fw.op("dve", lambda e, pK=pK, dec=dec: e.scalar_tensor_tensor(out=S32[:], in0=S32[:], scalar=dec[:, 0:1], in1=pK[:, 0:128],
                                                                       op0=ALU.mult, op1=ALU.add),
                  reads=[b_pK, b_dec, b_S32], writes=[b_S32])
            sb_cur, b_sb_cur = Sbf.next()
            fw.op("dve", lambda e, t=sb_cur: e.tensor_copy(out=t[:], in_=S32[:]), reads=[b_S32], writes=[b_sb_cur])

            gst["grp"] = grp; gst["sb_cur"] = sb_cur; gst["b_sb_cur"] = b_sb_cur

        def gla_step():
            if gst["next"] < NCH:
                gla_chunk(gst["next"])
                gst["next"] += 1

        e32 = Rot(fw, "e32", [128, 512], F32, 2)
        spb = Rot(fw, "spb", [128, 512], BF16, 4)
        Ab = Rot(fw, "Ab", [128, 512], BF16, 4)
        SS = fw.sbuf("SS", [128, 512], F32); b_SS = Buf()
        SSb = Rot(fw, "SSb", [128, 512], BF16, 4)
        osb = Rot(fw, "osb", [128, 512], BF16, 2, sem=True)
        for g in range(NG):
            qs = slice(g * 512, (g + 1) * 512)
            kbs = list(range(4 * g + 3, -1, -1))
            pOa, b_pOa = pso.next()
            nkb = len(kbs)
            stA = {}

            def stageA(i, kb):
                ks = slice(kb * 128, (kb + 1) * 128)
                pz, b_pz = ps.next()
                fw.op("pe", lambda e, pz=pz, ks=ks, qs=qs: e.matmul(pz[:], lhsT=skT_s[:, ks], rhs=sqT_s[:, qs], start=True, stop=True),
                      reads=[b_skT, b_sqT], writes=[b_pz])
                ee, b_ee = e32.next()
                fw.op("act", lambda e, pz=pz, ee=ee: e.activation(out=ee[:], in_=pz[:], func=AF.Exp), reads=[b_pz], writes=[b_ee])
                sp, b_sp = spb.next()
                fw.op("act", lambda e, ee=ee, sp=sp: e.activation(out=sp[:], in_=ee[:], func=AF.Ln, bias=1.0), reads=[b_ee], writes=[b_sp])
                r = kb - 4 * g
                if r >= 0:
                    fw.op("dve", lambda e, sp=sp, r=r: e.tensor_tensor(out=sp[:], in0=sp[:], in1=M4_s[:, r, :], op=ALU.mult),
                          reads=[b_sp, b_M4], writes=[b_sp])
                if i == 0:
                    sprev = None
                    fw.op("pool", lambda e, sp=sp: e.tensor_copy(out=SS[:], in_=sp[:]), reads=[b_sp], writes=[b_SS])
                else:
                    sprev = stA[i - 1]["snext"]
                    fw.op("pool", lambda e, sp=sp: e.tensor_tensor(out=SS[:], in0=SS[:], in1=sp[:], op=ALU.add),
                          reads=[b_sp, b_SS], writes=[b_SS])
                sn, b_sn = SSb.next()
                fw.op("dve", lambda e, sn=sn: e.tensor_copy(out=sn[:], in_=SS[:]), reads=[b_SS], writes=[b_sn])
                stA[i] = dict(ks=ks, sp=(sp, b_sp), sprev=sprev, snext=(sn, b_sn), r=r, kb=kb)

            def stageB(i):
                d = stA[i]
                ks = d["ks"]; sp, b_sp = d["sp"]
                pl, b_pl = ps.next()
                last = d["sprev"] is None
                fw.op("pe", lambda e, pl=pl, ks=ks, qs=qs: e.matmul(pl[:], lhsT=skT_s[:, ks], rhs=sqT_s[:, qs], start=True, stop=False),
                      reads=[b_skT, b_sqT], writes=[b_pl])
                fw.op("pe", lambda e, pl=pl, sp=sp, last=last: e.matmul(pl[:], lhsT=Lpn_s[:], rhs=sp[:], start=False, stop=last),
                      reads=[b_Lpn, b_sp], writes=[b_pl])
                if not last:
                    sv_, b_sv_ = d["sprev"]
                    fw.op("pe", lambda e, pl=pl, sv_=sv_: e.matmul(pl[:], lhsT=On_s[:], rhs=sv_[:], start=False, stop=True),
                          reads=[b_On, b_sv_], writes=[b_pl])
                A, b_A = Ab.next()
                fw.op("act", lambda e, pl=pl, A=A: e.activation(out=A[:], in_=pl[:], func=AF.Exp), reads=[b_pl], writes=[b_A])
                if d["r"] >= 0:
                    fw.op("dve", lambda e, A=A, r=d["r"]: e.tensor_tensor(out=A[:], in0=A[:], in1=M4_s[:, r, :], op=ALU.mult),
                          reads=[b_A, b_M4], writes=[b_A])
                d["A"] = (A, b_A)
                del stA[i]["sp"]

            def stageC(i):
                d = stA[i]
                A, b_A = d["A"]
                kb = d["kb"]
                fw.op("pe", lambda e, A=A, kb=kb, i=i, pOa=pOa, nkb=nkb: e.matmul(pOa[:], lhsT=sv_s[:, kb, :], rhs=A[:], start=(i == 0), stop=(i == nkb - 1)),
                      reads=[b_sv, b_A], writes=[b_pOa])
                del stA[i]["A"]

            stageA(0, kbs[0])
            stageA(1, kbs[1])
            stageB(0)
            for i in range(nkb):
                if i + 2 < nkb:
                    stageA(i + 2, kbs[i + 2])
                if gst["pairs"] % GLA_EVERY == 0:
                    gla_step()
                gst["pairs"] += 1
                if i + 1 < nkb:
                    stageB(i + 1)
                stageC(i)
            o, b_o, s_o = osb.next()
            fw.op("act", lambda e, o=o, pOa=pOa: e.activation(out=o[:], in_=pOa[:], func=AF.Copy), reads=[b_pOa], writes=[b_o])
            outs.append(fw.dma("sp", sbT[:, qs], o[:], s_o, reads=[b_o]))
        while gst["next"] < NCH:
            gla_step()
        fw.wait("sp", outs[-6:])
        fw.emit()
    return nc


def build_l3(TOK):
    NB = TOK // TB
    NTB = TB // 128
    nc = bass.Bass("TRN2", target_bir_lowering=False)
    di = lambda n, s, dt: nc.dram_tensor(n, s, dt, kind="ExternalInput").ap()
    x = di("x", [TOK, D], F32)
    sbT = di("sbT", [1024, TOK], BF16)
    go = di("go", [TOK, 1024], BF16)
    mem = di("mem", [NMEM, D], F32)
    wl = di("wl", [D, NL], F32)
    wbr = [di(n, [1024, D], F32) for n in ("wbg", "wbs", "wbm")]
    wkv = di("wkv", [D, 2048], F32)
    wo = di("wo", [D, D], F32)
    wgu = di("wgu", [D, 2 * DFF], F32)
    wd = di("wd", [DFF, D], F32)
    an = di("an", [128, 16], F32); fn = di("fn", [128, 16], F32); mn = di("mn", [128, 16], F32)
    gon = di("gon", [128, 1024], F32)
    mqn = di("mqn", [128, 2], F32); mkn = di("mkn", [128, 2], F32)
    idn = di("idn", [128, 128], BF16)
    y = nc.dram_tensor("y", [TOK, D], F32, kind="ExternalOutput").ap()
    with contextlib.ExitStack() as st:
        fw = FW(nc, st)
        cm = Common(fw, nc)

        def const(name, src, shape, dt):
            t = fw.sbuf(name, shape, dt); b = Buf()
            cm.load_const(t[:], src, b)
            return t, b
        cm.load_const(cm.ident[:], idn, cm.b_ident)
        an_s, b_an = const("an_s", an, [128, 16], F32)
        fn_s, b_fn = const("fn_s", fn, [128, 16], F32)
        mn_s, b_mn = const("mn_s", mn, [128, 16], F32)
        gon_s, b_gon = const("gon_s", gon, [128, 1024], F32)
        mqn_s, b_mqn = const("mqn_s", mqn, [128, 2], F32)
        mkn_s, b_mkn = const("mkn_s", mkn, [128, 2], F32)

        if PRECAST:
            wl = cm.precast("wl", wl, D, NL)
            wbr = [cm.precast(n, w, 1024, D) for n, w in zip(("wbg", "wbs", "wbm"), wbr)]
            wo = cm.precast("wo", wo, D, D)
            wgu = cm.precast("wgu", wgu, D, 2 * DFF)
            wd = cm.precast("wd", wd, DFF, D)
        big = fw.sbuf("big", [128, 48, TB], BF16)
        b_big = [Buf() for _ in range(48)]
        hT = fw.sbuf("hT", [128, 16, TB], BF16); b_hT = Buf()
        mT = fw.sbuf("mT", [128, 16, TB], BF16); b_mT = [Buf() for _ in range(16)]
        ogT = fw.sbuf("ogT", [128, 8, TB], BF16); b_ogT = Buf()
        omT = fw.sbuf("omT", [128, 8, TB], BF16); b_omT = Buf()
        sbTs = fw.sbuf("sbTs", [128, 8, TB], BF16); b_sbTs = Buf(); s_sbTs = fw.new_sem("s_sbTs")
        sgr = fw.sbuf("sgr", [128, NTB, 1024], BF16); b_sgr = [Buf() for _ in range(NTB)]
        qm = fw.sbuf("qm", [128, NTB, 1024], BF16); b_qm = [Buf() for _ in range(NTB)]
        x1s = fw.sbuf("x1s", [128, NTB, D], F32); b_x1s = [Buf() for _ in range(NTB)]
        mkT = fw.sbuf("mkT", [128, 4, 2, NMEM], BF16); b_mkT = Buf()
        mv = fw.sbuf("mv", [128, 2, 1024], BF16); b_mv = Buf()
        mktm = fw.sbuf("mktm", [128, 2, 1024], BF16); b_mktm = Buf()
        sq32 = fw.sbuf("sq32", [128, 1024], F32); b_sq32 = Buf()
        t32 = Rot(fw, "t32", [128, 512], F32, 4)
        gor = Rot(fw, "gor", [128, 1024], BF16, 2, sem=True)
        ogb = Rot(fw, "ogb", [128, 1024], BF16, 2)
        og32 = fw.sbuf("og32", [128, 1024], F32); b_og32 = Buf()
        qmT = fw.sbuf("qmT", [128, 8, 128], BF16); b_qmT = Buf()
        pbf = Rot(fw, "pbf", [128, 256], BF16, 2)
        pTs = fw.sbuf("pTs", [128, 2, 128], BF16); b_pTs = Buf()
        ys = Rot(fw, "ys", [128, 512], F32, 3, sem=True)
        mg32 = fw.sbuf("mg32", [128, 16, TB], F32); b_mg = [Buf() for _ in range(16)]

        def mm_tm(lhs_fn, b_lhs, nk, wb, b_wb, ncols, ps, b_ps, first=True, last=True):
            for kc in range(nk):
                fw.op("pe", lambda e, kc=kc: e.matmul(ps[:, 0:ncols], lhsT=lhs_fn(kc), rhs=wb[:, kc, 0:ncols],
                                                      start=(first and kc == 0), stop=(last and kc == nk - 1)),
                      reads=list(b_lhs) + [b_wb], writes=[b_ps])

        def mm_fm(wb, b_wb, c0, nk, rhs_fn, b_rhs, ps, b_ps):
            for kc in range(nk):
                fw.op("pe", lambda e, kc=kc: e.matmul(ps[:, 0:TB], lhsT=wb[:, kc, c0:c0 + 128], rhs=rhs_fn(kc),
                                                      start=(kc == 0), stop=(kc == nk - 1)),
                      reads=list(b_rhs) + [b_wb], writes=[b_ps])

        def head_rstd(src_ap, b_src, nh, dh, scale, bias):
            fw.op("act", lambda e: e.activation(out=sq32[:, 0:nh * dh], in_=src_ap, func=AF.Square),
                  reads=[b_src], writes=[b_sq32])
            ss, b_ss = cm.st.next()
            fw.op("dve", lambda e: e.tensor_reduce(out=ss[:, 0:nh], in_=sq32[:, 0:nh * dh].rearrange("p (h d) -> p h d", h=nh),
                                                   axis=AX.X, op=ALU.add), reads=[b_sq32], writes=[b_ss])
            return cm.rstd(ss[:, 0:nh], b_ss, nh, scale, bias)

        memT = big
        for mt in range(2):
            xt, b_xt, s_xt = cm.xr.next()
            fw.dma("sp", xt[:], mem[mt * 128:(mt + 1) * 128, :], s_xt, writes=[b_xt])
            cm.norm_T(xt[:], b_xt, mn_s, b_mn, memT, b_big[0], mt * 128)
        for nb in range(4):
            wb, b_wb = cm.load_w(wkv, 0, 16, nb * 512, 512)
            for mt in range(2):
                ps, b_ps = cm.ps.next()
                mm_tm(lambda kc, mt=mt: memT[:, kc, mt * 128:(mt + 1) * 128], [b_big[0]], 16, wb, b_wb, 512, ps, b_ps)
                if nb < 2:
                    rs, b_rs = head_rstd(ps[:], b_ps, 2, 256, 1.0 / 256, EPS)
                    for h in range(2):
                        fw.op("dve", lambda e, h=h, ps=ps, rs=rs, mt=mt, nb=nb: e.tensor_scalar(
                            out=mktm[:, mt, nb * 512 + h * 256: nb * 512 + (h + 1) * 256], in0=ps[:, h * 256:(h + 1) * 256],
                            scalar1=rs[:, h:h + 1], scalar2=None, op0=ALU.mult), reads=[b_ps, b_rs], writes=[b_mktm])
                else:
                    fw.op("act", lambda e, ps=ps, mt=mt, nb=nb: e.activation(out=mv[:, mt, (nb - 2) * 512:(nb - 1) * 512], in_=ps[:], func=AF.Copy),
                          reads=[b_ps], writes=[b_mv])
        for mt in range(2):
            cm.transposes(mktm[:, mt, :], b_mktm, 8,
                          lambda c, mt=mt: mkT[:, c // 2, c % 2, mt * 128:(mt + 1) * 128], b_mkT,
                          lambda c: mkn_s[:, (c % 2):(c % 2) + 1], b_mkn)

        outs = []
        for blk in range(NB):
            t0 = blk * TB
            for i in range(NTB):
                xt, b_xt, s_xt = cm.xr.next()
                fw.dma("sp", xt[:], x[t0 + i * 128: t0 + (i + 1) * 128, :], s_xt, writes=[b_xt])
                cm.norm_T(xt[:], b_xt, an_s, b_an, hT, b_hT, i * 128)
            fw.dma("sp", sbTs[:], sbT[:, t0:t0 + TB].rearrange("(c p) t -> p c t", p=128), s_sbTs, writes=[b_sbTs])
            for nb in range(4):
                wb, b_wb = cm.load_w(wl, 0, 16, nb * 512, 512)
                for i in range(NTB):
                    ps, b_ps = cm.ps.next()
                    mm_tm(lambda kc, i=i: hT[:, kc, i * 128:(i + 1) * 128], [b_hT], 16, wb, b_wb, 512, ps, b_ps)
                    if nb < 2:
                        fw.op("act", lambda e, ps=ps, i=i, nb=nb: e.activation(out=sgr[:, i, nb * 512:(nb + 1) * 512], in_=ps[:], func=AF.Silu),
                              reads=[b_ps], writes=[b_sgr[i]])
                    else:
                        rs, b_rs = head_rstd(ps[:], b_ps, 2, 256, 1.0, 256 * EPS)
                        for h in range(2):
                            fw.op("dve", lambda e, h=h, ps=ps, rs=rs, i=i, nb=nb: e.tensor_scalar(
                                out=qm[:, i, (nb - 2) * 512 + h * 256:(nb - 2) * 512 + (h + 1) * 256], in0=ps[:, h * 256:(h + 1) * 256],
                                scalar1=rs[:, h:h + 1], scalar2=None, op0=ALU.mult), reads=[b_ps, b_rs], writes=[b_qm[i]])
            for nb in range(12):
                wb, b_wb = cm.load_w(wl, 0, 16, 2048 + nb * 512, 512)
                for s in range(4):
                    ch = nb * 4 + s
                    ps, b_ps = cm.ps.next()
                    mm_fm(wb, b_wb, s * 128, 16, lambda kc: hT[:, kc, :], [b_hT], ps, b_ps)
                    fw.op("act", lambda e, ps=ps, ch=ch: e.activation(out=big[:, ch, :], in_=ps[:, 0:TB], func=AF.Sigmoid),
                          reads=[b_ps], writes=[b_big[ch]])
            for i in range(NTB):
                gt, b_gt, s_gt = gor.next()
                fw.dma("sp", gt[:], go[t0 + i * 128:t0 + (i + 1) * 128, :], s_gt, writes=[b_gt])
                rs, b_rs = head_rstd(gt[:], b_gt, 4, 256, 1.0 / 256, EPS)
                for h in range(4):
                    hs = slice(h * 256, (h + 1) * 256)
                    fw.op("dve", lambda e, h=h, hs=hs, gt=gt, rs=rs: e.scalar_tensor_tensor(
                        out=og32[:, hs], in0=gt[:, hs], scalar=rs[:, h:h + 1], in1=gon_s[:, hs], op0=ALU.mult, op1=ALU.mult),
                        reads=[b_gt, b_rs, b_gon], writes=[b_og32])
                ob_, b_ob = ogb.next()
                fw.op("dve", lambda e, ob_=ob_, i=i: e.tensor_tensor(out=ob_[:], in0=og32[:], in1=sgr[:, i, :], op=ALU.mult),
                      reads=[b_og32, b_sgr[i]], writes=[b_ob])
                cm.transposes(ob_, b_ob, 8, lambda c, i=i: ogT[:, c, i * 128:(i + 1) * 128], b_ogT)
            for i in range(NTB):
                cm.transposes(qm[:, i, :], b_qm[i], 8, lambda c: qmT[:, c, :], b_qmT,
                              lambda c: mqn_s[:, (c % 2):(c % 2) + 1], b_mqn)
                for h in range(4):
                    ps, b_ps = cm.ps.next()
                    for c2 in range(2):
                        fw.op("pe", lambda e, ps=ps, h=h, c2=c2: e.matmul(ps[:, 0:NMEM], lhsT=qmT[:, h * 2 + c2, :], rhs=mkT[:, h, c2, :],
                                                                         start=(c2 == 0), stop=(c2 == 1)),
                              reads=[b_qmT, b_mkT], writes=[b_ps])
                    mx, b_mx = cm.st.next()
                    fw.op("dve", lambda e, ps=ps, mx=mx: e.tensor_reduce(out=mx[:, 0:1], in_=ps[:, 0:NMEM], axis=AX.X, op=ALU.max),
                          reads=[b_ps], writes=[b_mx])
                    fw.op("dve", lambda e, mx=mx: e.tensor_scalar(out=mx[:, 1:2], in0=mx[:, 0:1], scalar1=-1.0, scalar2=None, op0=ALU.mult),
                          reads=[b_mx], writes=[b_mx])
                    pe_, b_pe = t32.next()
                    fw.op("act", lambda e, ps=ps, mx=mx, pe_=pe_: e.activation(out=pe_[:, 0:NMEM], in_=ps[:, 0:NMEM], func=AF.Exp,
                                                                             bias=mx[:, 1:2], accum_out=mx[:, 2:3]),
                          reads=[b_ps, b_mx], writes=[b_pe, b_mx])
                    fw.op("dve", lambda e, mx=mx: e.reciprocal(out=mx[:, 3:4], in_=mx[:, 2:3]), reads=[b_mx], writes=[b_mx])
                    pb_, b_pb = pbf.next()
                    fw.op("dve", lambda e, pb_=pb_, pe_=pe_, mx=mx: e.tensor_scalar(out=pb_[:], in0=pe_[:, 0:NMEM], scalar1=mx[:, 3:4],
                                                                                  scalar2=None, op0=ALU.mult),
                          reads=[b_pe, b_mx], writes=[b_pb])
                    cm.transposes(pb_, b_pb, 2, lambda c: pTs[:, c, :], b_pTs)
                    for c2 in range(2):
                        ps2, b_ps2 = cm.ps.next()
                        for mc in range(2):
                            fw.op("pe", lambda e, ps2=ps2, h=h, c2=c2, mc=mc: e.matmul(
                                ps2[:, 0:128], lhsT=mv[:, mc, h * 256 + c2 * 128: h * 256 + (c2 + 1) * 128], rhs=pTs[:, mc, :],
                                start=(mc == 0), stop=(mc == 1)), reads=[b_mv, b_pTs], writes=[b_ps2])
                        fw.op("act", lambda e, ps2=ps2, h=h, c2=c2, i=i: e.activation(out=omT[:, h * 2 + c2, i * 128:(i + 1) * 128],
                                                                                    in_=ps2[:, 0:128], func=AF.Copy),
                              reads=[b_ps2], writes=[b_omT])
            srcs = [(ogT, b_ogT), (sbTs, b_sbTs), (omT, b_omT)]
            for br in range(3):
                src, b_src = srcs[br]
                for nb in range(4):
                    wb, b_wb = cm.load_w(wbr[br], 0, 8, nb * 512, 512)
                    for s in range(4):
                        ncn = nb * 4 + s
                        gch = br * 16 + ncn
                        ps, b_ps = cm.ps.next()
                        mm_fm(wb, b_wb, s * 128, 8, lambda kc, src=src: src[:, kc, :], [b_src], ps, b_ps)
                        if br == 0:
                            fw.op("dve", lambda e, ps=ps, ncn=ncn, gch=gch: e.tensor_tensor(out=mg32[:, ncn, :], in0=ps[:, 0:TB], in1=big[:, gch, :], op=ALU.mult),
                                  reads=[b_ps, b_big[gch]], writes=[b_mg[ncn]])
                        else:
                            tt, b_tt = t32.next()
                            fw.op("dve", lambda e, ps=ps, tt=tt, gch=gch: e.tensor_tensor(out=tt[:, 0:TB], in0=ps[:, 0:TB], in1=big[:, gch, :], op=ALU.mult),
                                  reads=[b_ps, b_big[gch]], writes=[b_tt])
                            if br == 1:
                                fw.op("pool", lambda e, tt=tt, ncn=ncn: e.tensor_tensor(out=mg32[:, ncn, :], in0=mg32[:, ncn, :], in1=tt[:, 0:TB], op=ALU.add),
                                      reads=[b_tt, b_mg[ncn]], writes=[b_mg[ncn]])
                            else:
                                fw.op("pool", lambda e, tt=tt, ncn=ncn: e.tensor_tensor(out=mT[:, ncn, :], in0=mg32[:, ncn, :], in1=tt[:, 0:TB], op=ALU.add),
                                      reads=[b_tt, b_mg[ncn]], writes=[b_mT[ncn]])
            for nb in range(4):
                wb, b_wb = cm.load_w(wo, 0, 16, nb * 512, 512)
                for i in range(NTB):
                    ps, b_ps = cm.ps.next()
                    mm_tm(lambda kc, i=i: mT[:, kc, i * 128:(i + 1) * 128], b_mT, 16, wb, b_wb, 512, ps, b_ps)
                    xt, b_xt, s_xt = ys.next()
                    fw.dma("sp", xt[:], x[t0 + i * 128:t0 + (i + 1) * 128, nb * 512:(nb + 1) * 512], s_xt, writes=[b_xt])
                    fw.op("dve", lambda e, ps=ps, xt=xt, i=i, nb=nb: e.tensor_tensor(out=x1s[:, i, nb * 512:(nb + 1) * 512], in0=ps[:], in1=xt[:], op=ALU.add),
                          reads=[b_ps, b_xt], writes=[b_x1s[i]])
            for i in range(NTB):
                cm.norm_T(x1s[:, i, :], b_x1s[i], fn_s, b_fn, hT, b_hT, i * 128)
            for nb in range(11):
                wg_, b_wg = cm.load_w(wgu, 0, 16, nb * 512, 512)
                sgs = []
                for s in range(4):
                    ps, b_ps = cm.ps.next()
                    mm_fm(wg_, b_wg, s * 128, 16, lambda kc: hT[:, kc, :], [b_hT], ps, b_ps)
                    sg, b_sg = t32.next()
                    fw.op("act", lambda e, ps=ps, sg=sg: e.activation(out=sg[:, 0:TB], in_=ps[:, 0:TB], func=AF.Silu), reads=[b_ps], writes=[b_sg])
                    sgs.append((sg, b_sg))
                wu_, b_wu = cm.load_w(wgu, 0, 16, DFF + nb * 512, 512)
                for s in range(4):
                    ch = nb * 4 + s
                    ps, b_ps = cm.ps.next()
                    mm_fm(wu_, b_wu, s * 128, 16, lambda kc: hT[:, kc, :], [b_hT], ps, b_ps)
                    sg, b_sg = sgs[s]
                    fw.op("dve", lambda e, ps=ps, sg=sg, ch=ch: e.tensor_tensor(out=big[:, ch, :], in0=ps[:, 0:TB], in1=sg[:, 0:TB], op=ALU.mult),
                          reads=[b_ps, b_sg], writes=[b_big[ch]])
            kgs = [(0, 16), (16, 16), (32, 12)]
            for nb in range(4):
                pss = [cm.ps.next() for _ in range(NTB)]
                for gi, (k0, nk) in enumerate(kgs):
                    wb, b_wb = cm.load_w(wd, k0, nk, nb * 512, 512)
                    for i in range(NTB):
                        ps, b_ps = pss[i]
                        mm_tm(lambda kc, i=i, k0=k0: big[:, k0 + kc, i * 128:(i + 1) * 128], b_big[k0:k0 + nk], nk, wb, b_wb, 512, ps, b_ps,
                              first=(gi == 0), last=(gi == 2))
                for i in range(NTB):
                    ps, b_ps = pss[i]
                    yt, b_yt, s_yt = ys.next()
                    fw.op("dve", lambda e, ps=ps, yt=yt, i=i, nb=nb: e.tensor_tensor(out=yt[:], in0=ps[:], in1=x1s[:, i, nb * 512:(nb + 1) * 512], op=ALU.add),
                          reads=[b_ps, b_x1s[i]], writes=[b_yt])
                    outs.append(fw.dma("sp", y[t0 + i * 128:t0 + (i + 1) * 128, nb * 512:(nb + 1) * 512], yt[:], s_yt, reads=[b_yt]))
        fw.wait("sp", outs[-6:])
        fw.emit()
    return nc


_CACHE = {}


def _get(name, fn, *a):
    k = (name,) + a
    if k not in _CACHE:
        _CACHE[k] = fn(*a)
    return _CACHE[k]


def _fm(g):
    return np.ascontiguousarray(np.asarray(g, np.float32).reshape(-1, 128).T)


def _bc(g, rep):
    return np.ascontiguousarray(np.broadcast_to(np.tile(np.asarray(g, np.float32), rep)[None, :], (128, g.shape[0] * rep)))


def run_l1(x2, wh, attn_norm, sbq, sbk):
    T = x2.shape[0]
    TOK = T // K1
    nc = _get("l1", build_l1, TOK)
    idn = np.eye(128, dtype=np.float32).astype(NPBF)
    ims = [dict(x=x2[c * TOK:(c + 1) * TOK], wh=wh, gn=_fm(attn_norm), qg=_bc(sbq, 8), kg=_bc(sbk, 8), idn=idn) for c in range(K1)]
    res = run_bass_kernel_spmd(nc, ims, core_ids=list(range(K1))).results
    oh = np.concatenate([r["oh"] for r in res], 0)
    oa = np.concatenate([r["oa"] for r in res], 0)
    return oh, oa


def run_l2(oh, oa, w_a2, b_a):
    T = oh.shape[0]
    nc = _get("l2", build_l2, T)
    cs = l2_consts()
    sq, sk, sv = oh[:, 0:1024], oh[:, 1024:2048], oh[:, 2048:3072]
    gq, gk, gv = oh[:, 3072:3584], oh[:, 3584:4096], oh[:, 4096:5120]
    ga = np.ascontiguousarray(np.concatenate([oa.T, np.ones((1, T), np.float32)], 0))
    ims = []
    for c in range(8):
        hg, half = c // 2, c % 2
        hs = slice(c * 128, (c + 1) * 128)
        gs = slice(hg * 128, (hg + 1) * 128)
        vs = slice(hg * 256 + half * 128, hg * 256 + (half + 1) * 128)
        wa = np.ascontiguousarray(np.concatenate([w_a2[:, gs], b_a[None, gs]], 0).astype(np.float32))
        ims.append(dict(sqT=np.ascontiguousarray(sq[:, hs].T), skT=np.ascontiguousarray(sk[:, hs].T), sv=np.ascontiguousarray(sv[:, hs]),
                        gqT=np.ascontiguousarray(gq[:, gs].T), gkT=np.ascontiguousarray(gk[:, gs].T),
                        gk=np.ascontiguousarray(gk[:, gs]), gv=np.ascontiguousarray(gv[:, vs]), ga=ga, wa=wa, **cs))
    res = run_bass_kernel_spmd(nc, ims, core_ids=list(range(8))).results
    sbT = np.concatenate([r["sbT"] for r in res], 0)
    go = np.concatenate([r["go"] for r in res], 1)
    return sbT, go


def run_l3(x2, sbT, go, mem2, p):
    T = x2.shape[0]
    TOK = T // K3
    nc = _get("l3", build_l3, TOK)
    idn = np.eye(128, dtype=np.float32).astype(NPBF)
    ims = []
    for c in range(K3):
        ts = slice(c * TOK, (c + 1) * TOK)
        ims.append(dict(x=x2[ts], sbT=np.ascontiguousarray(sbT[:, ts]), go=np.ascontiguousarray(go[ts]), mem=mem2,
                        wl=p["wl"], wbg=p["wbg"], wbs=p["wbs"], wbm=p["wbm"], wkv=p["wkv"], wo=p["wo"], wgu=p["wgu"], wd=p["wd"],
                        an=_fm(p["attn_norm"]), fn=_fm(p["ffn_norm"]), mn=_fm(p["mem_norm"]), gon=_bc(p["gla_out_norm"], 4),
                        mqn=np.ascontiguousarray(p["mem_q_norm"].reshape(2, 128).T), mkn=np.ascontiguousarray(p["mem_k_norm"].reshape(2, 128).T),
                        idn=idn))
    res = run_bass_kernel_spmd(nc, ims, core_ids=list(range(K3))).results
    return np.concatenate([r["y"] for r in res], 0)


def split_w_in(w):
    gq, gk, gv, gr = w[:, 0:512], w[:, 512:1024], w[:, 1024:2048], w[:, 2048:3072]
    ga1 = w[:, 3072:3088]
    sq, sk, sv = w[:, 3088:4112], w[:, 4112:5136], w[:, 5136:6160]
    mq, gates = w[:, 6160:7184], w[:, 7184:13328]
    wh = np.ascontiguousarray(np.concatenate([sq, sk, sv, gq, gk, gv, ga1], 1))
    wl = np.ascontiguousarray(np.concatenate([gr, mq, gates], 1))
    return wh, wl


def kernel(x, mem, attn_norm, w_in, gla_w_a2, gla_b_a, gla_out_norm, w_br_gla,
           sb_q_norm, sb_k_norm, w_br_sb, mem_norm, w_mem_kv, mem_q_norm, mem_k_norm,
           w_br_mem, w_o, ffn_norm, w_gate_up, w_down):
    f = lambda a: np.asarray(a, np.float32)
    x2 = np.ascontiguousarray(f(x)[0])
    mem2 = np.ascontiguousarray(f(mem)[0])
    for l in range(w_in.shape[0]):
        wh, wl = split_w_in(f(w_in[l]))
        oh, oa = run_l1(x2, wh, f(attn_norm[l]), f(sb_q_norm[l]), f(sb_k_norm[l]))
        sbT, go = run_l2(oh, oa, f(gla_w_a2[l]), f(gla_b_a[l]))
        p = dict(wl=wl, wbg=f(w_br_gla[l]), wbs=f(w_br_sb[l]), wbm=f(w_br_mem[l]), wkv=f(w_mem_kv[l]), wo=f(w_o[l]),
                 wgu=f(w_gate_up[l]), wd=f(w_down[l]), attn_norm=f(attn_norm[l]), ffn_norm=f(ffn_norm[l]),
                 mem_norm=f(mem_norm[l]), gla_out_norm=f(gla_out_norm[l]), mem_q_norm=f(mem_q_norm[l]),
                 mem_k_norm=f(mem_k_norm[l]))
        x2 = run_l3(x2, sbT, go, mem2, p)
    return x2[None].astype(np.float32)
```

```python
import contextlib
import numpy as np
import ml_dtypes
import concourse.bass as bass
import concourse.mybir as mybir
from concourse.bass_utils import run_bass_kernel_spmd

F32 = mybir.dt.float32
BF16 = mybir.dt.bfloat16
AF = mybir.ActivationFunctionType
ALU = mybir.AluOpType
AX = mybir.AxisListType
NPBF = ml_dtypes.bfloat16

D = 2048
DFF = 5632
NMEM = 256
EPS = 1e-6
NH1 = 5136
NL = 8192
K1 = 8
K3 = 8
L1P = 2048
TB = 256
NWB = 3
PRECAST = True


class Sem:
    def __init__(self, h):
        self.h = h
        self.count = 0


class Buf:
    __slots__ = ("w", "r")

    def __init__(self):
        self.w = None
        self.r = []


class FW:
    ENGS = ("pe", "act", "dve", "pool", "sp")
    M = 8

    def __init__(self, nc, stack):
        self.nc = nc
        self.stack = stack
        self.q = {e: [] for e in self.ENGS}
        self.esem = {e: [self.new_sem(f"e_{e}{j}") for j in range(self.M)] for e in self.ENGS}
        self.eidx = {e: 0 for e in self.ENGS}
        self.seen_e = {e: {s: 0 for s in self.ENGS} for e in self.ENGS}
        self.seen_d = {e: {} for e in self.ENGS}
        self.n = 0

    def new_sem(self, name):
        return Sem(self.stack.enter_context(self.nc.semaphore(name)))

    def sbuf(self, name, shape, dt):
        return self.stack.enter_context(self.nc.sbuf_tensor(name, list(shape), dt))

    def psum(self, name, shape, dt):
        return self.stack.enter_context(self.nc.psum_tensor(name, list(shape), dt))

    def _waits(self, eng, deps):
        out = {}
        for t in deps:
            if t is None:
                continue
            kind, src, val = t
            if kind == "e":
                if src == "pe" and eng == "pe":
                    continue
                if self.seen_e[eng][src] >= val:
                    continue
                key = ("e", src)
                if out.get(key, 0) < val:
                    out[key] = val
            else:
                if self.seen_d[eng].get(src, 0) >= val:
                    continue
                key = ("d", src)
                if out.get(key, 0) < val:
                    out[key] = val
        res = []
        for (kind, src), val in out.items():
            if kind == "e":
                self.seen_e[eng][src] = val
                j = (val - 1) % self.M
                res.append((self.esem[src][j].h, (val - 1) // self.M + 1))
            else:
                self.seen_d[eng][src] = val
                res.append((src.h, val))
        return res

    @staticmethod
    def _deps(reads, writes, extra):
        deps = list(extra)
        for b in reads:
            deps.append(b.w)
        for b in writes:
            deps.append(b.w)
            deps.extend(b.r)
        return deps

    @staticmethod
    def _commit(tok, reads, writes):
        for b in reads:
            b.r.append(tok)
            if len(b.r) > 48:
                del b.r[:-48]
        for b in writes:
            b.w = tok
            b.r = []

    def op(self, eng, fn, reads=(), writes=(), deps=()):
        waits = self._waits(eng, self._deps(reads, writes, deps))
        self.eidx[eng] += 1
        idx = self.eidx[eng]
        tok = ("e", eng, idx)
        s = self.esem[eng][(idx - 1) % self.M]
        self.q[eng].append((waits, fn, (s.h, 1)))
        self.n += 1 + len(waits)
        self._commit(tok, reads, writes)
        return tok

    def dma(self, eng, out, in_, sem, reads=(), writes=(), deps=()):
        waits = self._waits(eng, self._deps(reads, writes, deps))
        sem.count += 16
        tok = ("d", sem, sem.count)
        self.q[eng].append((waits, lambda e: e.dma_start(out=out, in_=in_), (sem.h, 16)))
        self.n += 1 + len(waits)
        self._commit(tok, reads, writes)
        return tok

    def wait(self, eng, deps):
        waits = self._waits(eng, deps)
        if waits:
            self.q[eng].append((waits, None, None))

    def emit(self):
        q = self.q

        def run(e, lst):
            for waits, fn, inc in lst:
                for (h, v) in waits:
                    e.wait_ge(h, v)
                if fn is not None:
                    fn(e).then_inc(inc[0], inc[1])

        with self.nc.Block() as block:
            @block.tensor
            def _(e):
                run(e, q["pe"])

            @block.scalar
            def _(e):
                run(e, q["act"])

            @block.vector
            def _(e):
                run(e, q["dve"])

            @block.gpsimd
            def _(e):
                run(e, q["pool"])

            @block.sync
            def _(e):
                run(e, q["sp"])


class Rot:
    def __init__(self, fw, name, shape, dt, n, psum=False, sem=False):
        self.t = [(fw.psum if psum else fw.sbuf)(f"{name}{i}", shape, dt) for i in range(n)]
        self.b = [Buf() for _ in range(n)]
        self.s = [fw.new_sem(f"s_{name}{i}") for i in range(n)] if sem else None
        self.i = -1
        self.n = n

    def next(self):
        self.i = (self.i + 1) % self.n
        if self.s:
            return self.t[self.i], self.b[self.i], self.s[self.i]
        return self.t[self.i], self.b[self.i]


class Common:
    def __init__(self, fw, nc):
        self.fw = fw
        self.nc = nc
        self.ident = fw.sbuf("ident", [128, 128], BF16)
        self.b_ident = Buf()
        self.s_const = fw.new_sem("s_const")
        self.xr = Rot(fw, "xr", [128, D], F32, 2, sem=True)
        self.hb = Rot(fw, "hb", [128, D], BF16, 2)
        self.junk = fw.sbuf("junk", [128, D], BF16)
        self.b_junk = Buf()
        self.st = Rot(fw, "st", [128, 8], F32, 4)
        self.ps = Rot(fw, "ps", [128, 512], F32, 6, psum=True)
        self.pt = Rot(fw, "pt", [128, 1024], BF16, 2, psum=True)
        self.wb = Rot(fw, "wb", [128, 16, 512], BF16, NWB, sem=True)

    def load_const(self, dst, src, b):
        self.fw.dma("sp", dst, src, self.s_const, writes=[b])

    def rstd(self, ssq_ap, b_ssq, n, scale, bias):
        fw = self.fw
        st, b_st = self.st.next()
        fw.op("act", lambda e: e.activation(out=st[:, 0:n], in_=ssq_ap, func=AF.Sqrt, scale=scale, bias=bias),
              reads=[b_ssq], writes=[b_st])
        fw.op("dve", lambda e: e.reciprocal(out=st[:, 0:n], in_=st[:, 0:n]), reads=[b_st], writes=[b_st])
        return st, b_st

    def norm_T(self, src, b_src, gain_fm, b_gain, dstT, b_dst, col0):
        fw = self.fw
        ss, b_ss = self.st.next()
        fw.op("act", lambda e: e.activation(out=self.junk[:], in_=src, func=AF.Square, accum_out=ss[:, 0:1]),
              reads=[b_src], writes=[self.b_junk, b_ss])
        rs, b_rs = self.rstd(ss[:, 0:1], b_ss, 1, 1.0 / D, EPS)
        hb, b_hb = self.hb.next()
        fw.op("dve", lambda e: e.tensor_scalar(out=hb[:], in0=src, scalar1=rs[:, 0:1], scalar2=None, op0=ALU.mult),
              reads=[b_src, b_rs], writes=[b_hb])
        self.transposes(hb, b_hb, D // 128, lambda c: dstT[:, c, col0:col0 + 128], b_dst,
                        lambda c: gain_fm[:, c:c + 1], b_gain)

    def transposes(self, src, b_src, nch, dst_fn, b_dst, scale_fn=None, b_scale=None):
        fw = self.fw
        for c0 in range(0, nch, 8):
            pt, b_pt = self.pt.next()
            m = min(8, nch - c0)
            for j in range(m):
                c = c0 + j
                fw.op("pe", lambda e, c=c, j=j, pt=pt: e.transpose(out=pt[:, j * 128:(j + 1) * 128],
                                                                 in_=src[:, c * 128:(c + 1) * 128],
                                                                 identity=self.ident[:]),
                      reads=[b_src, self.b_ident], writes=[b_pt])
            for j in range(m):
                c = c0 + j
                eng = "dve" if j % 2 == 0 else "act"
                rd = [b_pt] + ([b_scale] if b_scale is not None else [])
                if scale_fn is None:
                    if eng == "dve":
                        fw.op("dve", lambda e, c=c, j=j, pt=pt: e.tensor_copy(out=dst_fn(c), in_=pt[:, j * 128:(j + 1) * 128]),
                              reads=rd, writes=[b_dst])
                    else:
                        fw.op("act", lambda e, c=c, j=j, pt=pt: e.activation(out=dst_fn(c), in_=pt[:, j * 128:(j + 1) * 128], func=AF.Copy),
                              reads=rd, writes=[b_dst])
                else:
                    if eng == "dve":
                        fw.op("dve", lambda e, c=c, j=j, pt=pt: e.tensor_scalar(out=dst_fn(c), in0=pt[:, j * 128:(j + 1) * 128],
                                                                               scalar1=scale_fn(c), scalar2=None, op0=ALU.mult),
                              reads=rd, writes=[b_dst])
                    else:
                        fw.op("act", lambda e, c=c, j=j, pt=pt: e.activation(out=dst_fn(c), in_=pt[:, j * 128:(j + 1) * 128],
                                                                            func=AF.Copy, scale=scale_fn(c)),
                              reads=rd, writes=[b_dst])

    def load_w(self, w_ap, k0, nk, n0, ncols):
        wb, b_wb, s_wb = self.wb.next()
        if isinstance(w_ap, tuple):
            t, b_w = w_ap
            assert n0 % 512 == 0 and ncols == 512
            src = t[n0 // 512][:, k0:k0 + nk, :]
            self.fw.dma("pool", wb[:, 0:nk, 0:ncols], src, s_wb, reads=[b_w], writes=[b_wb])
        else:
            src = w_ap[k0 * 128:(k0 + nk) * 128, n0:n0 + ncols].rearrange("(c p) n -> p c n", p=128)
            self.fw.dma("pool", wb[:, 0:nk, 0:ncols], src, s_wb, writes=[b_wb])
        return wb, b_wb

    def precast(self, name, w_ap, K, N):
        KC, NBK = K // 128, N // 512
        t = self.nc.dram_tensor(name + "_bf", [NBK, 128, KC, 512], BF16)
        b = Buf()
        sem = self.fw.new_sem("s_pc_" + name)
        tok = None
        for nb in range(NBK):
            src = w_ap[:, nb * 512:(nb + 1) * 512].rearrange("(c p) n -> p c n", p=128)
            tok = self.fw.dma("pool", t[nb], src, sem)
        b.w = tok
        return (t, b)


def build_l1(TOK):
    PT = min(TOK, L1P)
    NTP = PT // 128
    nc = bass.Bass("TRN2", target_bir_lowering=False)
    x = nc.dram_tensor("x", [TOK, D], F32, kind="ExternalInput").ap()
    wh = nc.dram_tensor("wh", [D, NH1], F32, kind="ExternalInput").ap()
    gn = nc.dram_tensor("gn", [128, 16], F32, kind="ExternalInput").ap()
    qg = nc.dram_tensor("qg", [128, 1024], F32, kind="ExternalInput").ap()
    kg = nc.dram_tensor("kg", [128, 1024], F32, kind="ExternalInput").ap()
    idn = nc.dram_tensor("idn", [128, 128], BF16, kind="ExternalInput").ap()
    oh = nc.dram_tensor("oh", [TOK, 5120], BF16, kind="ExternalOutput").ap()
    oa = nc.dram_tensor("oa", [TOK, 16], F32, kind="ExternalOutput").ap()
    with contextlib.ExitStack() as st:
        fw = FW(nc, st)
        cm = Common(fw, nc)
        gn_s = fw.sbuf("gn_s", [128, 16], F32); b_gn = Buf()
        qg_s = fw.sbuf("qg_s", [128, 1024], F32); b_qg = Buf()
        kg_s = fw.sbuf("kg_s", [128, 1024], F32); b_kg = Buf()
        hT = fw.sbuf("hT", [128, 16, PT], BF16); b_hT = Buf()
        sqs = fw.sbuf("sqs", [128, 512], F32); b_sqs = Buf()
        ob = Rot(fw, "ob", [128, 512], BF16, 3, sem=True)
        oab = Rot(fw, "oab", [128, 16], F32, 2, sem=True)
        cm.load_const(cm.ident[:], idn, cm.b_ident)
        cm.load_const(gn_s[:], gn, b_gn)
        cm.load_const(qg_s[:], qg, b_qg)
        cm.load_const(kg_s[:], kg, b_kg)
        outs = []
        nblocks = [(i * 512, 512) for i in range(10)] + [(5120, 16)]
        def l1_body(tb):
            for bi, (n0, ncols) in enumerate(nblocks):
                wb, b_wb = cm.load_w(wh, 0, 16, n0, ncols)
                for t in range(NTP):
                    ps, b_ps = cm.ps.next()
                    for kc in range(16):
                        fw.op("pe", lambda e, kc=kc, t=t, ps=ps, wb=wb, ncols=ncols: e.matmul(
                            ps[:, 0:ncols], lhsT=hT[:, kc, t * 128:(t + 1) * 128], rhs=wb[:, kc, 0:ncols],
                            start=(kc == 0), stop=(kc == 15)), reads=[b_hT, b_wb], writes=[b_ps])
                    if bi < 4:
                        isq = bi < 2
                        g_s, b_g = (qg_s, b_qg) if isq else (kg_s, b_kg)
                        gofs = (bi % 2) * 512
                        fw.op("act", lambda e, ps=ps: e.activation(out=sqs[:], in_=ps[:], func=AF.Square),
                              reads=[b_ps], writes=[b_sqs])
                        ss, b_ss = cm.st.next()
                        fw.op("dve", lambda e, ss=ss: e.tensor_reduce(out=ss[:, 0:4], in_=sqs[:].rearrange("p (h d) -> p h d", h=4),
                                                                     axis=AX.X, op=ALU.add), reads=[b_sqs], writes=[b_ss])
                        if isq:
                            rs, b_rs = cm.rstd(ss[:, 0:4], b_ss, 4, 1.0, 128 * EPS)
                        else:
                            rs, b_rs = cm.rstd(ss[:, 0:4], b_ss, 4, 1.0 / 128, EPS)
                        o, b_o, s_o = ob.next()
                        for h in range(4):
                            fw.op("dve", lambda e, h=h, ps=ps, rs=rs, o=o, g_s=g_s, gofs=gofs: e.scalar_tensor_tensor(
                                out=o[:, h * 128:(h + 1) * 128], in0=ps[:, h * 128:(h + 1) * 128], scalar=rs[:, h:h + 1],
                                in1=g_s[:, gofs + h * 128:gofs + (h + 1) * 128], op0=ALU.mult, op1=ALU.mult),
                                reads=[b_ps, b_rs, b_g], writes=[b_o])
                        outs.append(fw.dma("sp", oh[(tb + t) * 128:(tb + t + 1) * 128, n0:n0 + 512], o[:], s_o, reads=[b_o]))
                    elif bi < 10:
                        o, b_o, s_o = ob.next()
                        if t % 2 == 0:
                            fw.op("act", lambda e, ps=ps, o=o: e.activation(out=o[:], in_=ps[:], func=AF.Copy),
                                  reads=[b_ps], writes=[b_o])
                        else:
                            fw.op("dve", lambda e, ps=ps, o=o: e.tensor_copy(out=o[:], in_=ps[:]), reads=[b_ps], writes=[b_o])
                        outs.append(fw.dma("sp", oh[(tb + t) * 128:(tb + t + 1) * 128, n0:n0 + 512], o[:], s_o, reads=[b_o]))
                    else:
                        o, b_o, s_o = oab.next()
                        fw.op("dve", lambda e, ps=ps, o=o: e.tensor_copy(out=o[:], in_=ps[:, 0:16]), reads=[b_ps], writes=[b_o])
                        outs.append(fw.dma("sp", oa[(tb + t) * 128:(tb + t + 1) * 128, :], o[:], s_o, reads=[b_o]))
        for p0 in range(0, TOK, PT):
          tb = p0 // 128
          for t in range(NTP):
            xt, b_xt, s_xt = cm.xr.next()
            fw.dma("sp", xt[:], x[(tb + t) * 128:(tb + t + 1) * 128, :], s_xt, writes=[b_xt])
            cm.norm_T(xt[:], b_xt, gn_s, b_gn, hT, b_hT, t * 128)
          l1_body(tb)
        fw.wait("sp", outs[-8:])
        fw.emit()
    return nc


def l2_consts():
    i = np.arange(128)
    c = {}
    c["idn"] = np.eye(128, dtype=np.float32).astype(NPBF)
    c["mU"] = (i[:, None] <= i[None, :]).astype(np.float32).astype(NPBF)
    c["Uc"] = ((i[:, None] <= i[None, :]) * (-1.0 / 16)).astype(np.float32)
    c["U2"] = ((i[:, None] > i[None, :]) * (-1.0 / 16)).astype(np.float32)
    c["Lpn"] = (-(i[:, None] >= i[None, :]).astype(np.float32)).astype(NPBF)
    c["On"] = (-np.ones((128, 128), np.float32)).astype(NPBF)
    m4 = np.zeros((4, 128, 512), np.float32)
    tri = (i[:, None] < i[None, :]).astype(np.float32)
    for r in range(4):
        for qb in range(4):
            if qb == r:
                m4[r][:, qb * 128:(qb + 1) * 128] = tri
            elif qb > r:
                m4[r][:, qb * 128:(qb + 1) * 128] = 1.0
    c["M4"] = np.ascontiguousarray(m4.transpose(1, 0, 2)).astype(NPBF)
    return c


def build_l2(T):
    NCH = T // 128
    NG = T // 512
    GC = min(8, NCH)
    GLA_EVERY = max(1, (NG * (2 * NG + 2)) // NCH)
    nc = bass.Bass("TRN2", target_bir_lowering=False)
    di = lambda n, s, dt: nc.dram_tensor(n, s, dt, kind="ExternalInput").ap()
    sqT = di("sqT", [128, T], BF16); skT = di("skT", [128, T], BF16); sv = di("sv", [T, 128], BF16)
    gqT = di("gqT", [128, T], BF16); gkT = di("gkT", [128, T], BF16)
    gk = di("gk", [T, 128], BF16); gv = di("gv", [T, 128], BF16)
    ga = di("ga", [17, T], F32); wa = di("wa", [17, 128], F32)
    idn = di("idn", [128, 128], BF16); mU = di("mU", [128, 128], BF16)
    Uc = di("Uc", [128, 128], F32); U2 = di("U2", [128, 128], F32)
    Lpn = di("Lpn", [128, 128], BF16); On = di("On", [128, 128], BF16); M4 = di("M4", [128, 4, 512], BF16)
    sbT = nc.dram_tensor("sbT", [128, T], BF16, kind="ExternalOutput").ap()
    go = nc.dram_tensor("go", [T, 128], BF16, kind="ExternalOutput").ap()
    with contextlib.ExitStack() as st:
        fw = FW(nc, st)
        s_const = fw.new_sem("s_const")

        def const(name, src, shape, dt):
            t = fw.sbuf(name, shape, dt); b = Buf()
            fw.dma("sp", t[:], src, s_const, writes=[b])
            return t, b
        mU_s, b_mU = const("mU_s", mU, [128, 128], BF16)
        Uc_s, b_Uc = const("Uc_s", Uc, [128, 128], F32)
        U2_s, b_U2 = const("U2_s", U2, [128, 128], F32)
        Lpn_s, b_Lpn = const("Lpn_s", Lpn, [128, 128], BF16)
        On_s, b_On = const("On_s", On, [128, 128], BF16)
        M4_s, b_M4 = const("M4_s", M4, [128, 4, 512], BF16)
        wa_s, b_wa = const("wa_s", wa, [17, 128], F32)
        sqT_s, b_sqT = const("sqT_s", sqT, [128, T], BF16)
        skT_s, b_skT = const("skT_s", skT, [128, T], BF16)
        sv_s, b_sv = const("sv_s", sv.rearrange("(c p) d -> p c d", p=128), [128, NCH, 128], BF16)
        ps = Rot(fw, "ps", [128, 512], F32, 6, psum=True)
        pso = Rot(fw, "pso", [128, 512], F32, 2, psum=True)

        gq_r = Rot(fw, "gq_r", [128, GC * 128], BF16, 2, sem=True)
        gkT_r = Rot(fw, "gkT_r", [128, GC * 128], BF16, 2, sem=True)
        gk_r = Rot(fw, "gk_r", [128, GC, 128], BF16, 2, sem=True)
        gv_r = Rot(fw, "gv_r", [128, GC, 128], BF16, 2, sem=True)
        ga_r = Rot(fw, "ga_r", [17, GC * 128], F32, 2, sem=True)
        f32t = Rot(fw, "f32t", [128, 128], F32, 6)
        bft = Rot(fw, "bft", [128, 128], BF16, 12)
        gob = Rot(fw, "gob", [128, 128], BF16, 3, sem=True)
        dec_r = Rot(fw, "dec_r", [128, 1], F32, 3)
        S32 = fw.sbuf("S32", [128, 128], F32); b_S32 = Buf()
        Sbf = Rot(fw, "Sbf", [128, 128], BF16, 2)
        fw.op("dve", lambda e: e.memset(S32[:], 0.0), writes=[b_S32])
        sb_cur, b_sb_cur = Sbf.next()
        fw.op("dve", lambda e, t=sb_cur: e.memset(t[:], 0.0), writes=[b_sb_cur])
        outs = []
        gst = dict(grp=None, sb_cur=sb_cur, b_sb_cur=b_sb_cur, next=0, pairs=0)

        def gla_chunk(c):
            grp = gst["grp"]; sb_cur = gst["sb_cur"]; b_sb_cur = gst["b_sb_cur"]
            if c % GC == 0:
                g0 = c * 128
                tq, bq, sq_ = gq_r.next(); fw.dma("sp", tq[:], gqT[:, g0:g0 + GC * 128], sq_, writes=[bq])
                tk, bk, sk_ = gkT_r.next(); fw.dma("sp", tk[:], gkT[:, g0:g0 + GC * 128], sk_, writes=[bk])
                tkk, bkk, skk = gk_r.next(); fw.dma("sp", tkk[:], gk[g0:g0 + GC * 128, :].rearrange("(c p) d -> p c d", p=128), skk, writes=[bkk])
                tv, bv, sv_ = gv_r.next(); fw.dma("sp", tv[:], gv[g0:g0 + GC * 128, :].rearrange("(c p) d -> p c d", p=128), sv_, writes=[bv])
                ta, ba, sa_ = ga_r.next(); fw.dma("sp", ta[:], ga[:, g0:g0 + GC * 128], sa_, writes=[ba])
                grp = (tq, bq, tk, bk, tkk, bkk, tv, bv, ta, ba)
            tq, bq, tk, bk, tkk, bkk, tv, bv, ta, ba = grp
            j = c % GC
            cs = slice(j * 128, (j + 1) * 128)
            pu, b_pu = ps.next()
            fw.op("pe", lambda e, pu=pu, ta=ta, cs=cs: e.matmul(pu[:, 0:128], lhsT=ta[:, cs], rhs=wa_s[:], start=True, stop=True),
                  reads=[ba, b_wa], writes=[b_pu])
            ex, b_ex = f32t.next()
            fw.op("act", lambda e, pu=pu, ex=ex: e.activation(out=ex[:], in_=pu[:, 0:128], func=AF.Exp, scale=-1.0),
                  reads=[b_pu], writes=[b_ex])
            spt, b_spt = f32t.next()
            fw.op("act", lambda e, ex=ex, spt=spt: e.activation(out=spt[:], in_=ex[:], func=AF.Ln, bias=1.0),
                  reads=[b_ex], writes=[b_spt])
            pb, b_pb = ps.next()
            fw.op("pe", lambda e, pb=pb, spt=spt: e.matmul(pb[:, 0:128], lhsT=spt[:], rhs=Uc_s[:], start=True, stop=True),
                  reads=[b_spt, b_Uc], writes=[b_pb])
            fw.op("pe", lambda e, pb=pb, spt=spt: e.matmul(pb[:, 128:256], lhsT=U2_s[:], rhs=spt[:], start=True, stop=True),
                  reads=[b_spt, b_U2], writes=[b_pb])
            e1, b_e1 = f32t.next()
            fw.op("act", lambda e, pb=pb, e1=e1: e.activation(out=e1[:], in_=pb[:, 0:128], func=AF.Exp),
                  reads=[b_pb], writes=[b_e1])
            e2, b_e2 = f32t.next()
            fw.op("act", lambda e, pb=pb, e2=e2: e.activation(out=e2[:], in_=pb[:, 0:128], func=AF.Exp, scale=-1.0),
                  reads=[b_pb], writes=[b_e2])
            e3, b_e3 = f32t.next()
            fw.op("act", lambda e, pb=pb, e3=e3: e.activation(out=e3[:], in_=pb[:, 128:256], func=AF.Exp),
                  reads=[b_pb], writes=[b_e3])
            dec, b_dec = dec_r.next()
            fw.op("dve", lambda e, dec=dec, e1=e1: e.tensor_copy(out=dec[:], in_=e1[:, 127:128]), reads=[b_e1], writes=[b_dec])
            qe, b_qe = bft.next()
            fw.op("dve", lambda e, qe=qe, tq=tq, cs=cs, e1=e1: e.scalar_tensor_tensor(
                out=qe[:], in0=tq[:, cs], scalar=float(128 ** -0.5), in1=e1[:], op0=ALU.mult, op1=ALU.mult),
                reads=[bq, b_e1], writes=[b_qe])
            ke, b_ke = bft.next()
            fw.op("dve", lambda e, ke=ke, tk=tk, cs=cs, e2=e2: e.tensor_tensor(out=ke[:], in0=tk[:, cs], in1=e2[:], op=ALU.mult),
                  reads=[bk, b_e2], writes=[b_ke])
            kd, b_kd = bft.next()
            fw.op("pool", lambda e, kd=kd, tkk=tkk, j=j, e3=e3: e.tensor_tensor(out=kd[:], in0=tkk[:, j, :], in1=e3[:], op=ALU.mult),
                  reads=[bkk, b_e3], writes=[b_kd])
            pS, b_pS = ps.next()
            fw.op("pe", lambda e, pS=pS, ke=ke, qe=qe: e.matmul(pS[:, 0:128], lhsT=ke[:], rhs=qe[:], start=True, stop=True),
                  reads=[b_ke, b_qe], writes=[b_pS])
            sTm, b_sTm = bft.next()
            fw.op("dve", lambda e, sTm=sTm, pS=pS: e.tensor_tensor(out=sTm[:], in0=pS[:, 0:128], in1=mU_s[:], op=ALU.mult),
                  reads=[b_pS, b_mU], writes=[b_sTm])
            pO, b_pO = ps.next()
            fw.op("pe", lambda e, pO=pO, sTm=sTm, tv=tv, j=j: e.matmul(pO[:, 0:128], lhsT=sTm[:], rhs=tv[:, j, :], start=True, stop=False),
                  reads=[b_sTm, bv], writes=[b_pO])
            fw.op("pe", lambda e, pO=pO, qe=qe, sbc=sb_cur: e.matmul(pO[:, 0:128], lhsT=qe[:], rhs=sbc[:], start=False, stop=True),
                  reads=[b_qe, b_sb_cur], writes=[b_pO])
            ot, b_ot, s_ot = gob.next()
            fw.op("act", lambda e, ot=ot, pO=pO: e.activation(out=ot[:], in_=pO[:, 0:128], func=AF.Copy), reads=[b_pO], writes=[b_ot])
            outs.append(fw.dma("sp", go[c * 128:(c + 1) * 128, :], ot[:], s_ot, reads=[b_ot]))
            pK, b_pK = ps.next()
            fw.op("pe", lambda e, pK=pK, kd=kd, tv=tv, j=j: e.matmul(pK[:, 0:128], lhsT=kd[:], rhs=tv[:, j, :], start=True, stop=True),
                  reads=[b_kd, bv], writes=[b_pK])
            fw.op("dve", lambda e, pK=pK, dec=dec: e.scalar_tensor_tensor(out=S32[:], in0=S32[:], scalar=dec[:, 0:1], in1=pK[:, 0:128],
                                                                       op0=ALU.mult, op1=ALU.add),
                  reads=[b_pK, b_dec, b_S32], writes=[b_S32])
            sb_cur, b_sb_cur = Sbf.next()
            fw.op("dve", lambda e, t=sb_cur: e.tensor_copy(out=t[:], in_=S32[:]), reads=[b_S32], writes=[b_sb_cur])

            gst["grp"] = grp; gst["sb_cur"] = sb_cur; gst["b_sb_cur"] = b_sb_cur

        def gla_step():
            if gst["next"] < NCH:
                gla_chunk(gst["next"])
                gst["next"] += 1

        e32 = Rot(fw, "e32", [128, 512], F32, 2)
        spb = Rot(fw, "spb", [128, 512], BF16, 4)
        Ab = Rot(fw, "Ab", [128, 512], BF16, 4)
        SS = fw.sbuf("SS", [128, 512], F32); b_SS = Buf()
        SSb = Rot(fw, "SSb", [128, 512], BF16, 4)
        osb = Rot(fw, "osb", [128, 512], BF16, 2, sem=True)
        for g in range(NG):
            qs = slice(g * 512, (g + 1) * 512)
            kbs = list(range(4 * g + 3, -1, -1))
            pOa, b_pOa = pso.next()
            nkb = len(kbs)
            stA = {}

            def stageA(i, kb):
                ks = slice(kb * 128, (kb + 1) * 128)
                pz, b_pz = ps.next()
                fw.op("pe", lambda e, pz=pz, ks=ks, qs=qs: e.matmul(pz[:], lhsT=skT_s[:, ks], rhs=sqT_s[:, qs], start=True, stop=True),
                      reads=[b_skT, b_sqT], writes=[b_pz])
                ee, b_ee = e32.next()
                fw.op("act", lambda e, pz=pz, ee=ee: e.activation(out=ee[:], in_=pz[:], func=AF.Exp), reads=[b_pz], writes=[b_ee])
                sp, b_sp = spb.next()
                fw.op("act", lambda e, ee=ee, sp=sp: e.activation(out=sp[:], in_=ee[:], func=AF.Ln, bias=1.0), reads=[b_ee], writes=[b_sp])
                r = kb - 4 * g
                if r >= 0:
                    fw.op("dve", lambda e, sp=sp, r=r: e.tensor_tensor(out=sp[:], in0=sp[:], in1=M4_s[:, r, :], op=ALU.mult),
                          reads=[b_sp, b_M4], writes=[b_sp])
                if i == 0:
                    sprev = None
                    fw.op("pool", lambda e, sp=sp: e.tensor_copy(out=SS[:], in_=sp[:]), reads=[b_sp], writes=[b_SS])
                else:
                    sprev = stA[i - 1]["snext"]
                    fw.op("pool", lambda e, sp=sp: e.tensor_tensor(out=SS[:], in0=SS[:], in1=sp[:], op=ALU.add),
                          reads=[b_sp, b_SS], writes=[b_SS])
                sn, b_sn = SSb.next()
                fw.op("dve", lambda e, sn=sn: e.tensor_copy(out=sn[:], in_=SS[:]), reads=[b_SS], writes=[b_sn])
                stA[i] = dict(ks=ks, sp=(sp, b_sp), sprev=sprev, snext=(sn, b_sn), r=r, kb=kb)

            def stageB(i):
                d = stA[i]
                ks = d["ks"]; sp, b_sp = d["sp"]
                pl, b_pl = ps.next()
                last = d["sprev"] is None
                fw.op("pe", lambda e, pl=pl, ks=ks, qs=qs: e.matmul(pl[:], lhsT=skT_s[:, ks], rhs=sqT_s[:, qs], start=True, stop=False),
                      reads=[b_skT, b_sqT], writes=[b_pl])
                fw.op("pe", lambda e, pl=pl, sp=sp, last=last: e.matmul(pl[:], lhsT=Lpn_s[:], rhs=sp[:], start=False, stop=last),
                      reads=[b_Lpn, b_sp], writes=[b_pl])
                if not last:
                    sv_, b_sv_ = d["sprev"]
                    fw.op("pe", lambda e, pl=pl, sv_=sv_: e.matmul(pl[:], lhsT=On_s[:], rhs=sv_[:], start=False, stop=True),
                          reads=[b_On, b_sv_], writes=[b_pl])
                A, b_A = Ab.next()
                fw.op("act", lambda e, pl=pl, A=A: e.activation(out=A[:], in_=pl[:], func=AF.Exp), reads=[b_pl], writes=[b_A])
                if d["r"] >= 0:
                    fw.op("dve", lambda e, A=A, r=d["r"]: e.tensor_tensor(out=A[:], in0=A[:], in1=M4_s[:, r, :], op=ALU.mult),
                          reads=[b_A, b_M4], writes=[b_A])
                d["A"] = (A, b_A)
                del stA[i]["sp"]

            def stageC(i):
                d = stA[i]
                A, b_A = d["A"]
                kb = d["kb"]
                fw.op("pe", lambda e, A=A, kb=kb, i=i, pOa=pOa, nkb=nkb: e.matmul(pOa[:], lhsT=sv_s[:, kb, :], rhs=A[:], start=(i == 0), stop=(i == nkb - 1)),
                      reads=[b_sv, b_A], writes=[b_pOa])
                del stA[i]["A"]

            stageA(0, kbs[0])
            stageA(1, kbs[1])
            stageB(0)
            for i in range(nkb):
                if i + 2 < nkb:
                    stageA(i + 2, kbs[i + 2])
                if gst["pairs"] % GLA_EVERY == 0:
                    gla_step()
                gst["pairs"] += 1
                if i + 1 < nkb:
                    stageB(i + 1)
                stageC(i)
            o, b_o, s_o = osb.next()
            fw.op("act", lambda e, o=o, pOa=pOa: e.activation(out=o[:], in_=pOa[:], func=AF.Copy), reads=[b_pOa], writes=[b_o])
            outs.append(fw.dma("sp", sbT[:, qs], o[:], s_o, reads=[b_o]))
        while gst["next"] < NCH:
            gla_step()
        fw.wait("sp", outs[-6:])
        fw.emit()
    return nc


def build_l3(TOK):
    NB = TOK // TB
    NTB = TB // 128
    nc = bass.Bass("TRN2", target_bir_lowering=False)
    di = lambda n, s, dt: nc.dram_tensor(n, s, dt, kind="ExternalInput").ap()
    x = di("x", [TOK, D], F32)
    sbT = di("sbT", [1024, TOK], BF16)
    go = di("go", [TOK, 1024], BF16)
    mem = di("mem", [NMEM, D], F32)
    wl = di("wl", [D, NL], F32)
    wbr = [di(n, [1024, D], F32) for n in ("wbg", "wbs", "wbm")]
    wkv = di("wkv", [D, 2048], F32)
    wo = di("wo", [D, D], F32)
    wgu = di("wgu", [D, 2 * DFF], F32)
    wd = di("wd", [DFF, D], F32)
    an = di("an", [128, 16], F32); fn = di("fn", [128, 16], F32); mn = di("mn", [128, 16], F32)
    gon = di("gon", [128, 1024], F32)
    mqn = di("mqn", [128, 2], F32); mkn = di("mkn", [128, 2], F32)
    idn = di("idn", [128, 128], BF16)
    y = nc.dram_tensor("y", [TOK, D], F32, kind="ExternalOutput").ap()
    with contextlib.ExitStack() as st:
        fw = FW(nc, st)
        cm = Common(fw, nc)

        def const(name, src, shape, dt):
            t = fw.sbuf(name, shape, dt); b = Buf()
            cm.load_const(t[:], src, b)
            return t, b
        cm.load_const(cm.ident[:], idn, cm.b_ident)
        an_s, b_an = const("an_s", an, [128, 16], F32)
        fn_s, b_fn = const("fn_s", fn, [128, 16], F32)
        mn_s, b_mn = const("mn_s", mn, [128, 16], F32)
        gon_s, b_gon = const("gon_s", gon, [128, 1024], F32)
        mqn_s, b_mqn = const("mqn_s", mqn, [128, 2], F32)
        mkn_s, b_mkn = const("mkn_s", mkn, [128, 2], F32)

        if PRECAST:
            wl = cm.precast("wl", wl, D, NL)
            wbr = [cm.precast(n, w, 1024, D) for n, w in zip(("wbg", "wbs", "wbm"), wbr)]
            wo = cm.precast("wo", wo, D, D)
            wgu = cm.precast("wgu", wgu, D, 2 * DFF)
            wd = cm.precast("wd", wd, DFF, D)
        big = fw.sbuf("big", [128, 48, TB], BF16)
        b_big = [Buf() for _ in range(48)]
        hT = fw.sbuf("hT", [128, 16, TB], BF16); b_hT = Buf()
        mT = fw.sbuf("mT", [128, 16, TB], BF16); b_mT = [Buf() for _ in range(16)]
        ogT = fw.sbuf("ogT", [128, 8, TB], BF16); b_ogT = Buf()
        omT = fw.sbuf("omT", [128, 8, TB], BF16); b_omT = Buf()
        sbTs = fw.sbuf("sbTs", [128, 8, TB], BF16); b_sbTs = Buf(); s_sbTs = fw.new_sem("s_sbTs")
        sgr = fw.sbuf("sgr", [128, NTB, 1024], BF16); b_sgr = [Buf() for _ in range(NTB)]
        qm = fw.sbuf("qm", [128, NTB, 1024], BF16); b_qm = [Buf() for _ in range(NTB)]
        x1s = fw.sbuf("x1s", [128, NTB, D], F32); b_x1s = [Buf() for _ in range(NTB)]
        mkT = fw.sbuf("mkT", [128, 4, 2, NMEM], BF16); b_mkT = Buf()
        mv = fw.sbuf("mv", [128, 2, 1024], BF16); b_mv = Buf()
        mktm = fw.sbuf("mktm", [128, 2, 1024], BF16); b_mktm = Buf()
        sq32 = fw.sbuf("sq32", [128, 1024], F32); b_sq32 = Buf()
        t32 = Rot(fw, "t32", [128, 512], F32, 4)
        gor = Rot(fw, "gor", [128, 1024], BF16, 2, sem=True)
        ogb = Rot(fw, "ogb", [128, 1024], BF16, 2)
        og32, b_og32 = sq32, b_sq32
        qmT = fw.sbuf("qmT", [128, 8, 128], BF16); b_qmT = Buf()
        pbf = Rot(fw, "pbf", [128, 256], BF16, 2)
        pTs = fw.sbuf("pTs", [128, 2, 128], BF16); b_pTs = Buf()
        ys = Rot(fw, "ys", [128, 512], F32, 2, sem=True)
        mg32 = fw.sbuf("mg32", [128, 16, TB], BF16); b_mg = [Buf() for _ in range(16)]

        def mm_tm(lhs_fn, b_lhs, nk, wb, b_wb, ncols, ps, b_ps, first=True, last=True):
            for kc in range(nk):
                fw.op("pe", lambda e, kc=kc: e.matmul(ps[:, 0:ncols], lhsT=lhs_fn(kc), rhs=wb[:, kc, 0:ncols],
                                                      start=(first and kc == 0), stop=(last and kc == nk - 1)),
                      reads=list(b_lhs) + [b_wb], writes=[b_ps])

        def mm_fm(wb, b_wb, c0, nk, rhs_fn, b_rhs, ps, b_ps):
            for kc in range(nk):
                fw.op("pe", lambda e, kc=kc: e.matmul(ps[:, 0:TB], lhsT=wb[:, kc, c0:c0 + 128], rhs=rhs_fn(kc),
                                                      start=(kc == 0), stop=(kc == nk - 1)),
                      reads=list(b_rhs) + [b_wb], writes=[b_ps])

        def head_rstd(src_ap, b_src, nh, dh, scale, bias):
            fw.op("act", lambda e: e.activation(out=sq32[:, 0:nh * dh], in_=src_ap, func=AF.Square),
                  reads=[b_src], writes=[b_sq32])
            ss, b_ss = cm.st.next()
            fw.op("dve", lambda e: e.tensor_reduce(out=ss[:, 0:nh], in_=sq32[:, 0:nh * dh].rearrange("p (h d) -> p h d", h=nh),
                                                   axis=AX.X, op=ALU.add), reads=[b_sq32], writes=[b_ss])
            return cm.rstd(ss[:, 0:nh], b_ss, nh, scale, bias)

        memT = big
        for mt in range(2):
            xt, b_xt, s_xt = cm.xr.next()
            fw.dma("sp", xt[:], mem[mt * 128:(mt + 1) * 128, :], s_xt, writes=[b_xt])
            cm.norm_T(xt[:], b_xt, mn_s, b_mn, memT, b_big[0], mt * 128)
        for nb in range(4):
            wb, b_wb = cm.load_w(wkv, 0, 16, nb * 512, 512)
            for mt in range(2):
                ps, b_ps = cm.ps.next()
                mm_tm(lambda kc, mt=mt: memT[:, kc, mt * 128:(mt + 1) * 128], [b_big[0]], 16, wb, b_wb, 512, ps, b_ps)
                if nb < 2:
                    rs, b_rs = head_rstd(ps[:], b_ps, 2, 256, 1.0 / 256, EPS)
                    for h in range(2):
                        fw.op("dve", lambda e, h=h, ps=ps, rs=rs, mt=mt, nb=nb: e.tensor_scalar(
                            out=mktm[:, mt, nb * 512 + h * 256: nb * 512 + (h + 1) * 256], in0=ps[:, h * 256:(h + 1) * 256],
                            scalar1=rs[:, h:h + 1], scalar2=None, op0=ALU.mult), reads=[b_ps, b_rs], writes=[b_mktm])
                else:
                    fw.op("act", lambda e, ps=ps, mt=mt, nb=nb: e.activation(out=mv[:, mt, (nb - 2) * 512:(nb - 1) * 512], in_=ps[:], func=AF.Copy),
                          reads=[b_ps], writes=[b_mv])
        for mt in range(2):
            cm.transposes(mktm[:, mt, :], b_mktm, 8,
                          lambda c, mt=mt: mkT[:, c // 2, c % 2, mt * 128:(mt + 1) * 128], b_mkT,
                          lambda c: mkn_s[:, (c % 2):(c % 2) + 1], b_mkn)

        outs = []
        for blk in range(NB):
            t0 = blk * TB
            for i in range(NTB):
                xt, b_xt, s_xt = cm.xr.next()
                fw.dma("sp", xt[:], x[t0 + i * 128: t0 + (i + 1) * 128, :], s_xt, writes=[b_xt])
                cm.norm_T(xt[:], b_xt, an_s, b_an, hT, b_hT, i * 128)
            fw.dma("sp", sbTs[:], sbT[:, t0:t0 + TB].rearrange("(c p) t -> p c t", p=128), s_sbTs, writes=[b_sbTs])
            for nb in range(4):
                wb, b_wb = cm.load_w(wl, 0, 16, nb * 512, 512)
                for i in range(NTB):
                    ps, b_ps = cm.ps.next()
                    mm_tm(lambda kc, i=i: hT[:, kc, i * 128:(i + 1) * 128], [b_hT], 16, wb, b_wb, 512, ps, b_ps)
                    if nb < 2:
                        fw.op("act", lambda e, ps=ps, i=i, nb=nb: e.activation(out=sgr[:, i, nb * 512:(nb + 1) * 512], in_=ps[:], func=AF.Silu),
                              reads=[b_ps], writes=[b_sgr[i]])
                    else:
                        rs, b_rs = head_rstd(ps[:], b_ps, 2, 256, 1.0, 256 * EPS)
                        for h in range(2):
                            fw.op("dve", lambda e, h=h, ps=ps, rs=rs, i=i, nb=nb: e.tensor_scalar(
                                out=qm[:, i, (nb - 2) * 512 + h * 256:(nb - 2) * 512 + (h + 1) * 256], in0=ps[:, h * 256:(h + 1) * 256],
                                scalar1=rs[:, h:h + 1], scalar2=None, op0=ALU.mult), reads=[b_ps, b_rs], writes=[b_qm[i]])
            for nb in range(12):
                wb, b_wb = cm.load_w(wl, 0, 16, 2048 + nb * 512, 512)
                for s in range(4):
                    ch = nb * 4 + s
                    ps, b_ps = cm.ps.next()
                    mm_fm(wb, b_wb, s * 128, 16, lambda kc: hT[:, kc, :], [b_hT], ps, b_ps)
                    fw.op("act", lambda e, ps=ps, ch=ch: e.activation(out=big[:, ch, :], in_=ps[:, 0:TB], func=AF.Sigmoid),
                          reads=[b_ps], writes=[b_big[ch]])
            for i in range(NTB):
                gt, b_gt, s_gt = gor.next()
                fw.dma("sp", gt[:], go[t0 + i * 128:t0 + (i + 1) * 128, :], s_gt, writes=[b_gt])
                rs, b_rs = head_rstd(gt[:], b_gt, 4, 256, 1.0 / 256, EPS)
                for h in range(4):
                    hs = slice(h * 256, (h + 1) * 256)
                    fw.op("dve", lambda e, h=h, hs=hs, gt=gt, rs=rs: e.scalar_tensor_tensor(
                        out=og32[:, hs], in0=gt[:, hs], scalar=rs[:, h:h + 1], in1=gon_s[:, hs], op0=ALU.mult, op1=ALU.mult),
                        reads=[b_gt, b_rs, b_gon], writes=[b_og32])
                ob_, b_ob = ogb.next()
                fw.op("dve", lambda e, ob_=ob_, i=i: e.tensor_tensor(out=ob_[:], in0=og32[:], in1=sgr[:, i, :], op=ALU.mult),
                      reads=[b_og32, b_sgr[i]], writes=[b_ob])
                cm.transposes(ob_, b_ob, 8, lambda c, i=i: ogT[:, c, i * 128:(i + 1) * 128], b_ogT)
            for i in range(NTB):
                cm.transposes(qm[:, i, :], b_qm[i], 8, lambda c: qmT[:, c, :], b_qmT,
                              lambda c: mqn_s[:, (c % 2):(c % 2) + 1], b_mqn)
                for h in range(4):
                    ps, b_ps = cm.ps.next()
                    for c2 in range(2):
                        fw.op("pe", lambda e, ps=ps, h=h, c2=c2: e.matmul(ps[:, 0:NMEM], lhsT=qmT[:, h * 2 + c2, :], rhs=mkT[:, h, c2, :],
                                                                         start=(c2 == 0), stop=(c2 == 1)),
                              reads=[b_qmT, b_mkT], writes=[b_ps])
                    mx, b_mx = cm.st.next()
                    fw.op("dve", lambda e, ps=ps, mx=mx: e.tensor_reduce(out=mx[:, 0:1], in_=ps[:, 0:NMEM], axis=AX.X, op=ALU.max),
                          reads=[b_ps], writes=[b_mx])
                    fw.op("dve", lambda e, mx=mx: e.tensor_scalar(out=mx[:, 1:2], in0=mx[:, 0:1], scalar1=-1.0, scalar2=None, op0=ALU.mult),
                          reads=[b_mx], writes=[b_mx])
                    pe_, b_pe = t32.next()
                    fw.op("act", lambda e, ps=ps, mx=mx, pe_=pe_: e.activation(out=pe_[:, 0:NMEM], in_=ps[:, 0:NMEM], func=AF.Exp,
                                                                             bias=mx[:, 1:2], accum_out=mx[:, 2:3]),
                          reads=[b_ps, b_mx], writes=[b_pe, b_mx])
                    fw.op("dve", lambda e, mx=mx: e.reciprocal(out=mx[:, 3:4], in_=mx[:, 2:3]), reads=[b_mx], writes=[b_mx])
                    pb_, b_pb = pbf.next()
                    fw.op("dve", lambda e, pb_=pb_, pe_=pe_, mx=mx: e.tensor_scalar(out=pb_[:], in0=pe_[:, 0:NMEM], scalar1=mx[:, 3:4],
                                                                                  scalar2=None, op0=ALU.mult),
                          reads=[b_pe, b_mx], writes=[b_pb])
                    cm.transposes(pb_, b_pb, 2, lambda c: pTs[:, c, :], b_pTs)
                    for c2 in range(2):
                        ps2, b_ps2 = cm.ps.next()
                        for mc in range(2):
                            fw.op("pe", lambda e, ps2=ps2, h=h, c2=c2, mc=mc: e.matmul(
                                ps2[:, 0:128], lhsT=mv[:, mc, h * 256 + c2 * 128: h * 256 + (c2 + 1) * 128], rhs=pTs[:, mc, :],
                                start=(mc == 0), stop=(mc == 1)), reads=[b_mv, b_pTs], writes=[b_ps2])
                        fw.op("act", lambda e, ps2=ps2, h=h, c2=c2, i=i: e.activation(out=omT[:, h * 2 + c2, i * 128:(i + 1) * 128],
                                                                                    in_=ps2[:, 0:128], func=AF.Copy),
                              reads=[b_ps2], writes=[b_omT])
            srcs = [(ogT, b_ogT), (sbTs, b_sbTs), (omT, b_omT)]
            for br in range(3):
                src, b_src = srcs[br]
                for nb in range(4):
                    wb, b_wb = cm.load_w(wbr[br], 0, 8, nb * 512, 512)
                    for s in range(4):
                        ncn = nb * 4 + s
                        gch = br * 16 + ncn
                        ps, b_ps = cm.ps.next()
                        mm_fm(wb, b_wb, s * 128, 8, lambda kc, src=src: src[:, kc, :], [b_src], ps, b_ps)
                        if br == 0:
                            fw.op("dve", lambda e, ps=ps, ncn=ncn, gch=gch: e.tensor_tensor(out=mg32[:, ncn, :], in0=ps[:, 0:TB], in1=big[:, gch, :], op=ALU.mult),
                                  reads=[b_ps, b_big[gch]], writes=[b_mg[ncn]])
                        else:
                            tt, b_tt = t32.next()
                            fw.op("dve", lambda e, ps=ps, tt=tt, gch=gch: e.tensor_tensor(out=tt[:, 0:TB], in0=ps[:, 0:TB], in1=big[:, gch, :], op=ALU.mult),
                                  reads=[b_ps, b_big[gch]], writes=[b_tt])
                            if br == 1:
                                fw.op("pool", lambda e, tt=tt, ncn=ncn: e.tensor_tensor(out=mg32[:, ncn, :], in0=mg32[:, ncn, :], in1=tt[:, 0:TB], op=ALU.add),
                                      reads=[b_tt, b_mg[ncn]], writes=[b_mg[ncn]])
                            else:
                                fw.op("pool", lambda e, tt=tt, ncn=ncn: e.tensor_tensor(out=mT[:, ncn, :], in0=mg32[:, ncn, :], in1=tt[:, 0:TB], op=ALU.add),
                                      reads=[b_tt, b_mg[ncn]], writes=[b_mT[ncn]])
            for nb in range(4):
                wb, b_wb = cm.load_w(wo, 0, 16, nb * 512, 512)
                for i in range(NTB):
                    ps, b_ps = cm.ps.next()
                    mm_tm(lambda kc, i=i: mT[:, kc, i * 128:(i + 1) * 128], b_mT, 16, wb, b_wb, 512, ps, b_ps)
                    xt, b_xt, s_xt = ys.next()
                    fw.dma("sp", xt[:], x[t0 + i * 128:t0 + (i + 1) * 128, nb * 512:(nb + 1) * 512], s_xt, writes=[b_xt])
                    fw.op("dve", lambda e, ps=ps, xt=xt, i=i, nb=nb: e.tensor_tensor(out=x1s[:, i, nb * 512:(nb + 1) * 512], in0=ps[:], in1=xt[:], op=ALU.add),
                          reads=[b_ps, b_xt], writes=[b_x1s[i]])
            for i in range(NTB):
                cm.norm_T(x1s[:, i, :], b_x1s[i], fn_s, b_fn, hT, b_hT, i * 128)
            for nb in range(11):
                wg_, b_wg = cm.load_w(wgu, 0, 16, nb * 512, 512)
                sgs = []
                for s in range(4):
                    ps, b_ps = cm.ps.next()
                    mm_fm(wg_, b_wg, s * 128, 16, lambda kc: hT[:, kc, :], [b_hT], ps, b_ps)
                    sg, b_sg = t32.next()
                    fw.op("act", lambda e, ps=ps, sg=sg: e.activation(out=sg[:, 0:TB], in_=ps[:, 0:TB], func=AF.Silu), reads=[b_ps], writes=[b_sg])
                    sgs.append((sg, b_sg))
                wu_, b_wu = cm.load_w(wgu, 0, 16, DFF + nb * 512, 512)
                for s in range(4):
                    ch = nb * 4 + s
                    ps, b_ps = cm.ps.next()
                    mm_fm(wu_, b_wu, s * 128, 16, lambda kc: hT[:, kc, :], [b_hT], ps, b_ps)
                    sg, b_sg = sgs[s]
                    fw.op("dve", lambda e, ps=ps, sg=sg, ch=ch: e.tensor_tensor(out=big[:, ch, :], in0=ps[:, 0:TB], in1=sg[:, 0:TB], op=ALU.mult),
                          reads=[b_ps, b_sg], writes=[b_big[ch]])
            kgs = [(0, 16), (16, 16), (32, 12)]
            for nb in range(4):
                pss = [cm.ps.next() for _ in range(NTB)]
                for gi, (k0, nk) in enumerate(kgs):
                    wb, b_wb = cm.load_w(wd, k0, nk, nb * 512, 512)
                    for i in range(NTB):
                        ps, b_ps = pss[i]
                        mm_tm(lambda kc, i=i, k0=k0: big[:, k0 + kc, i * 128:(i + 1) * 128], b_big[k0:k0 + nk], nk, wb, b_wb, 512, ps, b_ps,
                              first=(gi == 0), last=(gi == 2))
                for i in range(NTB):
                    ps, b_ps = pss[i]
                    yt, b_yt, s_yt = ys.next()
                    fw.op("dve", lambda e, ps=ps, yt=yt, i=i, nb=nb: e.tensor_tensor(out=yt[:], in0=ps[:], in1=x1s[:, i, nb * 512:(nb + 1) * 512], op=ALU.add),
                          reads=[b_ps, b_x1s[i]], writes=[b_yt])
                    outs.append(fw.dma("sp", y[t0 + i * 128:t0 + (i + 1) * 128, nb * 512:(nb + 1) * 512], yt[:], s_yt, reads=[b_yt]))
        fw.wait("sp", outs[-6:])
        fw.emit()
    return nc


_CACHE = {}


def _get(name, fn, *a):
    k = (name,) + a
    if k not in _CACHE:
        _CACHE[k] = fn(*a)
    return _CACHE[k]


def _fm(g):
    return np.ascontiguousarray(np.asarray(g, np.float32).reshape(-1, 128).T)


def _bc(g, rep):
    return np.ascontiguousarray(np.broadcast_to(np.tile(np.asarray(g, np.float32), rep)[None, :], (128, g.shape[0] * rep)))


def run_l1(x2, wh, attn_norm, sbq, sbk):
    T = x2.shape[0]
    TOK = T // K1
    nc = _get("l1", build_l1, TOK)
    idn = np.eye(128, dtype=np.float32).astype(NPBF)
    ims = [dict(x=x2[c * TOK:(c + 1) * TOK], wh=wh, gn=_fm(attn_norm), qg=_bc(sbq, 8), kg=_bc(sbk, 8), idn=idn) for c in range(K1)]
    res = run_bass_kernel_spmd(nc, ims, core_ids=list(range(K1))).results
    oh = np.concatenate([r["oh"] for r in res], 0)
    oa = np.concatenate([r["oa"] for r in res], 0)
    return oh, oa


def run_l2(oh, oa, w_a2, b_a):
    T = oh.shape[0]
    nc = _get("l2", build_l2, T)
    cs = l2_consts()
    sq, sk, sv = oh[:, 0:1024], oh[:, 1024:2048], oh[:, 2048:3072]
    gq, gk, gv = oh[:, 3072:3584], oh[:, 3584:4096], oh[:, 4096:5120]
    ga = np.ascontiguousarray(np.concatenate([oa.T, np.ones((1, T), np.float32)], 0))
    ims = []
    for c in range(8):
        hg, half = c // 2, c % 2
        hs = slice(c * 128, (c + 1) * 128)
        gs = slice(hg * 128, (hg + 1) * 128)
        vs = slice(hg * 256 + half * 128, hg * 256 + (half + 1) * 128)
        wa = np.ascontiguousarray(np.concatenate([w_a2[:, gs], b_a[None, gs]], 0).astype(np.float32))
        ims.append(dict(sqT=np.ascontiguousarray(sq[:, hs].T), skT=np.ascontiguousarray(sk[:, hs].T), sv=np.ascontiguousarray(sv[:, hs]),
                        gqT=np.ascontiguousarray(gq[:, gs].T), gkT=np.ascontiguousarray(gk[:, gs].T),
                        gk=np.ascontiguousarray(gk[:, gs]), gv=np.ascontiguousarray(gv[:, vs]), ga=ga, wa=wa, **cs))
    res = run_bass_kernel_spmd(nc, ims, core_ids=list(range(8))).results
    sbT = np.concatenate([r["sbT"] for r in res], 0)
    go = np.concatenate([r["go"] for r in res], 1)
    return sbT, go


def run_l3(x2, sbT, go, mem2, p):
    T = x2.shape[0]
    TOK = T // K3
    nc = _get("l3", build_l3, TOK)
    idn = np.eye(128, dtype=np.float32).astype(NPBF)
    ims = []
    for c in range(K3):
        ts = slice(c * TOK, (c + 1) * TOK)
        ims.append(dict(x=x2[ts], sbT=np.ascontiguousarray(sbT[:, ts]), go=np.ascontiguousarray(go[ts]), mem=mem2,
                        wl=p["wl"], wbg=p["wbg"], wbs=p["wbs"], wbm=p["wbm"], wkv=p["wkv"], wo=p["wo"], wgu=p["wgu"], wd=p["wd"],
                        an=_fm(p["attn_norm"]), fn=_fm(p["ffn_norm"]), mn=_fm(p["mem_norm"]), gon=_bc(p["gla_out_norm"], 4),
                        mqn=np.ascontiguousarray(p["mem_q_norm"].reshape(2, 128).T), mkn=np.ascontiguousarray(p["mem_k_norm"].reshape(2, 128).T),
                        idn=idn))
    res = run_bass_kernel_spmd(nc, ims, core_ids=list(range(K3))).results
    return np.concatenate([r["y"] for r in res], 0)


def split_w_in(w):
    gq, gk, gv, gr = w[:, 0:512], w[:, 512:1024], w[:, 1024:2048], w[:, 2048:3072]
    ga1 = w[:, 3072:3088]
    sq, sk, sv = w[:, 3088:4112], w[:, 4112:5136], w[:, 5136:6160]
    mq, gates = w[:, 6160:7184], w[:, 7184:13328]
    wh = np.ascontiguousarray(np.concatenate([sq, sk, sv, gq, gk, gv, ga1], 1))
    wl = np.ascontiguousarray(np.concatenate([gr, mq, gates], 1))
    return wh, wl


def kernel(x, mem, attn_norm, w_in, gla_w_a2, gla_b_a, gla_out_norm, w_br_gla,
           sb_q_norm, sb_k_norm, w_br_sb, mem_norm, w_mem_kv, mem_q_norm, mem_k_norm,
           w_br_mem, w_o, ffn_norm, w_gate_up, w_down):
    f = lambda a: np.asarray(a, np.float32)
    x2 = np.ascontiguousarray(f(x)[0])
    mem2 = np.ascontiguousarray(f(mem)[0])
    for l in range(w_in.shape[0]):
        wh, wl = split_w_in(f(w_in[l]))
        oh, oa = run_l1(x2, wh, f(attn_norm[l]), f(sb_q_norm[l]), f(sb_k_norm[l]))
        sbT, go = run_l2(oh, oa, f(gla_w_a2[l]), f(gla_b_a[l]))
        p = dict(wl=wl, wbg=f(w_br_gla[l]), wbs=f(w_br_sb[l]), wbm=f(w_br_mem[l]), wkv=f(w_mem_kv[l]), wo=f(w_o[l]),
                 wgu=f(w_gate_up[l]), wd=f(w_down[l]), attn_norm=f(attn_norm[l]), ffn_norm=f(ffn_norm[l]),
                 mem_norm=f(mem_norm[l]), gla_out_norm=f(gla_out_norm[l]), mem_q_norm=f(mem_q_norm[l]),
                 mem_k_norm=f(mem_k_norm[l]))
        x2 = run_l3(x2, sbT, go, mem2, p)
    return x2[None].astype(np.float32)
```

```python
import contextlib
import numpy as np
import ml_dtypes
import concourse.bass as bass
import concourse.mybir as mybir
from concourse.bass_utils import run_bass_kernel_spmd

F32 = mybir.dt.float32
BF16 = mybir.dt.bfloat16
AF = mybir.ActivationFunctionType
ALU = mybir.AluOpType
AX = mybir.AxisListType
NPBF = ml_dtypes.bfloat16

D = 2048
DFF = 5632
NMEM = 256
EPS = 1e-6
NH1 = 5136
NL = 8192
K1 = 8
K3 = 8
L1P = 2048
TB = 256
NWB = 3
PRECAST = True


class Sem:
    def __init__(self, h):
        self.h = h
        self.count = 0


class Buf:
    __slots__ = ("w", "r")

    def __init__(self):
        self.w = None
        self.r = []


class FW:
    ENGS = ("pe", "act", "dve", "pool", "sp")
    M = 8

    def __init__(self, nc, stack):
        self.nc = nc
        self.stack = stack
        self.q = {e: [] for e in self.ENGS}
        self.esem = {e: [self.new_sem(f"e_{e}{j}") for j in range(self.M)] for e in self.ENGS}
        self.eidx = {e: 0 for e in self.ENGS}
        self.seen_e = {e: {s: 0 for s in self.ENGS} for e in self.ENGS}
        self.seen_d = {e: {} for e in self.ENGS}
        self.n = 0

    def new_sem(self, name):
        return Sem(self.stack.enter_context(self.nc.semaphore(name)))

    def sbuf(self, name, shape, dt):
        return self.stack.enter_context(self.nc.sbuf_tensor(name, list(shape), dt))

    def psum(self, name, shape, dt):
        return self.stack.enter_context(self.nc.psum_tensor(name, list(shape), dt))

    def _waits(self, eng, deps):
        out = {}
        for t in deps:
            if t is None:
                continue
            kind, src, val = t
            if kind == "e":
                if src == "pe" and eng == "pe":
                    continue
                if self.seen_e[eng][src] >= val:
                    continue
                key = ("e", src)
                if out.get(key, 0) < val:
                    out[key] = val
            else:
                if self.seen_d[eng].get(src, 0) >= val:
                    continue
                key = ("d", src)
                if out.get(key, 0) < val:
                    out[key] = val
        res = []
        for (kind, src), val in out.items():
            if kind == "e":
                self.seen_e[eng][src] = val
                j = (val - 1) % self.M
                res.append((self.esem[src][j].h, (val - 1) // self.M + 1))
            else:
                self.seen_d[eng][src] = val
                res.append((src.h, val))
        return res

    @staticmethod
    def _deps(reads, writes, extra):
        deps = list(extra)
        for b in reads:
            deps.append(b.w)
        for b in writes:
            deps.append(b.w)
            deps.extend(b.r)
        return deps

    @staticmethod
    def _commit(tok, reads, writes):
        for b in reads:
            b.r.append(tok)
            if len(b.r) > 48:
                del b.r[:-48]
        for b in writes:
            b.w = tok
            b.r = []

    def op(self, eng, fn, reads=(), writes=(), deps=()):
        waits = self._waits(eng, self._deps(reads, writes, deps))
        self.eidx[eng] += 1
        idx = self.eidx[eng]
        tok = ("e", eng, idx)
        s = self.esem[eng][(idx - 1) % self.M]
        self.q[eng].append((waits, fn, (s.h, 1)))
        self.n += 1 + len(waits)
        self._commit(tok, reads, writes)
        return tok

    def dma(self, eng, out, in_, sem, reads=(), writes=(), deps=()):
        waits = self._waits(eng, self._deps(reads, writes, deps))
        sem.count += 16
        tok = ("d", sem, sem.count)
        self.q[eng].append((waits, lambda e: e.dma_start(out=out, in_=in_), (sem.h, 16)))
        self.n += 1 + len(waits)
        self._commit(tok, reads, writes)
        return tok

    def wait(self, eng, deps):
        waits = self._waits(eng, deps)
        if waits:
            self.q[eng].append((waits, None, None))

    def emit(self):
        q = self.q

        def run(e, lst):
            for waits, fn, inc in lst:
                for (h, v) in waits:
                    e.wait_ge(h, v)
                if fn is not None:
                    fn(e).then_inc(inc[0], inc[1])

        with self.nc.Block() as block:
            @block.tensor
            def _(e):
                run(e, q["pe"])

            @block.scalar
            def _(e):
                run(e, q["act"])

            @block.vector
            def _(e):
                run(e, q["dve"])

            @block.gpsimd
            def _(e):
                run(e, q["pool"])

            @block.sync
            def _(e):
                run(e, q["sp"])


class Rot:
    def __init__(self, fw, name, shape, dt, n, psum=False, sem=False):
        self.t = [(fw.psum if psum else fw.sbuf)(f"{name}{i}", shape, dt) for i in range(n)]
        self.b = [Buf() for _ in range(n)]
        self.s = [fw.new_sem(f"s_{name}{i}") for i in range(n)] if sem else None
        self.i = -1
        self.n = n

    def final_tokens(self):
        return [("d", sm, sm.count) for sm in (self.s or []) if sm.count > 0]

    def next(self):
        self.i = (self.i + 1) % self.n
        if self.s:
            return self.t[self.i], self.b[self.i], self.s[self.i]
        return self.t[self.i], self.b[self.i]


class Common:
    def __init__(self, fw, nc):
        self.fw = fw
        self.nc = nc
        self.ident = fw.sbuf("ident", [128, 128], BF16)
        self.b_ident = Buf()
        self.s_const = fw.new_sem("s_const")
        self.const_bufs = []
        self.xr = Rot(fw, "xr", [128, D], F32, 2, sem=True)
        self.hb = Rot(fw, "hb", [128, D], BF16, 2)
        self.junk = fw.sbuf("junk", [128, D], BF16)
        self.b_junk = Buf()
        self.st = Rot(fw, "st", [128, 8], F32, 4)
        self.ps = Rot(fw, "ps", [128, 512], F32, 6, psum=True)
        self.pt = Rot(fw, "pt", [128, 1024], BF16, 2, psum=True)
        self.wb = Rot(fw, "wb", [128, 16, 512], BF16, NWB, sem=True)

    def load_const(self, dst, src, b):
        self.fw.dma("sp", dst, src, self.s_const, writes=[b])
        self.const_bufs.append(b)

    def finish_consts(self):
        tok = ("d", self.s_const, self.s_const.count)
        for b in self.const_bufs:
            b.w = tok

    def rstd(self, ssq_ap, b_ssq, n, scale, bias):
        fw = self.fw
        st, b_st = self.st.next()
        fw.op("act", lambda e: e.activation(out=st[:, 0:n], in_=ssq_ap, func=AF.Sqrt, scale=scale, bias=bias),
              reads=[b_ssq], writes=[b_st])
        fw.op("dve", lambda e: e.reciprocal(out=st[:, 0:n], in_=st[:, 0:n]), reads=[b_st], writes=[b_st])
        return st, b_st

    def norm_T(self, src, b_src, gain_fm, b_gain, dstT, b_dst, col0):
        fw = self.fw
        ss, b_ss = self.st.next()
        fw.op("act", lambda e: e.activation(out=self.junk[:], in_=src, func=AF.Square, accum_out=ss[:, 0:1]),
              reads=[b_src], writes=[self.b_junk, b_ss])
        rs, b_rs = self.rstd(ss[:, 0:1], b_ss, 1, 1.0 / D, EPS)
        hb, b_hb = self.hb.next()
        fw.op("dve", lambda e: e.tensor_scalar(out=hb[:], in0=src, scalar1=rs[:, 0:1], scalar2=None, op0=ALU.mult),
              reads=[b_src, b_rs], writes=[b_hb])
        self.transposes(hb, b_hb, D // 128, lambda c: dstT[:, c, col0:col0 + 128], b_dst,
                        lambda c: gain_fm[:, c:c + 1], b_gain)

    def transposes(self, src, b_src, nch, dst_fn, b_dst, scale_fn=None, b_scale=None):
        fw = self.fw
        for c0 in range(0, nch, 8):
            pt, b_pt = self.pt.next()
            m = min(8, nch - c0)
            for j in range(m):
                c = c0 + j
                fw.op("pe", lambda e, c=c, j=j, pt=pt: e.transpose(out=pt[:, j * 128:(j + 1) * 128],
                                                                 in_=src[:, c * 128:(c + 1) * 128],
                                                                 identity=self.ident[:]),
                      reads=[b_src, self.b_ident], writes=[b_pt])
            for j in range(m):
                c = c0 + j
                eng = "dve" if j % 2 == 0 else "act"
                rd = [b_pt] + ([b_scale] if b_scale is not None else [])
                if scale_fn is None:
                    if eng == "dve":
                        fw.op("dve", lambda e, c=c, j=j, pt=pt: e.tensor_copy(out=dst_fn(c), in_=pt[:, j * 128:(j + 1) * 128]),
                              reads=rd, writes=[b_dst])
                    else:
                        fw.op("act", lambda e, c=c, j=j, pt=pt: e.activation(out=dst_fn(c), in_=pt[:, j * 128:(j + 1) * 128], func=AF.Copy),
                              reads=rd, writes=[b_dst])
                else:
                    if eng == "dve":
                        fw.op("dve", lambda e, c=c, j=j, pt=pt: e.tensor_scalar(out=dst_fn(c), in0=pt[:, j * 128:(j + 1) * 128],
                                                                               scalar1=scale_fn(c), scalar2=None, op0=ALU.mult),
                              reads=rd, writes=[b_dst])
                    else:
                        fw.op("act", lambda e, c=c, j=j, pt=pt: e.activation(out=dst_fn(c), in_=pt[:, j * 128:(j + 1) * 128],
                                                                            func=AF.Copy, scale=scale_fn(c)),
                              reads=rd, writes=[b_dst])

    def load_w(self, w_ap, k0, nk, n0, ncols):
        wb, b_wb, s_wb = self.wb.next()
        if isinstance(w_ap, tuple):
            t, b_w = w_ap
            assert n0 % 512 == 0 and ncols == 512
            src = t[n0 // 512][:, k0:k0 + nk, :]
            self.fw.dma("pool", wb[:, 0:nk, 0:ncols], src, s_wb, reads=[b_w], writes=[b_wb])
        else:
            src = w_ap[k0 * 128:(k0 + nk) * 128, n0:n0 + ncols].rearrange("(c p) n -> p c n", p=128)
            self.fw.dma("pool", wb[:, 0:nk, 0:ncols], src, s_wb, writes=[b_wb])
        return wb, b_wb

    def precast(self, name, w_ap, K, N):
        KC, NBK = K // 128, N // 512
        t = self.nc.dram_tensor(name + "_bf", [NBK, 128, KC, 512], BF16)
        b = Buf()
        sem = self.fw.new_sem("s_pc_" + name)
        tok = None
        for nb in range(NBK):
            src = w_ap[:, nb * 512:(nb + 1) * 512].rearrange("(c p) n -> p c n", p=128)
            tok = self.fw.dma("pool", t[nb], src, sem)
        b.w = tok
        return (t, b)


def build_l1(TOK):
    PT = min(TOK, L1P)
    NTP = PT // 128
    nc = bass.Bass("TRN2", target_bir_lowering=False)
    x = nc.dram_tensor("x", [TOK, D], F32, kind="ExternalInput").ap()
    wh = nc.dram_tensor("wh", [D, NH1], F32, kind="ExternalInput").ap()
    gn = nc.dram_tensor("gn", [128, 16], F32, kind="ExternalInput").ap()
    qg = nc.dram_tensor("qg", [128, 1024], F32, kind="ExternalInput").ap()
    kg = nc.dram_tensor("kg", [128, 1024], F32, kind="ExternalInput").ap()
    idn = nc.dram_tensor("idn", [128, 128], BF16, kind="ExternalInput").ap()
    oh = nc.dram_tensor("oh", [TOK, 5120], BF16, kind="ExternalOutput").ap()
    oa = nc.dram_tensor("oa", [TOK, 16], F32, kind="ExternalOutput").ap()
    with contextlib.ExitStack() as st:
        fw = FW(nc, st)
        cm = Common(fw, nc)
        gn_s = fw.sbuf("gn_s", [128, 16], F32); b_gn = Buf()
        qg_s = fw.sbuf("qg_s", [128, 1024], F32); b_qg = Buf()
        kg_s = fw.sbuf("kg_s", [128, 1024], F32); b_kg = Buf()
        hT = fw.sbuf("hT", [128, 16, PT], BF16); b_hT = Buf()
        sqs = fw.sbuf("sqs", [128, 512], F32); b_sqs = Buf()
        ob = Rot(fw, "ob", [128, 512], BF16, 3, sem=True)
        oab = Rot(fw, "oab", [128, 16], F32, 2, sem=True)
        cm.load_const(cm.ident[:], idn, cm.b_ident)
        cm.load_const(gn_s[:], gn, b_gn)
        cm.load_const(qg_s[:], qg, b_qg)
        cm.load_const(kg_s[:], kg, b_kg)
        cm.finish_consts()
        outs = []
        nblocks = [(i * 512, 512) for i in range(10)] + [(5120, 16)]
        def l1_body(tb):
            for bi, (n0, ncols) in enumerate(nblocks):
                wb, b_wb = cm.load_w(wh, 0, 16, n0, ncols)
                for t in range(NTP):
                    ps, b_ps = cm.ps.next()
                    for kc in range(16):
                        fw.op("pe", lambda e, kc=kc, t=t, ps=ps, wb=wb, ncols=ncols: e.matmul(
                            ps[:, 0:ncols], lhsT=hT[:, kc, t * 128:(t + 1) * 128], rhs=wb[:, kc, 0:ncols],
                            start=(kc == 0), stop=(kc == 15)), reads=[b_hT, b_wb], writes=[b_ps])
                    if bi < 4:
                        isq = bi < 2
                        g_s, b_g = (qg_s, b_qg) if isq else (kg_s, b_kg)
                        gofs = (bi % 2) * 512
                        fw.op("act", lambda e, ps=ps: e.activation(out=sqs[:], in_=ps[:], func=AF.Square),
                              reads=[b_ps], writes=[b_sqs])
                        ss, b_ss = cm.st.next()
                        fw.op("dve", lambda e, ss=ss: e.tensor_reduce(out=ss[:, 0:4], in_=sqs[:].rearrange("p (h d) -> p h d", h=4),
                                                                     axis=AX.X, op=ALU.add), reads=[b_sqs], writes=[b_ss])
                        if isq:
                            rs, b_rs = cm.rstd(ss[:, 0:4], b_ss, 4, 1.0, 128 * EPS)
                        else:
                            rs, b_rs = cm.rstd(ss[:, 0:4], b_ss, 4, 1.0 / 128, EPS)
                        o, b_o, s_o = ob.next()
                        for h in range(4):
                            fw.op("dve", lambda e, h=h, ps=ps, rs=rs, o=o, g_s=g_s, gofs=gofs: e.scalar_tensor_tensor(
                                out=o[:, h * 128:(h + 1) * 128], in0=ps[:, h * 128:(h + 1) * 128], scalar=rs[:, h:h + 1],
                                in1=g_s[:, gofs + h * 128:gofs + (h + 1) * 128], op0=ALU.mult, op1=ALU.mult),
                                reads=[b_ps, b_rs, b_g], writes=[b_o])
                        outs.append(fw.dma("sp", oh[(tb + t) * 128:(tb + t + 1) * 128, n0:n0 + 512], o[:], s_o, reads=[b_o]))
                    elif bi < 10:
                        o, b_o, s_o = ob.next()
                        if t % 2 == 0:
                            fw.op("act", lambda e, ps=ps, o=o: e.activation(out=o[:], in_=ps[:], func=AF.Copy),
                                  reads=[b_ps], writes=[b_o])
                        else:
                            fw.op("dve", lambda e, ps=ps, o=o: e.tensor_copy(out=o[:], in_=ps[:]), reads=[b_ps], writes=[b_o])
                        outs.append(fw.dma("sp", oh[(tb + t) * 128:(tb + t + 1) * 128, n0:n0 + 512], o[:], s_o, reads=[b_o]))
                    else:
                        o, b_o, s_o = oab.next()
                        fw.op("dve", lambda e, ps=ps, o=o: e.tensor_copy(out=o[:], in_=ps[:, 0:16]), reads=[b_ps], writes=[b_o])
                        outs.append(fw.dma("sp", oa[(tb + t) * 128:(tb + t + 1) * 128, :], o[:], s_o, reads=[b_o]))
        for p0 in range(0, TOK, PT):
          tb = p0 // 128
          for t in range(NTP):
            xt, b_xt, s_xt = cm.xr.next()
            fw.dma("sp", xt[:], x[(tb + t) * 128:(tb + t + 1) * 128, :], s_xt, writes=[b_xt])
            cm.norm_T(xt[:], b_xt, gn_s, b_gn, hT, b_hT, t * 128)
          l1_body(tb)
        fw.wait("sp", ob.final_tokens() + oab.final_tokens())
        fw.emit()
    return nc


def l2_consts():
    i = np.arange(128)
    c = {}
    c["idn"] = np.eye(128, dtype=np.float32).astype(NPBF)
    c["mU"] = (i[:, None] <= i[None, :]).astype(np.float32).astype(NPBF)
    c["Uc"] = ((i[:, None] <= i[None, :]) * (-1.0 / 16)).astype(np.float32)
    c["U2"] = ((i[:, None] > i[None, :]) * (-1.0 / 16)).astype(np.float32)
    c["Lpn"] = (-(i[:, None] >= i[None, :]).astype(np.float32)).astype(NPBF)
    c["On"] = (-np.ones((128, 128), np.float32)).astype(NPBF)
    m4 = np.zeros((4, 128, 512), np.float32)
    tri = (i[:, None] < i[None, :]).astype(np.float32)
    for r in range(4):
        for qb in range(4):
            if qb == r:
                m4[r][:, qb * 128:(qb + 1) * 128] = tri
            elif qb > r:
                m4[r][:, qb * 128:(qb + 1) * 128] = 1.0
    c["M4"] = np.ascontiguousarray(m4.transpose(1, 0, 2)).astype(NPBF)
    return c


def build_l2(T):
    NCH = T // 128
    NG = T // 512
    GC = min(8, NCH)
    GLA_EVERY = max(1, (NG * (2 * NG + 2)) // NCH)
    nc = bass.Bass("TRN2", target_bir_lowering=False)
    di = lambda n, s, dt: nc.dram_tensor(n, s, dt, kind="ExternalInput").ap()
    sqT = di("sqT", [128, T], BF16); skT = di("skT", [128, T], BF16); sv = di("sv", [T, 128], BF16)
    gqT = di("gqT", [128, T], BF16); gkT = di("gkT", [128, T], BF16)
    gk = di("gk", [T, 128], BF16); gv = di("gv", [T, 128], BF16)
    ga = di("ga", [17, T], F32); wa = di("wa", [17, 128], F32)
    idn = di("idn", [128, 128], BF16); mU = di("mU", [128, 128], BF16)
    Uc = di("Uc", [128, 128], F32); U2 = di("U2", [128, 128], F32)
    Lpn = di("Lpn", [128, 128], BF16); On = di("On", [128, 128], BF16); M4 = di("M4", [128, 4, 512], BF16)
    sbT = nc.dram_tensor("sbT", [128, T], BF16, kind="ExternalOutput").ap()
    go = nc.dram_tensor("go", [T, 128], BF16, kind="ExternalOutput").ap()
    with contextlib.ExitStack() as st:
        fw = FW(nc, st)
        s_const = fw.new_sem("s_const")

        const_bufs = []

        def const(name, src, shape, dt):
            t = fw.sbuf(name, shape, dt); b = Buf()
            fw.dma("sp", t[:], src, s_const, writes=[b])
            const_bufs.append(b)
            return t, b
        mU_s, b_mU = const("mU_s", mU, [128, 128], BF16)
        Uc_s, b_Uc = const("Uc_s", Uc, [128, 128], F32)
        U2_s, b_U2 = const("U2_s", U2, [128, 128], F32)
        Lpn_s, b_Lpn = const("Lpn_s", Lpn, [128, 128], BF16)
        On_s, b_On = const("On_s", On, [128, 128], BF16)
        M4_s, b_M4 = const("M4_s", M4, [128, 4, 512], BF16)
        wa_s, b_wa = const("wa_s", wa, [17, 128], F32)
        sqT_s, b_sqT = const("sqT_s", sqT, [128, T], BF16)
        skT_s, b_skT = const("skT_s", skT, [128, T], BF16)
        sv_s, b_sv = const("sv_s", sv.rearrange("(c p) d -> p c d", p=128), [128, NCH, 128], BF16)
        for b in const_bufs:
            b.w = ("d", s_const, s_const.count)
        ps = Rot(fw, "ps", [128, 512], F32, 6, psum=True)
        pso = Rot(fw, "pso", [128, 512], F32, 2, psum=True)

        gq_r = Rot(fw, "gq_r", [128, GC * 128], BF16, 2, sem=True)
        gkT_r = Rot(fw, "gkT_r", [128, GC * 128], BF16, 2, sem=True)
        gk_r = Rot(fw, "gk_r", [128, GC, 128], BF16, 2, sem=True)
        gv_r = Rot(fw, "gv_r", [128, GC, 128], BF16, 2, sem=True)
        ga_r = Rot(fw, "ga_r", [17, GC * 128], F32, 2, sem=True)
        f32t = Rot(fw, "f32t", [128, 128], F32, 6)
        bft = Rot(fw, "bft", [128, 128], BF16, 12)
        gob = Rot(fw, "gob", [128, 128], BF16, 3, sem=True)
        dec_r = Rot(fw, "dec_r", [128, 1], F32, 3)
        S32 = fw.sbuf("S32", [128, 128], F32); b_S32 = Buf()
        Sbf = Rot(fw, "Sbf", [128, 128], BF16, 2)
        fw.op("dve", lambda e: e.memset(S32[:], 0.0), writes=[b_S32])
        sb_cur, b_sb_cur = Sbf.next()
        fw.op("dve", lambda e, t=sb_cur: e.memset(t[:], 0.0), writes=[b_sb_cur])
        outs = []
        gst = dict(grp=None, sb_cur=sb_cur, b_sb_cur=b_sb_cur, next=0, pairs=0)

        def gla_chunk(c):
            grp = gst["grp"]; sb_cur = gst["sb_cur"]; b_sb_cur = gst["b_sb_cur"]
            if c % GC == 0:
                g0 = c * 128
                tq, bq, sq_ = gq_r.next(); fw.dma("sp", tq[:], gqT[:, g0:g0 + GC * 128], sq_, writes=[bq])
                tk, bk, sk_ = gkT_r.next(); fw.dma("sp", tk[:], gkT[:, g0:g0 + GC * 128], sk_, writes=[bk])
                tkk, bkk, skk = gk_r.next(); fw.dma("sp", tkk[:], gk[g0:g0 + GC * 128, :].rearrange("(c p) d -> p c d", p=128), skk, writes=[bkk])
                tv, bv, sv_ = gv_r.next(); fw.dma("sp", tv[:], gv[g0:g0 + GC * 128, :].rearrange("(c p) d -> p c d", p=128), sv_, writes=[bv])
                ta, ba, sa_ = ga_r.next(); fw.dma("sp", ta[:], ga[:, g0:g0 + GC * 128], sa_, writes=[ba])
                grp = (tq, bq, tk, bk, tkk, bkk, tv, bv, ta, ba)
            tq, bq, tk, bk, tkk, bkk, tv, bv, ta, ba = grp
            j = c % GC
            cs = slice(j * 128, (j + 1) * 128)
            pu, b_pu = ps.next()
            fw.op("pe", lambda e, pu=pu, ta=ta, cs=cs: e.matmul(pu[:, 0:128], lhsT=ta[:, cs], rhs=wa_s[:], start=True, stop=True),
                  reads=[ba, b_wa], writes=[b_pu])
            ex, b_ex = f32t.next()
            fw.op("act", lambda e, pu=pu, ex=ex: e.activation(out=ex[:], in_=pu[:, 0:128], func=AF.Exp, scale=-1.0),
                  reads=[b_pu], writes=[b_ex])
            spt, b_spt = f32t.next()
            fw.op("act", lambda e, ex=ex, spt=spt: e.activation(out=spt[:], in_=ex[:], func=AF.Ln, bias=1.0),
                  reads=[b_ex], writes=[b_spt])
            pb, b_pb = ps.next()
            fw.op("pe", lambda e, pb=pb, spt=spt: e.matmul(pb[:, 0:128], lhsT=spt[:], rhs=Uc_s[:], start=True, stop=True),
                  reads=[b_spt, b_Uc], writes=[b_pb])
            fw.op("pe", lambda e, pb=pb, spt=spt: e.matmul(pb[:, 128:256], lhsT=U2_s[:], rhs=spt[:], start=True, stop=True),
                  reads=[b_spt, b_U2], writes=[b_pb])
            e1, b_e1 = f32t.next()
            fw.op("act", lambda e, pb=pb, e1=e1: e.activation(out=e1[:], in_=pb[:, 0:128], func=AF.Exp),
                  reads=[b_pb], writes=[b_e1])
            e2, b_e2 = f32t.next()
            fw.op("act", lambda e, pb=pb, e2=e2: e.activation(out=e2[:], in_=pb[:, 0:128], func=AF.Exp, scale=-1.0),
                  reads=[b_pb], writes=[b_e2])
            e3, b_e3 = f32t.next()
            fw.op("act", lambda e, pb=pb, e3=e3: e.activation(out=e3[:], in_=pb[:, 128:256], func=AF.Exp),
                  reads=[b_pb], writes=[b_e3])
            dec, b_dec = dec_r.next()
            fw.op("dve", lambda e, dec=dec, e1=e1: e.tensor_copy(out=dec[:], in_=e1[:, 127:128]), reads=[b_e1], writes=[b_dec])
            qe, b_qe = bft.next()
            fw.op("dve", lambda e, qe=qe, tq=tq, cs=cs, e1=e1: e.scalar_tensor_tensor(
                out=qe[:], in0=tq[:, cs], scalar=float(128 ** -0.5), in1=e1[:], op0=ALU.mult, op1=ALU.mult),
                reads=[bq, b_e1], writes=[b_qe])
            ke, b_ke = bft.next()
            fw.op("dve", lambda e, ke=ke, tk=tk, cs=cs, e2=e2: e.tensor_tensor(out=ke[:], in0=tk[:, cs], in1=e2[:], op=ALU.mult),
                  reads=[bk, b_e2], writes=[b_ke])
            kd, b_kd = bft.next()
            fw.op("pool", lambda e, kd=kd, tkk=tkk, j=j, e3=e3: e.tensor_tensor(out=kd[:], in0=tkk[:, j, :], in1=e3[:], op=ALU.mult),
                  reads=[bkk, b_e3], writes=[b_kd])
            pS, b_pS = ps.next()
            fw.op("pe", lambda e, pS=pS, ke=ke, qe=qe: e.matmul(pS[:, 0:128], lhsT=ke[:], rhs=qe[:], start=True, stop=True),
                  reads=[b_ke, b_qe], writes=[b_pS])
            sTm, b_sTm = bft.next()
            fw.op("dve", lambda e, sTm=sTm, pS=pS: e.tensor_tensor(out=sTm[:], in0=pS[:, 0:128], in1=mU_s[:], op=ALU.mult),
                  reads=[b_pS, b_mU], writes=[b_sTm])
            pO, b_pO = ps.next()
            fw.op("pe", lambda e, pO=pO, sTm=sTm, tv=tv, j=j: e.matmul(pO[:, 0:128], lhsT=sTm[:], rhs=tv[:, j, :], start=True, stop=False),
                  reads=[b_sTm, bv], writes=[b_pO])
            fw.op("pe", lambda e, pO=pO, qe=qe, sbc=sb_cur: e.matmul(pO[:, 0:128], lhsT=qe[:], rhs=sbc[:], start=False, stop=True),
                  reads=[b_qe, b_sb_cur], writes=[b_pO])
            ot, b_ot, s_ot = gob.next()
            fw.op("act", lambda e, ot=ot, pO=pO: e.activation(out=ot[:], in_=pO[:, 0:128], func=AF.Copy), reads=[b_pO], writes=[b_ot])
            outs.append(fw.dma("sp", go[c * 128:(c + 1) * 128, :], ot[:], s_ot, reads=[b_ot]))
            pK, b_pK = ps.next()
            fw.op("pe", lambda e, pK=pK, kd=kd, tv=tv, j=j: e.matmul(pK[:, 0:128], lhsT=kd[:], rhs=tv[:, j, :], start=True, stop=True),
                  reads=[b_kd, bv], writes=[b_pK])
            fw.op("dve", lambda e, pK=pK, dec=dec: e.scalar_tensor_tensor(out=S32[:], in0=S32[:], scalar=dec[:, 0:1], in1=pK[:, 0:128],
                                                                       op0=ALU.mult, op1=ALU.add),
                  reads=[b_pK, b_dec, b_S32], writes=[b_S32])
            sb_cur, b_sb_cur = Sbf.next()
            fw.op("dve", lambda e, t=sb_cur: e.tensor_copy(out=t[:], in_=S32[:]), reads=[b_S32], writes=[b_sb_cur])

            gst["grp"] = grp; gst["sb_cur"] = sb_cur; gst["b_sb_cur"] = b_sb_cur

        def gla_step():
            if gst["next"] < NCH:
                gla_chunk(gst["next"])
                gst["next"] += 1

        e32 = Rot(fw, "e32", [128, 512], F32, 2)
        spb = Rot(fw, "spb", [128, 512], BF16, 4)
        Ab = Rot(fw, "Ab", [128, 512], BF16, 4)
        SS = fw.sbuf("SS", [128, 512], F32); b_SS = Buf()
        SSb = Rot(fw, "SSb", [128, 512], BF16, 4)
        osb = Rot(fw, "osb", [128, 512], BF16, 2, sem=True)
        for g in range(NG):
            qs = slice(g * 512, (g + 1) * 512)
            kbs = list(range(4 * g + 3, -1, -1))
            pOa, b_pOa = pso.next()
            nkb = len(kbs)
            stA = {}

            def stageA(i, kb):
                ks = slice(kb * 128, (kb + 1) * 128)
                pz, b_pz = ps.next()
                fw.op("pe", lambda e, pz=pz, ks=ks, qs=qs: e.matmul(pz[:], lhsT=skT_s[:, ks], rhs=sqT_s[:, qs], start=True, stop=True),
                      reads=[b_skT, b_sqT], writes=[b_pz])
                ee, b_ee = e32.next()
                fw.op("act", lambda e, pz=pz, ee=ee: e.activation(out=ee[:], in_=pz[:], func=AF.Exp), reads=[b_pz], writes=[b_ee])
                sp, b_sp = spb.next()
                fw.op("act", lambda e, ee=ee, sp=sp: e.activation(out=sp[:], in_=ee[:], func=AF.Ln, bias=1.0), reads=[b_ee], writes=[b_sp])
                r = kb - 4 * g
                if r >= 0:
                    fw.op("dve", lambda e, sp=sp, r=r: e.tensor_tensor(out=sp[:], in0=sp[:], in1=M4_s[:, r, :], op=ALU.mult),
                          reads=[b_sp, b_M4], writes=[b_sp])
                if i == 0:
                    sprev = None
                    fw.op("pool", lambda e, sp=sp: e.tensor_copy(out=SS[:], in_=sp[:]), reads=[b_sp], writes=[b_SS])
                else:
                    sprev = stA[i - 1]["snext"]
                    fw.op("pool", lambda e, sp=sp: e.tensor_tensor(out=SS[:], in0=SS[:], in1=sp[:], op=ALU.add),
                          reads=[b_sp, b_SS], writes=[b_SS])
                sn, b_sn = SSb.next()
                fw.op("dve", lambda e, sn=sn: e.tensor_copy(out=sn[:], in_=SS[:]), reads=[b_SS], writes=[b_sn])
                stA[i] = dict(ks=ks, sp=(sp, b_sp), sprev=sprev, snext=(sn, b_sn), r=r, kb=kb)

            def stageB(i):
                d = stA[i]
                ks = d["ks"]; sp, b_sp = d["sp"]
                pl, b_pl = ps.next()
                last = d["sprev"] is None
                fw.op("pe", lambda e, pl=pl, ks=ks, qs=qs: e.matmul(pl[:], lhsT=skT_s[:, ks], rhs=sqT_s[:, qs], start=True, stop=False),
                      reads=[b_skT, b_sqT], writes=[b_pl])
                fw.op("pe", lambda e, pl=pl, sp=sp, last=last: e.matmul(pl[:], lhsT=Lpn_s[:], rhs=sp[:], start=False, stop=last),
                      reads=[b_Lpn, b_sp], writes=[b_pl])
                if not last:
                    sv_, b_sv_ = d["sprev"]
                    fw.op("pe", lambda e, pl=pl, sv_=sv_: e.matmul(pl[:], lhsT=On_s[:], rhs=sv_[:], start=False, stop=True),
                          reads=[b_On, b_sv_], writes=[b_pl])
                A, b_A = Ab.next()
                fw.op("act", lambda e, pl=pl, A=A: e.activation(out=A[:], in_=pl[:], func=AF.Exp), reads=[b_pl], writes=[b_A])
                if d["r"] >= 0:
                    fw.op("dve", lambda e, A=A, r=d["r"]: e.tensor_tensor(out=A[:], in0=A[:], in1=M4_s[:, r, :], op=ALU.mult),
                          reads=[b_A, b_M4], writes=[b_A])
                d["A"] = (A, b_A)
                del stA[i]["sp"]

            def stageC(i):
                d = stA[i]
                A, b_A = d["A"]
                kb = d["kb"]
                fw.op("pe", lambda e, A=A, kb=kb, i=i, pOa=pOa, nkb=nkb: e.matmul(pOa[:], lhsT=sv_s[:, kb, :], rhs=A[:], start=(i == 0), stop=(i == nkb - 1)),
                      reads=[b_sv, b_A], writes=[b_pOa])
                del stA[i]["A"]

            stageA(0, kbs[0])
            stageA(1, kbs[1])
            stageB(0)
            for i in range(nkb):
                if i + 2 < nkb:
                    stageA(i + 2, kbs[i + 2])
                if gst["pairs"] % GLA_EVERY == 0:
                    gla_step()
                gst["pairs"] += 1
                if i + 1 < nkb:
                    stageB(i + 1)
                stageC(i)
            o, b_o, s_o = osb.next()
            fw.op("act", lambda e, o=o, pOa=pOa: e.activation(out=o[:], in_=pOa[:], func=AF.Copy), reads=[b_pOa], writes=[b_o])
            outs.append(fw.dma("sp", sbT[:, qs], o[:], s_o, reads=[b_o]))
        while gst["next"] < NCH:
            gla_step()
        fw.wait("sp", gob.final_tokens() + osb.final_tokens())
        fw.emit()
    return nc


def build_l3(TOK):
    NB = TOK // TB
    NTB = TB // 128
    nc = bass.Bass("TRN2", target_bir_lowering=False)
    di = lambda n, s, dt: nc.dram_tensor(n, s, dt, kind="ExternalInput").ap()
    x = di("x", [TOK, D], F32)
    sbT = di("sbT", [1024, TOK], BF16)
    go = di("go", [TOK, 1024], BF16)
    mem = di("mem", [NMEM, D], F32)
    wl = di("wl", [D, NL], F32)
    wbr = [di(n, [1024, D], F32) for n in ("wbg", "wbs", "wbm")]
    wkv = di("wkv", [D, 2048], F32)
    wo = di("wo", [D, D], F32)
    wgu = di("wgu", [D, 2 * DFF], F32)
    wd = di("wd", [DFF, D], F32)
    an = di("an", [128, 16], F32); fn = di("fn", [128, 16], F32); mn = di("mn", [128, 16], F32)
    gon = di("gon", [128, 1024], F32)
    mqn = di("mqn", [128, 2], F32); mkn = di("mkn", [128, 2], F32)
    idn = di("idn", [128, 128], BF16)
    y = nc.dram_tensor("y", [TOK, D], F32, kind="ExternalOutput").ap()
    with contextlib.ExitStack() as st:
        fw = FW(nc, st)
        cm = Common(fw, nc)

        def const(name, src, shape, dt):
            t = fw.sbuf(name, shape, dt); b = Buf()
            cm.load_const(t[:], src, b)
            return t, b
        cm.load_const(cm.ident[:], idn, cm.b_ident)
        an_s, b_an = const("an_s", an, [128, 16], F32)
        fn_s, b_fn = const("fn_s", fn, [128, 16], F32)
        mn_s, b_mn = const("mn_s", mn, [128, 16], F32)
        gon_s, b_gon = const("gon_s", gon, [128, 1024], F32)
        mqn_s, b_mqn = const("mqn_s", mqn, [128, 2], F32)
        mkn_s, b_mkn = const("mkn_s", mkn, [128, 2], F32)
        cm.finish_consts()

        if PRECAST:
            wl = cm.precast("wl", wl, D, NL)
            wbr = [cm.precast(n, w, 1024, D) for n, w in zip(("wbg", "wbs", "wbm"), wbr)]
            wo = cm.precast("wo", wo, D, D)
            wgu = cm.precast("wgu", wgu, D, 2 * DFF)
            wd = cm.precast("wd", wd, DFF, D)
        big = fw.sbuf("big", [128, 48, TB], BF16)
        b_big = [Buf() for _ in range(48)]
        hT = fw.sbuf("hT", [128, 16, TB], BF16); b_hT = Buf()
        mT = fw.sbuf("mT", [128, 16, TB], BF16); b_mT = [Buf() for _ in range(16)]
        ogT = fw.sbuf("ogT", [128, 8, TB], BF16); b_ogT = Buf()
        omT = fw.sbuf("omT", [128, 8, TB], BF16); b_omT = Buf()
        sbTs = fw.sbuf("sbTs", [128, 8, TB], BF16); b_sbTs = Buf(); s_sbTs = fw.new_sem("s_sbTs")
        sgr = fw.sbuf("sgr", [128, NTB, 1024], BF16); b_sgr = [Buf() for _ in range(NTB)]
        qm = fw.sbuf("qm", [128, NTB, 1024], BF16); b_qm = [Buf() for _ in range(NTB)]
        x1s = fw.sbuf("x1s", [128, NTB, D], F32); b_x1s = [Buf() for _ in range(NTB)]
        mkT = fw.sbuf("mkT", [128, 4, 2, NMEM], BF16); b_mkT = Buf()
        mv = fw.sbuf("mv", [128, 2, 1024], BF16); b_mv = Buf()
        mktm = fw.sbuf("mktm", [128, 2, 1024], BF16); b_mktm = Buf()
        sq32 = fw.sbuf("sq32", [128, 1024], F32); b_sq32 = Buf()
        t32 = Rot(fw, "t32", [128, 512], F32, 4)
        gor = Rot(fw, "gor", [128, 1024], BF16, 2, sem=True)
        ogb = Rot(fw, "ogb", [128, 1024], BF16, 2)
        og32, b_og32 = sq32, b_sq32
        qmT = fw.sbuf("qmT", [128, 8, 128], BF16); b_qmT = Buf()
        pbf = Rot(fw, "pbf", [128, 256], BF16, 2)
        pTs = fw.sbuf("pTs", [128, 2, 128], BF16); b_pTs = Buf()
        ys = Rot(fw, "ys", [128, 512], F32, 2, sem=True)
        mg32 = fw.sbuf("mg32", [128, 16, TB], BF16); b_mg = [Buf() for _ in range(16)]

        def mm_tm(lhs_fn, b_lhs, nk, wb, b_wb, ncols, ps, b_ps, first=True, last=True):
            for kc in range(nk):
                fw.op("pe", lambda e, kc=kc: e.matmul(ps[:, 0:ncols], lhsT=lhs_fn(kc), rhs=wb[:, kc, 0:ncols],
                                                      start=(first and kc == 0), stop=(last and kc == nk - 1)),
                      reads=list(b_lhs) + [b_wb], writes=[b_ps])

        def mm_fm(wb, b_wb, c0, nk, rhs_fn, b_rhs, ps, b_ps):
            for kc in range(nk):
                fw.op("pe", lambda e, kc=kc: e.matmul(ps[:, 0:TB], lhsT=wb[:, kc, c0:c0 + 128], rhs=rhs_fn(kc),
                                                      start=(kc == 0), stop=(kc == nk - 1)),
                      reads=list(b_rhs) + [b_wb], writes=[b_ps])

        def head_rstd(src_ap, b_src, nh, dh, scale, bias):
            fw.op("act", lambda e: e.activation(out=sq32[:, 0:nh * dh], in_=src_ap, func=AF.Square),
                  reads=[b_src], writes=[b_sq32])
            ss, b_ss = cm.st.next()
            fw.op("dve", lambda e: e.tensor_reduce(out=ss[:, 0:nh], in_=sq32[:, 0:nh * dh].rearrange("p (h d) -> p h d", h=nh),
                                                   axis=AX.X, op=ALU.add), reads=[b_sq32], writes=[b_ss])
            return cm.rstd(ss[:, 0:nh], b_ss, nh, scale, bias)

        memT = big
        for mt in range(2):
            xt, b_xt, s_xt = cm.xr.next()
            fw.dma("sp", xt[:], mem[mt * 128:(mt + 1) * 128, :], s_xt, writes=[b_xt])
            cm.norm_T(xt[:], b_xt, mn_s, b_mn, memT, b_big[0], mt * 128)
        for nb in range(4):
            wb, b_wb = cm.load_w(wkv, 0, 16, nb * 512, 512)
            for mt in range(2):
                ps, b_ps = cm.ps.next()
                mm_tm(lambda kc, mt=mt: memT[:, kc, mt * 128:(mt + 1) * 128], [b_big[0]], 16, wb, b_wb, 512, ps, b_ps)
                if nb < 2:
                    rs, b_rs = head_rstd(ps[:], b_ps, 2, 256, 1.0 / 256, EPS)
                    for h in range(2):
                        fw.op("dve", lambda e, h=h, ps=ps, rs=rs, mt=mt, nb=nb: e.tensor_scalar(
                            out=mktm[:, mt, nb * 512 + h * 256: nb * 512 + (h + 1) * 256], in0=ps[:, h * 256:(h + 1) * 256],
                            scalar1=rs[:, h:h + 1], scalar2=None, op0=ALU.mult), reads=[b_ps, b_rs], writes=[b_mktm])
                else:
                    fw.op("act", lambda e, ps=ps, mt=mt, nb=nb: e.activation(out=mv[:, mt, (nb - 2) * 512:(nb - 1) * 512], in_=ps[:], func=AF.Copy),
                          reads=[b_ps], writes=[b_mv])
        for mt in range(2):
            cm.transposes(mktm[:, mt, :], b_mktm, 8,
                          lambda c, mt=mt: mkT[:, c // 2, c % 2, mt * 128:(mt + 1) * 128], b_mkT,
                          lambda c: mkn_s[:, (c % 2):(c % 2) + 1], b_mkn)

        outs = []
        for blk in range(NB):
            t0 = blk * TB
            for i in range(NTB):
                xt, b_xt, s_xt = cm.xr.next()
                fw.dma("sp", xt[:], x[t0 + i * 128: t0 + (i + 1) * 128, :], s_xt, writes=[b_xt])
                cm.norm_T(xt[:], b_xt, an_s, b_an, hT, b_hT, i * 128)
            fw.dma("sp", sbTs[:], sbT[:, t0:t0 + TB].rearrange("(c p) t -> p c t", p=128), s_sbTs, writes=[b_sbTs])
            for nb in range(4):
                wb, b_wb = cm.load_w(wl, 0, 16, nb * 512, 512)
                for i in range(NTB):
                    ps, b_ps = cm.ps.next()
                    mm_tm(lambda kc, i=i: hT[:, kc, i * 128:(i + 1) * 128], [b_hT], 16, wb, b_wb, 512, ps, b_ps)
                    if nb < 2:
                        fw.op("act", lambda e, ps=ps, i=i, nb=nb: e.activation(out=sgr[:, i, nb * 512:(nb + 1) * 512], in_=ps[:], func=AF.Silu),
                              reads=[b_ps], writes=[b_sgr[i]])
                    else:
                        rs, b_rs = head_rstd(ps[:], b_ps, 2, 256, 1.0, 256 * EPS)
                        for h in range(2):
                            fw.op("dve", lambda e, h=h, ps=ps, rs=rs, i=i, nb=nb: e.tensor_scalar(
                                out=qm[:, i, (nb - 2) * 512 + h * 256:(nb - 2) * 512 + (h + 1) * 256], in0=ps[:, h * 256:(h + 1) * 256],
                                scalar1=rs[:, h:h + 1], scalar2=None, op0=ALU.mult), reads=[b_ps, b_rs], writes=[b_qm[i]])
            for nb in range(12):
                wb, b_wb = cm.load_w(wl, 0, 16, 2048 + nb * 512, 512)
                for s in range(4):
                    ch = nb * 4 + s
                    ps, b_ps = cm.ps.next()
                    mm_fm(wb, b_wb, s * 128, 16, lambda kc: hT[:, kc, :], [b_hT], ps, b_ps)
                    fw.op("act", lambda e, ps=ps, ch=ch: e.activation(out=big[:, ch, :], in_=ps[:, 0:TB], func=AF.Sigmoid),
                          reads=[b_ps], writes=[b_big[ch]])
            for i in range(NTB):
                gt, b_gt, s_gt = gor.next()
                fw.dma("sp", gt[:], go[t0 + i * 128:t0 + (i + 1) * 128, :], s_gt, writes=[b_gt])
                rs, b_rs = head_rstd(gt[:], b_gt, 4, 256, 1.0 / 256, EPS)
                for h in range(4):
                    hs = slice(h * 256, (h + 1) * 256)
                    fw.op("dve", lambda e, h=h, hs=hs, gt=gt, rs=rs: e.scalar_tensor_tensor(
                        out=og32[:, hs], in0=gt[:, hs], scalar=rs[:, h:h + 1], in1=gon_s[:, hs], op0=ALU.mult, op1=ALU.mult),
                        reads=[b_gt, b_rs, b_gon], writes=[b_og32])
                ob_, b_ob = ogb.next()
                fw.op("dve", lambda e, ob_=ob_, i=i: e.tensor_tensor(out=ob_[:], in0=og32[:], in1=sgr[:, i, :], op=ALU.mult),
                      reads=[b_og32, b_sgr[i]], writes=[b_ob])
                cm.transposes(ob_, b_ob, 8, lambda c, i=i: ogT[:, c, i * 128:(i + 1) * 128], b_ogT)
            for i in range(NTB):
                cm.transposes(qm[:, i, :], b_qm[i], 8, lambda c: qmT[:, c, :], b_qmT,
                              lambda c: mqn_s[:, (c % 2):(c % 2) + 1], b_mqn)
                for h in range(4):
                    ps, b_ps = cm.ps.next()
                    for c2 in range(2):
                        fw.op("pe", lambda e, ps=ps, h=h, c2=c2: e.matmul(ps[:, 0:NMEM], lhsT=qmT[:, h * 2 + c2, :], rhs=mkT[:, h, c2, :],
                                                                         start=(c2 == 0), stop=(c2 == 1)),
                              reads=[b_qmT, b_mkT], writes=[b_ps])
                    mx, b_mx = cm.st.next()
                    fw.op("dve", lambda e, ps=ps, mx=mx: e.tensor_reduce(out=mx[:, 0:1], in_=ps[:, 0:NMEM], axis=AX.X, op=ALU.max),
                          reads=[b_ps], writes=[b_mx])
                    fw.op("dve", lambda e, mx=mx: e.tensor_scalar(out=mx[:, 1:2], in0=mx[:, 0:1], scalar1=-1.0, scalar2=None, op0=ALU.mult),
                          reads=[b_mx], writes=[b_mx])
                    pe_, b_pe = t32.next()
                    fw.op("act", lambda e, ps=ps, mx=mx, pe_=pe_: e.activation(out=pe_[:, 0:NMEM], in_=ps[:, 0:NMEM], func=AF.Exp,
                                                                             bias=mx[:, 1:2], accum_out=mx[:, 2:3]),
                          reads=[b_ps, b_mx], writes=[b_pe, b_mx])
                    fw.op("dve", lambda e, mx=mx: e.reciprocal(out=mx[:, 3:4], in_=mx[:, 2:3]), reads=[b_mx], writes=[b_mx])
                    pb_, b_pb = pbf.next()
                    fw.op("dve", lambda e, pb_=pb_, pe_=pe_, mx=mx: e.tensor_scalar(out=pb_[:], in0=pe_[:, 0:NMEM], scalar1=mx[:, 3:4],
                                                                                  scalar2=None, op0=ALU.mult),
                          reads=[b_pe, b_mx], writes=[b_pb])
                    cm.transposes(pb_, b_pb, 2, lambda c: pTs[:, c, :], b_pTs)
                    for c2 in range(2):
                        ps2, b_ps2 = cm.ps.next()
                        for mc in range(2):
                            fw.op("pe", lambda e, ps2=ps2, h=h, c2=c2, mc=mc: e.matmul(
                                ps2[:, 0:128], lhsT=mv[:, mc, h * 256 + c2 * 128: h * 256 + (c2 + 1) * 128], rhs=pTs[:, mc, :],
                                start=(mc == 0), stop=(mc == 1)), reads=[b_mv, b_pTs], writes=[b_ps2])
                        fw.op("act", lambda e, ps2=ps2, h=h, c2=c2, i=i: e.activation(out=omT[:, h * 2 + c2, i * 128:(i + 1) * 128],
                                                                                    in_=ps2[:, 0:128], func=AF.Copy),
                              reads=[b_ps2], writes=[b_omT])
            srcs = [(ogT, b_ogT), (sbTs, b_sbTs), (omT, b_omT)]
            for br in range(3):
                src, b_src = srcs[br]
                for nb in range(4):
                    wb, b_wb = cm.load_w(wbr[br], 0, 8, nb * 512, 512)
                    for s in range(4):
                        ncn = nb * 4 + s
                        gch = br * 16 + ncn
                        ps, b_ps = cm.ps.next()
                        mm_fm(wb, b_wb, s * 128, 8, lambda kc, src=src: src[:, kc, :], [b_src], ps, b_ps)
                        if br == 0:
                            fw.op("dve", lambda e, ps=ps, ncn=ncn, gch=gch: e.tensor_tensor(out=mg32[:, ncn, :], in0=ps[:, 0:TB], in1=big[:, gch, :], op=ALU.mult),
                                  reads=[b_ps, b_big[gch]], writes=[b_mg[ncn]])
                        else:
                            tt, b_tt = t32.next()
                            fw.op("dve", lambda e, ps=ps, tt=tt, gch=gch: e.tensor_tensor(out=tt[:, 0:TB], in0=ps[:, 0:TB], in1=big[:, gch, :], op=ALU.mult),
                                  reads=[b_ps, b_big[gch]], writes=[b_tt])
                            if br == 1:
                                fw.op("pool", lambda e, tt=tt, ncn=ncn: e.tensor_tensor(out=mg32[:, ncn, :], in0=mg32[:, ncn, :], in1=tt[:, 0:TB], op=ALU.add),
                                      reads=[b_tt, b_mg[ncn]], writes=[b_mg[ncn]])
                            else:
                                fw.op("pool", lambda e, tt=tt, ncn=ncn: e.tensor_tensor(out=mT[:, ncn, :], in0=mg32[:, ncn, :], in1=tt[:, 0:TB], op=ALU.add),
                                      reads=[b_tt, b_mg[ncn]], writes=[b_mT[ncn]])
            for nb in range(4):
                wb, b_wb = cm.load_w(wo, 0, 16, nb * 512, 512)
                for i in range(NTB):
                    ps, b_ps = cm.ps.next()
                    mm_tm(lambda kc, i=i: mT[:, kc, i * 128:(i + 1) * 128], b_mT, 16, wb, b_wb, 512, ps, b_ps)
                    xt, b_xt, s_xt = ys.next()
                    fw.dma("sp", xt[:], x[t0 + i * 128:t0 + (i + 1) * 128, nb * 512:(nb + 1) * 512], s_xt, writes=[b_xt])
                    fw.op("dve", lambda e, ps=ps, xt=xt, i=i, nb=nb: e.tensor_tensor(out=x1s[:, i, nb * 512:(nb + 1) * 512], in0=ps[:], in1=xt[:], op=ALU.add),
                          reads=[b_ps, b_xt], writes=[b_x1s[i]])
            for i in range(NTB):
                cm.norm_T(x1s[:, i, :], b_x1s[i], fn_s, b_fn, hT, b_hT, i * 128)
            for nb in range(11):
                wg_, b_wg = cm.load_w(wgu, 0, 16, nb * 512, 512)
                sgs = []
                for s in range(4):
                    ps, b_ps = cm.ps.next()
                    mm_fm(wg_, b_wg, s * 128, 16, lambda kc: hT[:, kc, :], [b_hT], ps, b_ps)
                    sg, b_sg = t32.next()
                    fw.op("act", lambda e, ps=ps, sg=sg: e.activation(out=sg[:, 0:TB], in_=ps[:, 0:TB], func=AF.Silu), reads=[b_ps], writes=[b_sg])
                    sgs.append((sg, b_sg))
                wu_, b_wu = cm.load_w(wgu, 0, 16, DFF + nb * 512, 512)
                for s in range(4):
                    ch = nb * 4 + s
                    ps, b_ps = cm.ps.next()
                    mm_fm(wu_, b_wu, s * 128, 16, lambda kc: hT[:, kc, :], [b_hT], ps, b_ps)
                    sg, b_sg = sgs[s]
                    fw.op("dve", lambda e, ps=ps, sg=sg, ch=ch: e.tensor_tensor(out=big[:, ch, :], in0=ps[:, 0:TB], in1=sg[:, 0:TB], op=ALU.mult),
                          reads=[b_ps, b_sg], writes=[b_big[ch]])
            kgs = [(0, 16), (16, 16), (32, 12)]
            for nb in range(4):
                pss = [cm.ps.next() for _ in range(NTB)]
                for gi, (k0, nk) in enumerate(kgs):
                    wb, b_wb = cm.load_w(wd, k0, nk, nb * 512, 512)
                    for i in range(NTB):
                        ps, b_ps = pss[i]
                        mm_tm(lambda kc, i=i, k0=k0: big[:, k0 + kc, i * 128:(i + 1) * 128], b_big[k0:k0 + nk], nk, wb, b_wb, 512, ps, b_ps,
                              first=(gi == 0), last=(gi == 2))
                for i in range(NTB):
                    ps, b_ps = pss[i]
                    yt, b_yt, s_yt = ys.next()
                    fw.op("dve", lambda e, ps=ps, yt=yt, i=i, nb=nb: e.tensor_tensor(out=yt[:], in0=ps[:], in1=x1s[:, i, nb * 512:(nb + 1) * 512], op=ALU.add),
                          reads=[b_ps, b_x1s[i]], writes=[b_yt])
                    outs.append(fw.dma("sp", y[t0 + i * 128:t0 + (i + 1) * 128, nb * 512:(nb + 1) * 512], yt[:], s_yt, reads=[b_yt]))
        fw.wait("sp", ys.final_tokens())
        fw.emit()
    return nc


_CACHE = {}


def _get(name, fn, *a):
    k = (name,) + a
    if k not in _CACHE:
        _CACHE[k] = fn(*a)
    return _CACHE[k]


def _fm(g):
    return np.ascontiguousarray(np.asarray(g, np.float32).reshape(-1, 128).T)


def _bc(g, rep):
    return np.ascontiguousarray(np.broadcast_to(np.tile(np.asarray(g, np.float32), rep)[None, :], (128, g.shape[0] * rep)))


def run_l1(x2, wh, attn_norm, sbq, sbk):
    T = x2.shape[0]
    TOK = T // K1
    nc = _get("l1", build_l1, TOK)
    idn = np.eye(128, dtype=np.float32).astype(NPBF)
    ims = [dict(x=x2[c * TOK:(c + 1) * TOK], wh=wh, gn=_fm(attn_norm), qg=_bc(sbq, 8), kg=_bc(sbk, 8), idn=idn) for c in range(K1)]
    res = run_bass_kernel_spmd(nc, ims, core_ids=list(range(K1))).results
    oh = np.concatenate([r["oh"] for r in res], 0)
    oa = np.concatenate([r["oa"] for r in res], 0)
    return oh, oa


def run_l2(oh, oa, w_a2, b_a):
    T = oh.shape[0]
    nc = _get("l2", build_l2, T)
    cs = l2_consts()
    sq, sk, sv = oh[:, 0:1024], oh[:, 1024:2048], oh[:, 2048:3072]
    gq, gk, gv = oh[:, 3072:3584], oh[:, 3584:4096], oh[:, 4096:5120]
    ga = np.ascontiguousarray(np.concatenate([oa.T, np.ones((1, T), np.float32)], 0))
    ims = []
    for c in range(8):
        hg, half = c // 2, c % 2
        hs = slice(c * 128, (c + 1) * 128)
        gs = slice(hg * 128, (hg + 1) * 128)
        vs = slice(hg * 256 + half * 128, hg * 256 + (half + 1) * 128)
        wa = np.ascontiguousarray(np.concatenate([w_a2[:, gs], b_a[None, gs]], 0).astype(np.float32))
        ims.append(dict(sqT=np.ascontiguousarray(sq[:, hs].T), skT=np.ascontiguousarray(sk[:, hs].T), sv=np.ascontiguousarray(sv[:, hs]),
                        gqT=np.ascontiguousarray(gq[:, gs].T), gkT=np.ascontiguousarray(gk[:, gs].T),
                        gk=np.ascontiguousarray(gk[:, gs]), gv=np.ascontiguousarray(gv[:, vs]), ga=ga, wa=wa, **cs))
    res = run_bass_kernel_spmd(nc, ims, core_ids=list(range(8))).results
    sbT = np.concatenate([r["sbT"] for r in res], 0)
    go = np.concatenate([r["go"] for r in res], 1)
    return sbT, go


def run_l3(x2, sbT, go, mem2, p):
    T = x2.shape[0]
    TOK = T // K3
    nc = _get("l3", build_l3, TOK)
    idn = np.eye(128, dtype=np.float32).astype(NPBF)
    ims = []
    for c in range(K3):
        ts = slice(c * TOK, (c + 1) * TOK)
        ims.append(dict(x=x2[ts], sbT=np.ascontiguousarray(sbT[:, ts]), go=np.ascontiguousarray(go[ts]), mem=mem2,
                        wl=p["wl"], wbg=p["wbg"], wbs=p["wbs"], wbm=p["wbm"], wkv=p["wkv"], wo=p["wo"], wgu=p["wgu"], wd=p["wd"],
                        an=_fm(p["attn_norm"]), fn=_fm(p["ffn_norm"]), mn=_fm(p["mem_norm"]), gon=_bc(p["gla_out_norm"], 4),
                        mqn=np.ascontiguousarray(p["mem_q_norm"].reshape(2, 128).T), mkn=np.ascontiguousarray(p["mem_k_norm"].reshape(2, 128).T),
                        idn=idn))
    res = run_bass_kernel_spmd(nc, ims, core_ids=list(range(K3))).results
    return np.concatenate([r["y"] for r in res], 0)


def split_w_in(w):
    gq, gk, gv, gr = w[:, 0:512], w[:, 512:1024], w[:, 1024:2048], w[:, 2048:3072]
    ga1 = w[:, 3072:3088]
    sq, sk, sv = w[:, 3088:4112], w[:, 4112:5136], w[:, 5136:6160]
    mq, gates = w[:, 6160:7184], w[:, 7184:13328]
    wh = np.ascontiguousarray(np.concatenate([sq, sk, sv, gq, gk, gv, ga1], 1))
    wl = np.ascontiguousarray(np.concatenate([gr, mq, gates], 1))
    return wh, wl


def kernel(x, mem, attn_norm, w_in, gla_w_a2, gla_b_a, gla_out_norm, w_br_gla,
           sb_q_norm, sb_k_norm, w_br_sb, mem_norm, w_mem_kv, mem_q_norm, mem_k_norm,
           w_br_mem, w_o, ffn_norm, w_gate_up, w_down):
    f = lambda a: np.asarray(a, np.float32)
    x2 = np.ascontiguousarray(f(x)[0])
    mem2 = np.ascontiguousarray(f(mem)[0])
    for l in range(w_in.shape[0]):
        wh, wl = split_w_in(f(w_in[l]))
        oh, oa = run_l1(x2, wh, f(attn_norm[l]), f(sb_q_norm[l]), f(sb_k_norm[l]))
        sbT, go = run_l2(oh, oa, f(gla_w_a2[l]), f(gla_b_a[l]))
        p = dict(wl=wl, wbg=f(w_br_gla[l]), wbs=f(w_br_sb[l]), wbm=f(w_br_mem[l]), wkv=f(w_mem_kv[l]), wo=f(w_o[l]),
                 wgu=f(w_gate_up[l]), wd=f(w_down[l]), attn_norm=f(attn_norm[l]), ffn_norm=f(ffn_norm[l]),
                 mem_norm=f(mem_norm[l]), gla_out_norm=f(gla_out_norm[l]), mem_q_norm=f(mem_q_norm[l]),
                 mem_k_norm=f(mem_k_norm[l]))
        x2 = run_l3(x2, sbT, go, mem2, p)
    return x2[None].astype(np.float32)
```

```python
import contextlib
import numpy as np
import ml_dtypes
import concourse.bass as bass
import concourse.mybir as mybir
from concourse.bass_utils import run_bass_kernel_spmd

F32 = mybir.dt.float32
BF16 = mybir.dt.bfloat16
AF = mybir.ActivationFunctionType
ALU = mybir.AluOpType
AX = mybir.AxisListType
NPBF = ml_dtypes.bfloat16

D = 2048
DFF = 5632
NMEM = 256
EPS = 1e-6
NH1 = 5136
NL = 8192
K1 = 8
K3 = 8
L1P = 2048
TB = 256
NWB = 3
PRECAST = True


class Sem:
    def __init__(self, h):
        self.h = h
        self.count = 0


class Buf:
    __slots__ = ("w", "r")

    def __init__(self):
        self.w = None
        self.r = []


class FW:
    ENGS = ("pe", "act", "dve", "pool", "sp")
    M = 8

    def __init__(self, nc, stack):
        self.nc = nc
        self.stack = stack
        self.q = {e: [] for e in self.ENGS}
        self.esem = {e: [self.new_sem(f"e_{e}{j}") for j in range(self.M)] for e in self.ENGS}
        self.eidx = {e: 0 for e in self.ENGS}
        self.seen_e = {e: {s: 0 for s in self.ENGS} for e in self.ENGS}
        self.seen_d = {e: {} for e in self.ENGS}
        self.n = 0

    def new_sem(self, name):
        return Sem(self.stack.enter_context(self.nc.semaphore(name)))

    def sbuf(self, name, shape, dt):
        return self.stack.enter_context(self.nc.sbuf_tensor(name, list(shape), dt))

    def psum(self, name, shape, dt):
        return self.stack.enter_context(self.nc.psum_tensor(name, list(shape), dt))

    def _waits(self, eng, deps):
        out = {}
        for t in deps:
            if t is None:
                continue
            kind, src, val = t
            if kind == "e":
                if src == "pe" and eng == "pe":
                    continue
                if self.seen_e[eng][src] >= val:
                    continue
                key = ("e", src)
                if out.get(key, 0) < val:
                    out[key] = val
            else:
                if self.seen_d[eng].get(src, 0) >= val:
                    continue
                key = ("d", src)
                if out.get(key, 0) < val:
                    out[key] = val
        res = []
        for (kind, src), val in out.items():
            if kind == "e":
                self.seen_e[eng][src] = val
                j = (val - 1) % self.M
                res.append((self.esem[src][j].h, (val - 1) // self.M + 1))
            else:
                self.seen_d[eng][src] = val
                res.append((src.h, val))
        return res

    @staticmethod
    def _deps(reads, writes, extra):
        deps = list(extra)
        for b in reads:
            deps.append(b.w)
        for b in writes:
            deps.append(b.w)
            deps.extend(b.r)
        return deps

    @staticmethod
    def _commit(tok, reads, writes):
        for b in reads:
            b.r.append(tok)
            if len(b.r) > 48:
                del b.r[:-48]
        for b in writes:
            b.w = tok
            b.r = []

    def op(self, eng, fn, reads=(), writes=(), deps=()):
        waits = self._waits(eng, self._deps(reads, writes, deps))
        self.eidx[eng] += 1
        idx = self.eidx[eng]
        tok = ("e", eng, idx)
        s = self.esem[eng][(idx - 1) % self.M]
        self.q[eng].append((waits, fn, (s.h, 1)))
        self.n += 1 + len(waits)
        self._commit(tok, reads, writes)
        return tok

    def dma(self, eng, out, in_, sem, reads=(), writes=(), deps=()):
        waits = self._waits(eng, self._deps(reads, writes, deps))
        sem.count += 16
        tok = ("d", sem, sem.count)
        self.q[eng].append((waits, lambda e: e.dma_start(out=out, in_=in_), (sem.h, 16)))
        self.n += 1 + len(waits)
        self._commit(tok, reads, writes)
        return tok

    def wait(self, eng, deps):
        waits = self._waits(eng, deps)
        if waits:
            self.q[eng].append((waits, None, None))

    def emit(self):
        q = self.q

        def run(e, lst):
            for waits, fn, inc in lst:
                for (h, v) in waits:
                    e.wait_ge(h, v)
                if fn is not None:
                    fn(e).then_inc(inc[0], inc[1])

        with self.nc.Block() as block:
            @block.tensor
            def _(e):
                run(e, q["pe"])

            @block.scalar
            def _(e):
                run(e, q["act"])

            @block.vector
            def _(e):
                run(e, q["dve"])

            @block.gpsimd
            def _(e):
                run(e, q["pool"])

            @block.sync
            def _(e):
                run(e, q["sp"])


class Rot:
    def __init__(self, fw, name, shape, dt, n, psum=False, sem=False):
        self.t = [(fw.psum if psum else fw.sbuf)(f"{name}{i}", shape, dt) for i in range(n)]
        self.b = [Buf() for _ in range(n)]
        self.s = [fw.new_sem(f"s_{name}{i}") for i in range(n)] if sem else None
        self.i = -1
        self.n = n

    def final_tokens(self):
        return [("d", sm, sm.count) for sm in (self.s or []) if sm.count > 0]

    def next(self):
        self.i = (self.i + 1) % self.n
        if self.s:
            return self.t[self.i], self.b[self.i], self.s[self.i]
        return self.t[self.i], self.b[self.i]


class Common:
    def __init__(self, fw, nc):
        self.fw = fw
        self.nc = nc
        self.ident = fw.sbuf("ident", [128, 128], BF16)
        self.b_ident = Buf()
        self.s_const = fw.new_sem("s_const")
        self.const_bufs = []
        self.xr = Rot(fw, "xr", [128, D], F32, 2, sem=True)
        self.hb = Rot(fw, "hb", [128, D], BF16, 2)
        self.junk = fw.sbuf("junk", [128, D], BF16)
        self.b_junk = Buf()
        self.st = Rot(fw, "st", [128, 8], F32, 4)
        self.ps = Rot(fw, "ps", [128, 512], F32, 6, psum=True)
        self.pt = Rot(fw, "pt", [128, 1024], BF16, 2, psum=True)
        self.wb = Rot(fw, "wb", [128, 16, 512], BF16, NWB, sem=True)

    def load_const(self, dst, src, b):
        self.fw.dma("sp", dst, src, self.s_const, writes=[b])
        self.const_bufs.append(b)

    def finish_consts(self):
        tok = ("d", self.s_const, self.s_const.count)
        for b in self.const_bufs:
            b.w = tok

    def rstd(self, ssq_ap, b_ssq, n, scale, bias):
        fw = self.fw
        st, b_st = self.st.next()
        fw.op("act", lambda e: e.activation(out=st[:, 0:n], in_=ssq_ap, func=AF.Sqrt, scale=scale, bias=bias),
              reads=[b_ssq], writes=[b_st])
        fw.op("dve", lambda e: e.reciprocal(out=st[:, 0:n], in_=st[:, 0:n]), reads=[b_st], writes=[b_st])
        return st, b_st

    def norm_T(self, src, b_src, gain_fm, b_gain, dstT, b_dst, col0):
        fw = self.fw
        ss, b_ss = self.st.next()
        fw.op("act", lambda e: e.activation(out=self.junk[:], in_=src, func=AF.Square, accum_out=ss[:, 0:1]),
              reads=[b_src], writes=[self.b_junk, b_ss])
        rs, b_rs = self.rstd(ss[:, 0:1], b_ss, 1, 1.0 / D, EPS)
        hb, b_hb = self.hb.next()
        fw.op("dve", lambda e: e.tensor_scalar(out=hb[:], in0=src, scalar1=rs[:, 0:1], scalar2=None, op0=ALU.mult),
              reads=[b_src, b_rs], writes=[b_hb])
        self.transposes(hb, b_hb, D // 128, lambda c: dstT[:, c, col0:col0 + 128], b_dst,
                        lambda c: gain_fm[:, c:c + 1], b_gain)

    def transposes(self, src, b_src, nch, dst_fn, b_dst, scale_fn=None, b_scale=None):
        fw = self.fw
        for c0 in range(0, nch, 8):
            pt, b_pt = self.pt.next()
            m = min(8, nch - c0)
            for j in range(m):
                c = c0 + j
                fw.op("pe", lambda e, c=c, j=j, pt=pt: e.transpose(out=pt[:, j * 128:(j + 1) * 128],
                                                                 in_=src[:, c * 128:(c + 1) * 128],
                                                                 identity=self.ident[:]),
                      reads=[b_src, self.b_ident], writes=[b_pt])
            for j in range(m):
                c = c0 + j
                eng = "dve" if j % 2 == 0 else "act"
                rd = [b_pt] + ([b_scale] if b_scale is not None else [])
                if scale_fn is None:
                    if eng == "dve":
                        fw.op("dve", lambda e, c=c, j=j, pt=pt: e.tensor_copy(out=dst_fn(c), in_=pt[:, j * 128:(j + 1) * 128]),
                              reads=rd, writes=[b_dst])
                    else:
                        fw.op("act", lambda e, c=c, j=j, pt=pt: e.activation(out=dst_fn(c), in_=pt[:, j * 128:(j + 1) * 128], func=AF.Copy),
                              reads=rd, writes=[b_dst])
                else:
                    if eng == "dve":
                        fw.op("dve", lambda e, c=c, j=j, pt=pt: e.tensor_scalar(out=dst_fn(c), in0=pt[:, j * 128:(j + 1) * 128],
                                                                               scalar1=scale_fn(c), scalar2=None, op0=ALU.mult),
                              reads=rd, writes=[b_dst])
                    else:
                        fw.op("act", lambda e, c=c, j=j, pt=pt: e.activation(out=dst_fn(c), in_=pt[:, j * 128:(j + 1) * 128],
                                                                            func=AF.Copy, scale=scale_fn(c)),
                              reads=rd, writes=[b_dst])

    def load_w(self, w_ap, k0, nk, n0, ncols):
        wb, b_wb, s_wb = self.wb.next()
        if isinstance(w_ap, tuple):
            t, b_w = w_ap
            assert n0 % 512 == 0 and ncols == 512
            src = t[n0 // 512][:, k0:k0 + nk, :]
            self.fw.dma("pool", wb[:, 0:nk, 0:ncols], src, s_wb, reads=[b_w], writes=[b_wb])
        else:
            src = w_ap[k0 * 128:(k0 + nk) * 128, n0:n0 + ncols].rearrange("(c p) n -> p c n", p=128)
            self.fw.dma("pool", wb[:, 0:nk, 0:ncols], src, s_wb, writes=[b_wb])
        return wb, b_wb

    def precast(self, name, w_ap, K, N):
        KC, NBK = K // 128, N // 512
        t = self.nc.dram_tensor(name + "_bf", [NBK, 128, KC, 512], BF16)
        b = Buf()
        sem = self.fw.new_sem("s_pc_" + name)
        tok = None
        for nb in range(NBK):
            src = w_ap[:, nb * 512:(nb + 1) * 512].rearrange("(c p) n -> p c n", p=128)
            tok = self.fw.dma("pool", t[nb], src, sem)
        b.w = tok
        return (t, b)


def build_l1(TOK):
    PT = min(TOK, L1P)
    NTP = PT // 128
    nc = bass.Bass("TRN2", target_bir_lowering=False)
    x = nc.dram_tensor("x", [TOK, D], F32, kind="ExternalInput").ap()
    wh = nc.dram_tensor("wh", [D, NH1], F32, kind="ExternalInput").ap()
    gn = nc.dram_tensor("gn", [128, 16], F32, kind="ExternalInput").ap()
    qg = nc.dram_tensor("qg", [128, 1024], F32, kind="ExternalInput").ap()
    kg = nc.dram_tensor("kg", [128, 1024], F32, kind="ExternalInput").ap()
    idn = nc.dram_tensor("idn", [128, 128], BF16, kind="ExternalInput").ap()
    oh = nc.dram_tensor("oh", [TOK, 5120], BF16, kind="ExternalOutput").ap()
    oa = nc.dram_tensor("oa", [TOK, 16], F32, kind="ExternalOutput").ap()
    with contextlib.ExitStack() as st:
        fw = FW(nc, st)
        cm = Common(fw, nc)
        gn_s = fw.sbuf("gn_s", [128, 16], F32); b_gn = Buf()
        qg_s = fw.sbuf("qg_s", [128, 1024], F32); b_qg = Buf()
        kg_s = fw.sbuf("kg_s", [128, 1024], F32); b_kg = Buf()
        hT = fw.sbuf("hT", [128, 16, PT], BF16); b_hT = Buf()
        sqs = fw.sbuf("sqs", [128, 512], F32); b_sqs = Buf()
        ob = Rot(fw, "ob", [128, 512], BF16, 3, sem=True)
        oab = Rot(fw, "oab", [128, 16], F32, 2, sem=True)
        cm.load_const(cm.ident[:], idn, cm.b_ident)
        cm.load_const(gn_s[:], gn, b_gn)
        cm.load_const(qg_s[:], qg, b_qg)
        cm.load_const(kg_s[:], kg, b_kg)
        cm.finish_consts()
        outs = []
        nblocks = [(i * 512, 512) for i in range(10)] + [(5120, 16)]
        def l1_body(tb):
            for bi, (n0, ncols) in enumerate(nblocks):
                wb, b_wb = cm.load_w(wh, 0, 16, n0, ncols)
                for t in range(NTP):
                    ps, b_ps = cm.ps.next()
                    for kc in range(16):
                        fw.op("pe", lambda e, kc=kc, t=t, ps=ps, wb=wb, ncols=ncols: e.matmul(
                            ps[:, 0:ncols], lhsT=hT[:, kc, t * 128:(t + 1) * 128], rhs=wb[:, kc, 0:ncols],
                            start=(kc == 0), stop=(kc == 15)), reads=[b_hT, b_wb], writes=[b_ps])
                    if bi < 4:
                        isq = bi < 2
                        g_s, b_g = (qg_s, b_qg) if isq else (kg_s, b_kg)
                        gofs = (bi % 2) * 512
                        fw.op("act", lambda e, ps=ps: e.activation(out=sqs[:], in_=ps[:], func=AF.Square),
                              reads=[b_ps], writes=[b_sqs])
                        ss, b_ss = cm.st.next()
                        fw.op("dve", lambda e, ss=ss: e.tensor_reduce(out=ss[:, 0:4], in_=sqs[:].rearrange("p (h d) -> p h d", h=4),
                                                                     axis=AX.X, op=ALU.add), reads=[b_sqs], writes=[b_ss])
                        if isq:
                            rs, b_rs = cm.rstd(ss[:, 0:4], b_ss, 4, 1.0, 128 * EPS)
                        else:
                            rs, b_rs = cm.rstd(ss[:, 0:4], b_ss, 4, 1.0 / 128, EPS)
                        o, b_o, s_o = ob.next()
                        for h in range(4):
                            fw.op("dve", lambda e, h=h, ps=ps, rs=rs, o=o, g_s=g_s, gofs=gofs: e.scalar_tensor_tensor(
                                out=o[:, h * 128:(h + 1) * 128], in0=ps[:, h * 128:(h + 1) * 128], scalar=rs[:, h:h + 1],
                                in1=g_s[:, gofs + h * 128:gofs + (h + 1) * 128], op0=ALU.mult, op1=ALU.mult),
                                reads=[b_ps, b_rs, b_g], writes=[b_o])
                        outs.append(fw.dma("sp", oh[(tb + t) * 128:(tb + t + 1) * 128, n0:n0 + 512], o[:], s_o, reads=[b_o]))
                    elif bi < 10:
                        o, b_o, s_o = ob.next()
                        if t % 2 == 0:
                            fw.op("act", lambda e, ps=ps, o=o: e.activation(out=o[:], in_=ps[:], func=AF.Copy),
                                  reads=[b_ps], writes=[b_o])
                        else:
                            fw.op("dve", lambda e, ps=ps, o=o: e.tensor_copy(out=o[:], in_=ps[:]), reads=[b_ps], writes=[b_o])
                        outs.append(fw.dma("sp", oh[(tb + t) * 128:(tb + t + 1) * 128, n0:n0 + 512], o[:], s_o, reads=[b_o]))
                    else:
                        o, b_o, s_o = oab.next()
                        fw.op("dve", lambda e, ps=ps, o=o: e.tensor_copy(out=o[:], in_=ps[:, 0:16]), reads=[b_ps], writes=[b_o])
                        outs.append(fw.dma("sp", oa[(tb + t) * 128:(tb + t + 1) * 128, :], o[:], s_o, reads=[b_o]))
        for p0 in range(0, TOK, PT):
          tb = p0 // 128
          for t in range(NTP):
            xt, b_xt, s_xt = cm.xr.next()
            fw.dma("sp", xt[:], x[(tb + t) * 128:(tb + t + 1) * 128, :], s_xt, writes=[b_xt])
            cm.norm_T(xt[:], b_xt, gn_s, b_gn, hT, b_hT, t * 128)
          l1_body(tb)
        fw.wait("sp", ob.final_tokens() + oab.final_tokens())
        fw.emit()
    return nc


def l2_consts():
    i = np.arange(128)
    c = {}
    c["idn"] = np.eye(128, dtype=np.float32).astype(NPBF)
    c["mU"] = (i[:, None] <= i[None, :]).astype(np.float32).astype(NPBF)
    c["Uc"] = ((i[:, None] <= i[None, :]) * (-1.0 / 16)).astype(np.float32)
    c["U2"] = ((i[:, None] > i[None, :]) * (-1.0 / 16)).astype(np.float32)
    c["Lpn"] = (-(i[:, None] >= i[None, :]).astype(np.float32)).astype(NPBF)
    c["On"] = (-np.ones((128, 128), np.float32)).astype(NPBF)
    m4 = np.zeros((4, 128, 512), np.float32)
    tri = (i[:, None] < i[None, :]).astype(np.float32)
    for r in range(4):
        for qb in range(4):
            if qb == r:
                m4[r][:, qb * 128:(qb + 1) * 128] = tri
            elif qb > r:
                m4[r][:, qb * 128:(qb + 1) * 128] = 1.0
    c["M4"] = np.ascontiguousarray(m4.transpose(1, 0, 2)).astype(NPBF)
    return c


def build_l2(T):
    NCH = T // 128
    NG = T // 512
    GC = min(8, NCH)
    GLA_EVERY = max(1, (NG * (2 * NG + 2)) // NCH)
    nc = bass.Bass("TRN2", target_bir_lowering=False)
    di = lambda n, s, dt: nc.dram_tensor(n, s, dt, kind="ExternalInput").ap()
    sqT = di("sqT", [128, T], BF16); skT = di("skT", [128, T], BF16); sv = di("sv", [T, 128], BF16)
    gqT = di("gqT", [128, T], BF16); gkT = di("gkT", [128, T], BF16)
    gk = di("gk", [T, 128], BF16); gv = di("gv", [T, 128], BF16)
    ga = di("ga", [17, T], F32); wa = di("wa", [17, 128], F32)
    idn = di("idn", [128, 128], BF16); mU = di("mU", [128, 128], BF16)
    Uc = di("Uc", [128, 128], F32); U2 = di("U2", [128, 128], F32)
    Lpn = di("Lpn", [128, 128], BF16); On = di("On", [128, 128], BF16); M4 = di("M4", [128, 4, 512], BF16)
    sbT = nc.dram_tensor("sbT", [128, T], BF16, kind="ExternalOutput").ap()
    go = nc.dram_tensor("go", [T, 128], BF16, kind="ExternalOutput").ap()
    with contextlib.ExitStack() as st:
        fw = FW(nc, st)
        s_const = fw.new_sem("s_const")

        const_bufs = []

        def const(name, src, shape, dt):
            t = fw.sbuf(name, shape, dt); b = Buf()
            fw.dma("sp", t[:], src, s_const, writes=[b])
            const_bufs.append(b)
            return t, b
        mU_s, b_mU = const("mU_s", mU, [128, 128], BF16)
        Uc_s, b_Uc = const("Uc_s", Uc, [128, 128], F32)
        U2_s, b_U2 = const("U2_s", U2, [128, 128], F32)
        Lpn_s, b_Lpn = const("Lpn_s", Lpn, [128, 128], BF16)
        On_s, b_On = const("On_s", On, [128, 128], BF16)
        M4_s, b_M4 = const("M4_s", M4, [128, 4, 512], BF16)
        wa_s, b_wa = const("wa_s", wa, [17, 128], F32)
        sqT_s, b_sqT = const("sqT_s", sqT, [128, T], BF16)
        skT_s, b_skT = const("skT_s", skT, [128, T], BF16)
        sv_s, b_sv = const("sv_s", sv.rearrange("(c p) d -> p c d", p=128), [128, NCH, 128], BF16)
        for b in const_bufs:
            b.w = ("d", s_const, s_const.count)
        ps = Rot(fw, "ps", [128, 512], F32, 2, psum=True)
        pzr = Rot(fw, "pzr", [128, 512], F32, 4, psum=True)
        pso = Rot(fw, "pso", [128, 512], F32, 2, psum=True)

        gq_r = Rot(fw, "gq_r", [128, GC * 128], BF16, 2, sem=True)
        gkT_r = Rot(fw, "gkT_r", [128, GC * 128], BF16, 2, sem=True)
        gk_r = Rot(fw, "gk_r", [128, GC, 128], BF16, 2, sem=True)
        gv_r = Rot(fw, "gv_r", [128, GC, 128], BF16, 2, sem=True)
        ga_r = Rot(fw, "ga_r", [17, GC * 128], F32, 2, sem=True)
        f32t = Rot(fw, "f32t", [128, 128], F32, 6)
        bft = Rot(fw, "bft", [128, 128], BF16, 12)
        gob = Rot(fw, "gob", [128, 128], BF16, 3, sem=True)
        dec_r = Rot(fw, "dec_r", [128, 1], F32, 3)
        S32 = fw.sbuf("S32", [128, 128], F32); b_S32 = Buf()
        Sbf = Rot(fw, "Sbf", [128, 128], BF16, 2)
        fw.op("dve", lambda e: e.memset(S32[:], 0.0), writes=[b_S32])
        sb_cur, b_sb_cur = Sbf.next()
        fw.op("dve", lambda e, t=sb_cur: e.memset(t[:], 0.0), writes=[b_sb_cur])
        outs = []
        gst = dict(grp=None, sb_cur=sb_cur, b_sb_cur=b_sb_cur, next=0, pairs=0)

        def gla_chunk(c):
            grp = gst["grp"]; sb_cur = gst["sb_cur"]; b_sb_cur = gst["b_sb_cur"]
            if c % GC == 0:
                g0 = c * 128
                tq, bq, sq_ = gq_r.next(); fw.dma("sp", tq[:], gqT[:, g0:g0 + GC * 128], sq_, writes=[bq])
                tk, bk, sk_ = gkT_r.next(); fw.dma("sp", tk[:], gkT[:, g0:g0 + GC * 128], sk_, writes=[bk])
                tkk, bkk, skk = gk_r.next(); fw.dma("sp", tkk[:], gk[g0:g0 + GC * 128, :].rearrange("(c p) d -> p c d", p=128), skk, writes=[bkk])
                tv, bv, sv_ = gv_r.next(); fw.dma("sp", tv[:], gv[g0:g0 + GC * 128, :].rearrange("(c p) d -> p c d", p=128), sv_, writes=[bv])
                ta, ba, sa_ = ga_r.next(); fw.dma("sp", ta[:], ga[:, g0:g0 + GC * 128], sa_, writes=[ba])
                grp = (tq, bq, tk, bk, tkk, bkk, tv, bv, ta, ba)
            tq, bq, tk, bk, tkk, bkk, tv, bv, ta, ba = grp
            j = c % GC
            cs = slice(j * 128, (j + 1) * 128)
            pu, b_pu = ps.next()
            fw.op("pe", lambda e, pu=pu, ta=ta, cs=cs: e.matmul(pu[:, 0:128], lhsT=ta[:, cs], rhs=wa_s[:], start=True, stop=True),
                  reads=[ba, b_wa], writes=[b_pu])
            ex, b_ex = f32t.next()
            fw.op("act", lambda e, pu=pu, ex=ex: e.activation(out=ex[:], in_=pu[:, 0:128], func=AF.Exp, scale=-1.0),
                  reads=[b_pu], writes=[b_ex])
            spt, b_spt = f32t.next()
            fw.op("act", lambda e, ex=ex, spt=spt: e.activation(out=spt[:], in_=ex[:], func=AF.Ln, bias=1.0),
                  reads=[b_ex], writes=[b_spt])
            pb, b_pb = ps.next()
            fw.op("pe", lambda e, pb=pb, spt=spt: e.matmul(pb[:, 0:128], lhsT=spt[:], rhs=Uc_s[:], start=True, stop=True),
                  reads=[b_spt, b_Uc], writes=[b_pb])
            fw.op("pe", lambda e, pb=pb, spt=spt: e.matmul(pb[:, 128:256], lhsT=U2_s[:], rhs=spt[:], start=True, stop=True),
                  reads=[b_spt, b_U2], writes=[b_pb])
            e1, b_e1 = f32t.next()
            fw.op("act", lambda e, pb=pb, e1=e1: e.activation(out=e1[:], in_=pb[:, 0:128], func=AF.Exp),
                  reads=[b_pb], writes=[b_e1])
            e2, b_e2 = f32t.next()
            fw.op("act", lambda e, pb=pb, e2=e2: e.activation(out=e2[:], in_=pb[:, 0:128], func=AF.Exp, scale=-1.0),
                  reads=[b_pb], writes=[b_e2])
            e3, b_e3 = f32t.next()
            fw.op("act", lambda e, pb=pb, e3=e3: e.activation(out=e3[:], in_=pb[:, 128:256], func=AF.Exp),
                  reads=[b_pb], writes=[b_e3])
            dec, b_dec = dec_r.next()
            fw.op("dve", lambda e, dec=dec, e1=e1: e.tensor_copy(out=dec[:], in_=e1[:, 127:128]), reads=[b_e1], writes=[b_dec])
            qe, b_qe = bft.next()
            fw.op("dve", lambda e, qe=qe, tq=tq, cs=cs, e1=e1: e.scalar_tensor_tensor(
                out=qe[:], in0=tq[:, cs], scalar=float(128 ** -0.5), in1=e1[:], op0=ALU.mult, op1=ALU.mult),
                reads=[bq, b_e1], writes=[b_qe])
            ke, b_ke = bft.next()
            fw.op("dve", lambda e, ke=ke, tk=tk, cs=cs, e2=e2: e.tensor_tensor(out=ke[:], in0=tk[:, cs], in1=e2[:], op=ALU.mult),
                  reads=[bk, b_e2], writes=[b_ke])
            kd, b_kd = bft.next()
            fw.op("pool", lambda e, kd=kd, tkk=tkk, j=j, e3=e3: e.tensor_tensor(out=kd[:], in0=tkk[:, j, :], in1=e3[:], op=ALU.mult),
                  reads=[bkk, b_e3], writes=[b_kd])
            pS, b_pS = ps.next()
            fw.op("pe", lambda e, pS=pS, ke=ke, qe=qe: e.matmul(pS[:, 0:128], lhsT=ke[:], rhs=qe[:], start=True, stop=True),
                  reads=[b_ke, b_qe], writes=[b_pS])
            sTm, b_sTm = bft.next()
            fw.op("dve", lambda e, sTm=sTm, pS=pS: e.tensor_tensor(out=sTm[:], in0=pS[:, 0:128], in1=mU_s[:], op=ALU.mult),
                  reads=[b_pS, b_mU], writes=[b_sTm])
            pO, b_pO = ps.next()
            fw.op("pe", lambda e, pO=pO, sTm=sTm, tv=tv, j=j: e.matmul(pO[:, 0:128], lhsT=sTm[:], rhs=tv[:, j, :], start=True, stop=False),
                  reads=[b_sTm, bv], writes=[b_pO])
            fw.op("pe", lambda e, pO=pO, qe=qe, sbc=sb_cur: e.matmul(pO[:, 0:128], lhsT=qe[:], rhs=sbc[:], start=False, stop=True),
                  reads=[b_qe, b_sb_cur], writes=[b_pO])
            ot, b_ot, s_ot = gob.next()
            fw.op("act", lambda e, ot=ot, pO=pO: e.activation(out=ot[:], in_=pO[:, 0:128], func=AF.Copy), reads=[b_pO], writes=[b_ot])
            outs.append(fw.dma("sp", go[c * 128:(c + 1) * 128, :], ot[:], s_ot, reads=[b_ot]))
            pK, b_pK = ps.next()
            fw.op("pe", lambda e, pK=pK, kd=kd, tv=tv, j=j: e.matmul(pK[:, 0:128], lhsT=kd[:], rhs=tv[:, j, :], start=True, stop=True),
                  reads=[b_kd, bv], writes=[b_pK])
            fw.op("dve", lambda e, pK=pK, dec=dec: e.scalar_tensor_tensor(out=S32[:], in0=S32[:], scalar=dec[:, 0:1], in1=pK[:, 0:128],
                                                                       op0=ALU.mult, op1=ALU.add),
                  reads=[b_pK, b_dec, b_S32], writes=[b_S32])
            sb_cur, b_sb_cur = Sbf.next()
            fw.op("dve", lambda e, t=sb_cur: e.tensor_copy(out=t[:], in_=S32[:]), reads=[b_S32], writes=[b_sb_cur])

            gst["grp"] = grp; gst["sb_cur"] = sb_cur; gst["b_sb_cur"] = b_sb_cur

        def gla_step():
            if gst["next"] < NCH:
                gla_chunk(gst["next"])
                gst["next"] += 1

        e32 = Rot(fw, "e32", [128, 512], F32, 2)
        spb = Rot(fw, "spb", [128, 512], BF16, 4)
        Ab = Rot(fw, "Ab", [128, 512], BF16, 4)
        SS = fw.sbuf("SS", [128, 512], F32); b_SS = Buf()
        SSb = Rot(fw, "SSb", [128, 512], BF16, 4)
        osb = Rot(fw, "osb", [128, 512], BF16, 2, sem=True)
        for g in range(NG):
            qs = slice(g * 512, (g + 1) * 512)
            kbs = list(range(4 * g + 3, -1, -1))
            pOa, b_pOa = pso.next()
            nkb = len(kbs)
            stA = {}

            def stageA(i, kb):
                ks = slice(kb * 128, (kb + 1) * 128)
                pz, b_pz = pzr.next()
                fw.op("pe", lambda e, pz=pz, ks=ks, qs=qs: e.matmul(pz[:], lhsT=skT_s[:, ks], rhs=sqT_s[:, qs], start=True, stop=True),
                      reads=[b_skT, b_sqT], writes=[b_pz])
                ee, b_ee = e32.next()
                fw.op("act", lambda e, pz=pz, ee=ee: e.activation(out=ee[:], in_=pz[:], func=AF.Exp), reads=[b_pz], writes=[b_ee])
                sp, b_sp = spb.next()
                fw.op("act", lambda e, ee=ee, sp=sp: e.activation(out=sp[:], in_=ee[:], func=AF.Ln, bias=1.0), reads=[b_ee], writes=[b_sp])
                r = kb - 4 * g
                if r >= 0:
                    fw.op("dve", lambda e, sp=sp, r=r: e.tensor_tensor(out=sp[:], in0=sp[:], in1=M4_s[:, r, :], op=ALU.mult),
                          reads=[b_sp, b_M4], writes=[b_sp])
                if i == 0:
                    sprev = None
                    fw.op("pool", lambda e, sp=sp: e.tensor_copy(out=SS[:], in_=sp[:]), reads=[b_sp], writes=[b_SS])
                else:
                    sprev = stA[i - 1]["snext"]
                    fw.op("pool", lambda e, sp=sp: e.tensor_tensor(out=SS[:], in0=SS[:], in1=sp[:], op=ALU.add),
                          reads=[b_sp, b_SS], writes=[b_SS])
                sn, b_sn = SSb.next()
                fw.op("dve", lambda e, sn=sn: e.tensor_copy(out=sn[:], in_=SS[:]), reads=[b_SS], writes=[b_sn])
                stA[i] = dict(ks=ks, sp=(sp, b_sp), sprev=sprev, snext=(sn, b_sn), r=r, kb=kb, pz=(pz, b_pz))

            def stageB(i):
                d = stA[i]
                ks = d["ks"]; sp, b_sp = d["sp"]
                pl, b_pl = d["pz"]
                last = d["sprev"] is None
                fw.op("pe", lambda e, pl=pl, sp=sp, last=last: e.matmul(pl[:], lhsT=Lpn_s[:], rhs=sp[:], start=False, stop=last),
                      reads=[b_Lpn, b_sp], writes=[b_pl])
                if not last:
                    sv_, b_sv_ = d["sprev"]
                    fw.op("pe", lambda e, pl=pl, sv_=sv_: e.matmul(pl[:], lhsT=On_s[:], rhs=sv_[:], start=False, stop=True),
                          reads=[b_On, b_sv_], writes=[b_pl])
                A, b_A = Ab.next()
                fw.op("act", lambda e, pl=pl, A=A: e.activation(out=A[:], in_=pl[:], func=AF.Exp), reads=[b_pl], writes=[b_A])
                if d["r"] >= 0:
                    fw.op("dve", lambda e, A=A, r=d["r"]: e.tensor_tensor(out=A[:], in0=A[:], in1=M4_s[:, r, :], op=ALU.mult),
                          reads=[b_A, b_M4], writes=[b_A])
                d["A"] = (A, b_A)
                del stA[i]["sp"]

            def stageC(i):
                d = stA[i]
                A, b_A = d["A"]
                kb = d["kb"]
                fw.op("pe", lambda e, A=A, kb=kb, i=i, pOa=pOa, nkb=nkb: e.matmul(pOa[:], lhsT=sv_s[:, kb, :], rhs=A[:], start=(i == 0), stop=(i == nkb - 1)),
                      reads=[b_sv, b_A], writes=[b_pOa])
                del stA[i]["A"]

            stageA(0, kbs[0])
            stageA(1, kbs[1])
            stageB(0)
            for i in range(nkb):
                if i + 2 < nkb:
                    stageA(i + 2, kbs[i + 2])
                if gst["pairs"] % GLA_EVERY == 0:
                    gla_step()
                gst["pairs"] += 1
                if i + 1 < nkb:
                    stageB(i + 1)
                stageC(i)
            o, b_o, s_o = osb.next()
            fw.op("act", lambda e, o=o, pOa=pOa: e.activation(out=o[:], in_=pOa[:], func=AF.Copy), reads=[b_pOa], writes=[b_o])
            outs.append(fw.dma("sp", sbT[:, qs], o[:], s_o, reads=[b_o]))
        while gst["next"] < NCH:
            gla_step()
        fw.wait("sp", gob.final_tokens() + osb.final_tokens())
        fw.emit()
    return nc


def build_l3(TOK):
    NB = TOK // TB
    NTB = TB // 128
    nc = bass.Bass("TRN2", target_bir_lowering=False)
    di = lambda n, s, dt: nc.dram_tensor(n, s, dt, kind="ExternalInput").ap()
    x = di("x", [TOK, D], F32)
    sbT = di("sbT", [1024, TOK], BF16)
    go = di("go", [TOK, 1024], BF16)
    mem = di("mem", [NMEM, D], F32)
    wl = di("wl", [D, NL], F32)
    wbr = [di(n, [1024, D], F32) for n in ("wbg", "wbs", "wbm")]
    wkv = di("wkv", [D, 2048], F32)
    wo = di("wo", [D, D], F32)
    wgu = di("wgu", [D, 2 * DFF], F32)
    wd = di("wd", [DFF, D], F32)
    an = di("an", [128, 16], F32); fn = di("fn", [128, 16], F32); mn = di("mn", [128, 16], F32)
    gon = di("gon", [128, 1024], F32)
    mqn = di("mqn", [128, 2], F32); mkn = di("mkn", [128, 2], F32)
    idn = di("idn", [128, 128], BF16)
    y = nc.dram_tensor("y", [TOK, D], F32, kind="ExternalOutput").ap()
    with contextlib.ExitStack() as st:
        fw = FW(nc, st)
        cm = Common(fw, nc)

        def const(name, src, shape, dt):
            t = fw.sbuf(name, shape, dt); b = Buf()
            cm.load_const(t[:], src, b)
            return t, b
        cm.load_const(cm.ident[:], idn, cm.b_ident)
        an_s, b_an = const("an_s", an, [128, 16], F32)
        fn_s, b_fn = const("fn_s", fn, [128, 16], F32)
        mn_s, b_mn = const("mn_s", mn, [128, 16], F32)
        gon_s, b_gon = const("gon_s", gon, [128, 1024], F32)
        mqn_s, b_mqn = const("mqn_s", mqn, [128, 2], F32)
        mkn_s, b_mkn = const("mkn_s", mkn, [128, 2], F32)
        cm.finish_consts()

        if PRECAST:
            wl = cm.precast("wl", wl, D, NL)
            wbr = [cm.precast(n, w, 1024, D) for n, w in zip(("wbg", "wbs", "wbm"), wbr)]
            wo = cm.precast("wo", wo, D, D)
            wgu = cm.precast("wgu", wgu, D, 2 * DFF)
            wd = cm.precast("wd", wd, DFF, D)
        big = fw.sbuf("big", [128, 48, TB], BF16)
        b_big = [Buf() for _ in range(48)]
        hT = fw.sbuf("hT", [128, 16, TB], BF16); b_hT = Buf()
        mT = fw.sbuf("mT", [128, 16, TB], BF16); b_mT = [Buf() for _ in range(16)]
        ogT = fw.sbuf("ogT", [128, 8, TB], BF16); b_ogT = Buf()
        omT = fw.sbuf("omT", [128, 8, TB], BF16); b_omT = Buf()
        sbTs = fw.sbuf("sbTs", [128, 8, TB], BF16); b_sbTs = Buf(); s_sbTs = fw.new_sem("s_sbTs")
        sgr = fw.sbuf("sgr", [128, NTB, 1024], BF16); b_sgr = [Buf() for _ in range(NTB)]
        qm = fw.sbuf("qm", [128, NTB, 1024], BF16); b_qm = [Buf() for _ in range(NTB)]
        x1s = fw.sbuf("x1s", [128, NTB, D], F32); b_x1s = [Buf() for _ in range(NTB)]
        mkT = fw.sbuf("mkT", [128, 4, 2, NMEM], BF16); b_mkT = Buf()
        mv = fw.sbuf("mv", [128, 2, 1024], BF16); b_mv = Buf()
        mktm = fw.sbuf("mktm", [128, 2, 1024], BF16); b_mktm = Buf()
        sq32 = fw.sbuf("sq32", [128, 1024], F32); b_sq32 = Buf()
        t32 = Rot(fw, "t32", [128, 512], F32, 4)
        gor = Rot(fw, "gor", [128, 1024], BF16, 2, sem=True)
        ogb = Rot(fw, "ogb", [128, 1024], BF16, 2)
        og32, b_og32 = sq32, b_sq32
        qmT = fw.sbuf("qmT", [128, 8, 128], BF16); b_qmT = Buf()
        pbf = Rot(fw, "pbf", [128, 256], BF16, 2)
        pTs = fw.sbuf("pTs", [128, 2, 128], BF16); b_pTs = Buf()
        ys = Rot(fw, "ys", [128, 512], F32, 2, sem=True)
        mg32 = fw.sbuf("mg32", [128, 16, TB], BF16); b_mg = [Buf() for _ in range(16)]

        def mm_tm(lhs_fn, b_lhs, nk, wb, b_wb, ncols, ps, b_ps, first=True, last=True):
            for kc in range(nk):
                fw.op("pe", lambda e, kc=kc: e.matmul(ps[:, 0:ncols], lhsT=lhs_fn(kc), rhs=wb[:, kc, 0:ncols],
                                                      start=(first and kc == 0), stop=(last and kc == nk - 1)),
                      reads=list(b_lhs) + [b_wb], writes=[b_ps])

        def mm_fm(wb, b_wb, c0, nk, rhs_fn, b_rhs, ps, b_ps):
            for kc in range(nk):
                fw.op("pe", lambda e, kc=kc: e.matmul(ps[:, 0:TB], lhsT=wb[:, kc, c0:c0 + 128], rhs=rhs_fn(kc),
                                                      start=(kc == 0), stop=(kc == nk - 1)),
                      reads=list(b_rhs) + [b_wb], writes=[b_ps])

        def head_rstd(src_ap, b_src, nh, dh, scale, bias):
            fw.op("act", lambda e: e.activation(out=sq32[:, 0:nh * dh], in_=src_ap, func=AF.Square),
                  reads=[b_src], writes=[b_sq32])
            ss, b_ss = cm.st.next()
            fw.op("dve", lambda e: e.tensor_reduce(out=ss[:, 0:nh], in_=sq32[:, 0:nh * dh].rearrange("p (h d) -> p h d", h=nh),
                                                   axis=AX.X, op=ALU.add), reads=[b_sq32], writes=[b_ss])
            return cm.rstd(ss[:, 0:nh], b_ss, nh, scale, bias)

        memT = big
        for mt in range(2):
            xt, b_xt, s_xt = cm.xr.next()
            fw.dma("sp", xt[:], mem[mt * 128:(mt + 1) * 128, :], s_xt, writes=[b_xt])
            cm.norm_T(xt[:], b_xt, mn_s, b_mn, memT, b_big[0], mt * 128)
        for nb in range(4):
            wb, b_wb = cm.load_w(wkv, 0, 16, nb * 512, 512)
            for mt in range(2):
                ps, b_ps = cm.ps.next()
                mm_tm(lambda kc, mt=mt: memT[:, kc, mt * 128:(mt + 1) * 128], [b_big[0]], 16, wb, b_wb, 512, ps, b_ps)
                if nb < 2:
                    rs, b_rs = head_rstd(ps[:], b_ps, 2, 256, 1.0 / 256, EPS)
                    for h in range(2):
                        fw.op("dve", lambda e, h=h, ps=ps, rs=rs, mt=mt, nb=nb: e.tensor_scalar(
                            out=mktm[:, mt, nb * 512 + h * 256: nb * 512 + (h + 1) * 256], in0=ps[:, h * 256:(h + 1) * 256],
                            scalar1=rs[:, h:h + 1], scalar2=None, op0=ALU.mult), reads=[b_ps, b_rs], writes=[b_mktm])
                else:
                    fw.op("act", lambda e, ps=ps, mt=mt, nb=nb: e.activation(out=mv[:, mt, (nb - 2) * 512:(nb - 1) * 512], in_=ps[:], func=AF.Copy),
                          reads=[b_ps], writes=[b_mv])
        for mt in range(2):
            cm.transposes(mktm[:, mt, :], b_mktm, 8,
                          lambda c, mt=mt: mkT[:, c // 2, c % 2, mt * 128:(mt + 1) * 128], b_mkT,
                          lambda c: mkn_s[:, (c % 2):(c % 2) + 1], b_mkn)

        outs = []
        for blk in range(NB):
            t0 = blk * TB
            for i in range(NTB):
                xt, b_xt, s_xt = cm.xr.next()
                fw.dma("sp", xt[:], x[t0 + i * 128: t0 + (i + 1) * 128, :], s_xt, writes=[b_xt])
                cm.norm_T(xt[:], b_xt, an_s, b_an, hT, b_hT, i * 128)
            fw.dma("sp", sbTs[:], sbT[:, t0:t0 + TB].rearrange("(c p) t -> p c t", p=128), s_sbTs, writes=[b_sbTs])
            for nb in range(4):
                wb, b_wb = cm.load_w(wl, 0, 16, nb * 512, 512)
                for i in range(NTB):
                    ps, b_ps = cm.ps.next()
                    mm_tm(lambda kc, i=i: hT[:, kc, i * 128:(i + 1) * 128], [b_hT], 16, wb, b_wb, 512, ps, b_ps)
                    if nb < 2:
                        fw.op("act", lambda e, ps=ps, i=i, nb=nb: e.activation(out=sgr[:, i, nb * 512:(nb + 1) * 512], in_=ps[:], func=AF.Silu),
                              reads=[b_ps], writes=[b_sgr[i]])
                    else:
                        rs, b_rs = head_rstd(ps[:], b_ps, 2, 256, 1.0, 256 * EPS)
                        for h in range(2):
                            fw.op("dve", lambda e, h=h, ps=ps, rs=rs, i=i, nb=nb: e.tensor_scalar(
                                out=qm[:, i, (nb - 2) * 512 + h * 256:(nb - 2) * 512 + (h + 1) * 256], in0=ps[:, h * 256:(h + 1) * 256],
                                scalar1=rs[:, h:h + 1], scalar2=None, op0=ALU.mult), reads=[b_ps, b_rs], writes=[b_qm[i]])
            for nb in range(12):
                wb, b_wb = cm.load_w(wl, 0, 16, 2048 + nb * 512, 512)
                for s in range(4):
                    ch = nb * 4 + s
                    ps, b_ps = cm.ps.next()
                    mm_fm(wb, b_wb, s * 128, 16, lambda kc: hT[:, kc, :], [b_hT], ps, b_ps)
                    fw.op("act", lambda e, ps=ps, ch=ch: e.activation(out=big[:, ch, :], in_=ps[:, 0:TB], func=AF.Sigmoid),
                          reads=[b_ps], writes=[b_big[ch]])
            for i in range(NTB):
                gt, b_gt, s_gt = gor.next()
                fw.dma("sp", gt[:], go[t0 + i * 128:t0 + (i + 1) * 128, :], s_gt, writes=[b_gt])
                rs, b_rs = head_rstd(gt[:], b_gt, 4, 256, 1.0 / 256, EPS)
                for h in range(4):
                    hs = slice(h * 256, (h + 1) * 256)
                    fw.op("dve", lambda e, h=h, hs=hs, gt=gt, rs=rs: e.scalar_tensor_tensor(
                        out=og32[:, hs], in0=gt[:, hs], scalar=rs[:, h:h + 1], in1=gon_s[:, hs], op0=ALU.mult, op1=ALU.mult),
                        reads=[b_gt, b_rs, b_gon], writes=[b_og32])
                ob_, b_ob = ogb.next()
                fw.op("dve", lambda e, ob_=ob_, i=i: e.tensor_tensor(out=ob_[:], in0=og32[:], in1=sgr[:, i, :], op=ALU.mult),
                      reads=[b_og32, b_sgr[i]], writes=[b_ob])
                cm.transposes(ob_, b_ob, 8, lambda c, i=i: ogT[:, c, i * 128:(i + 1) * 128], b_ogT)
            for i in range(NTB):
                cm.transposes(qm[:, i, :], b_qm[i], 8, lambda c: qmT[:, c, :], b_qmT,
                              lambda c: mqn_s[:, (c % 2):(c % 2) + 1], b_mqn)
                for h in range(4):
                    ps, b_ps = cm.ps.next()
                    for c2 in range(2):
                        fw.op("pe", lambda e, ps=ps, h=h, c2=c2: e.matmul(ps[:, 0:NMEM], lhsT=qmT[:, h * 2 + c2, :], rhs=mkT[:, h, c2, :],
                                                                         start=(c2 == 0), stop=(c2 == 1)),
                              reads=[b_qmT, b_mkT], writes=[b_ps])
                    mx, b_mx = cm.st.next()
                    fw.op("dve", lambda e, ps=ps, mx=mx: e.tensor_reduce(out=mx[:, 0:1], in_=ps[:, 0:NMEM], axis=AX.X, op=ALU.max),
                          reads=[b_ps], writes=[b_mx])
                    fw.op("dve", lambda e, mx=mx: e.tensor_scalar(out=mx[:, 1:2], in0=mx[:, 0:1], scalar1=-1.0, scalar2=None, op0=ALU.mult),
                          reads=[b_mx], writes=[b_mx])
                    pe_, b_pe = t32.next()
                    fw.op("act", lambda e, ps=ps, mx=mx, pe_=pe_: e.activation(out=pe_[:, 0:NMEM], in_=ps[:, 0:NMEM], func=AF.Exp,
                                                                             bias=mx[:, 1:2], accum_out=mx[:, 2:3]),
                          reads=[b_ps, b_mx], writes=[b_pe, b_mx])
                    fw.op("dve", lambda e, mx=mx: e.reciprocal(out=mx[:, 3:4], in_=mx[:, 2:3]), reads=[b_mx], writes=[b_mx])
                    pb_, b_pb = pbf.next()
                    fw.op("dve", lambda e, pb_=pb_, pe_=pe_, mx=mx: e.tensor_scalar(out=pb_[:], in0=pe_[:, 0:NMEM], scalar1=mx[:, 3:4],
                                                                                  scalar2=None, op0=ALU.mult),
                          reads=[b_pe, b_mx], writes=[b_pb])
                    cm.transposes(pb_, b_pb, 2, lambda c: pTs[:, c, :], b_pTs)
                    for c2 in range(2):
                        ps2, b_ps2 = cm.ps.next()
                        for mc in range(2):
                            fw.op("pe", lambda e, ps2=ps2, h=h, c2=c2, mc=mc: e.matmul(
                                ps2[:, 0:128], lhsT=mv[:, mc, h * 256 + c2 * 128: h * 256 + (c2 + 1) * 128], rhs=pTs[:, mc, :],
                                start=(mc == 0), stop=(mc == 1)), reads=[b_mv, b_pTs], writes=[b_ps2])
                        fw.op("act", lambda e, ps2=ps2, h=h, c2=c2, i=i: e.activation(out=omT[:, h * 2 + c2, i * 128:(i + 1) * 128],
                                                                                    in_=ps2[:, 0:128], func=AF.Copy),
                              reads=[b_ps2], writes=[b_omT])
            srcs = [(ogT, b_ogT), (sbTs, b_sbTs), (omT, b_omT)]
            for br in range(3):
                src, b_src = srcs[br]
                for nb in range(4):
                    wb, b_wb = cm.load_w(wbr[br], 0, 8, nb * 512, 512)
                    for s in range(4):
                        ncn = nb * 4 + s
                        gch = br * 16 + ncn
                        ps, b_ps = cm.ps.next()
                        mm_fm(wb, b_wb, s * 128, 8, lambda kc, src=src: src[:, kc, :], [b_src], ps, b_ps)
                        if br == 0:
                            fw.op("dve", lambda e, ps=ps, ncn=ncn, gch=gch: e.tensor_tensor(out=mg32[:, ncn, :], in0=ps[:, 0:TB], in1=big[:, gch, :], op=ALU.mult),
                                  reads=[b_ps, b_big[gch]], writes=[b_mg[ncn]])
                        else:
                            tt, b_tt = t32.next()
                            fw.op("dve", lambda e, ps=ps, tt=tt, gch=gch: e.tensor_tensor(out=tt[:, 0:TB], in0=ps[:, 0:TB], in1=big[:, gch, :], op=ALU.mult),
                                  reads=[b_ps, b_big[gch]], writes=[b_tt])
                            if br == 1:
                                fw.op("pool", lambda e, tt=tt, ncn=ncn: e.tensor_tensor(out=mg32[:, ncn, :], in0=mg32[:, ncn, :], in1=tt[:, 0:TB], op=ALU.add),
                                      reads=[b_tt, b_mg[ncn]], writes=[b_mg[ncn]])
                            else:
                                fw.op("pool", lambda e, tt=tt, ncn=ncn: e.tensor_tensor(out=mT[:, ncn, :], in0=mg32[:, ncn, :], in1=tt[:, 0:TB], op=ALU.add),
                                      reads=[b_tt, b_mg[ncn]], writes=[b_mT[ncn]])
            for nb in range(4):
                wb, b_wb = cm.load_w(wo, 0, 16, nb * 512, 512)
                for i in range(NTB):
                    ps, b_ps = cm.ps.next()
                    mm_tm(lambda kc, i=i: mT[:, kc, i * 128:(i + 1) * 128], b_mT, 16, wb, b_wb, 512, ps, b_ps)
                    xt, b_xt, s_xt = ys.next()
                    fw.dma("sp", xt[:], x[t0 + i * 128:t0 + (i + 1) * 128, nb * 512:(nb + 1) * 512], s_xt, writes=[b_xt])
                    fw.op("dve", lambda e, ps=ps, xt=xt, i=i, nb=nb: e.tensor_tensor(out=x1s[:, i, nb * 512:(nb + 1) * 512], in0=ps[:], in1=xt[:], op=ALU.add),
                          reads=[b_ps, b_xt], writes=[b_x1s[i]])
            for i in range(NTB):
                cm.norm_T(x1s[:, i, :], b_x1s[i], fn_s, b_fn, hT, b_hT, i * 128)
            for nb in range(11):
                wg_, b_wg = cm.load_w(wgu, 0, 16, nb * 512, 512)
                sgs = []
                for s in range(4):
                    ps, b_ps = cm.ps.next()
                    mm_fm(wg_, b_wg, s * 128, 16, lambda kc: hT[:, kc, :], [b_hT], ps, b_ps)
                    sg, b_sg = t32.next()
                    fw.op("act", lambda e, ps=ps, sg=sg: e.activation(out=sg[:, 0:TB], in_=ps[:, 0:TB], func=AF.Silu), reads=[b_ps], writes=[b_sg])
                    sgs.append((sg, b_sg))
                wu_, b_wu = cm.load_w(wgu, 0, 16, DFF + nb * 512, 512)
                for s in range(4):
                    ch = nb * 4 + s
                    ps, b_ps = cm.ps.next()
                    mm_fm(wu_, b_wu, s * 128, 16, lambda kc: hT[:, kc, :], [b_hT], ps, b_ps)
                    sg, b_sg = sgs[s]
                    fw.op("dve", lambda e, ps=ps, sg=sg, ch=ch: e.tensor_tensor(out=big[:, ch, :], in0=ps[:, 0:TB], in1=sg[:, 0:TB], op=ALU.mult),
                          reads=[b_ps, b_sg], writes=[b_big[ch]])
            kgs = [(0, 16), (16, 16), (32, 12)]
            for nb in range(4):
                pss = [cm.ps.next() for _ in range(NTB)]
                for gi, (k0, nk) in enumerate(kgs):
                    wb, b_wb = cm.load_w(wd, k0, nk, nb * 512, 512)
                    for i in range(NTB):
                        ps, b_ps = pss[i]
                        mm_tm(lambda kc, i=i, k0=k0: big[:, k0 + kc, i * 128:(i + 1) * 128], b_big[k0:k0 + nk], nk, wb, b_wb, 512, ps, b_ps,
                              first=(gi == 0), last=(gi == 2))
                for i in range(NTB):
                    ps, b_ps = pss[i]
                    yt, b_yt, s_yt = ys.next()
                    fw.op("dve", lambda e, ps=ps, yt=yt, i=i, nb=nb: e.tensor_tensor(out=yt[:], in0=ps[:], in1=x1s[:, i, nb * 512:(nb + 1) * 512], op=ALU.add),
                          reads=[b_ps, b_x1s[i]], writes=[b_yt])
                    outs.append(fw.dma("sp", y[t0 + i * 128:t0 + (i + 1) * 128, nb * 512:(nb + 1) * 512], yt[:], s_yt, reads=[b_yt]))
        fw.wait("sp", ys.final_tokens())
        fw.emit()
    return nc


_CACHE = {}


def _get(name, fn, *a):
    k = (name,) + a
    if k not in _CACHE:
        _CACHE[k] = fn(*a)
    return _CACHE[k]


def _fm(g):
    return np.ascontiguousarray(np.asarray(g, np.float32).reshape(-1, 128).T)


def _bc(g, rep):
    return np.ascontiguousarray(np.broadcast_to(np.tile(np.asarray(g, np.float32), rep)[None, :], (128, g.shape[0] * rep)))


def run_l1(x2, wh, attn_norm, sbq, sbk):
    T = x2.shape[0]
    TOK = T // K1
    nc = _get("l1", build_l1, TOK)
    idn = np.eye(128, dtype=np.float32).astype(NPBF)
    ims = [dict(x=x2[c * TOK:(c + 1) * TOK], wh=wh, gn=_fm(attn_norm), qg=_bc(sbq, 8), kg=_bc(sbk, 8), idn=idn) for c in range(K1)]
    res = run_bass_kernel_spmd(nc, ims, core_ids=list(range(K1))).results
    oh = np.concatenate([r["oh"] for r in res], 0)
    oa = np.concatenate([r["oa"] for r in res], 0)
    return oh, oa


def run_l2(oh, oa, w_a2, b_a):
    T = oh.shape[0]
    nc = _get("l2", build_l2, T)
    cs = l2_consts()
    sq, sk, sv = oh[:, 0:1024], oh[:, 1024:2048], oh[:, 2048:3072]
    gq, gk, gv = oh[:, 3072:3584], oh[:, 3584:4096], oh[:, 4096:5120]
    ga = np.ascontiguousarray(np.concatenate([oa.T, np.ones((1, T), np.float32)], 0))
    ims = []
    for c in range(8):
        hg, half = c // 2, c % 2
        hs = slice(c * 128, (c + 1) * 128)
        gs = slice(hg * 128, (hg + 1) * 128)
        vs = slice(hg * 256 + half * 128, hg * 256 + (half + 1) * 128)
        wa = np.ascontiguousarray(np.concatenate([w_a2[:, gs], b_a[None, gs]], 0).astype(np.float32))
        ims.append(dict(sqT=np.ascontiguousarray(sq[:, hs].T), skT=np.ascontiguousarray(sk[:, hs].T), sv=np.ascontiguousarray(sv[:, hs]),
                        gqT=np.ascontiguousarray(gq[:, gs].T), gkT=np.ascontiguousarray(gk[:, gs].T),
                        gk=np.ascontiguousarray(gk[:, gs]), gv=np.ascontiguousarray(gv[:, vs]), ga=ga, wa=wa, **cs))
    res = run_bass_kernel_spmd(nc, ims, core_ids=list(range(8))).results
    sbT = np.concatenate([r["sbT"] for r in res], 0)
    go = np.concatenate([r["go"] for r in res], 1)
    return sbT, go


def run_l3(x2, sbT, go, mem2, p):
    T = x2.shape[0]
    TOK = T // K3
    nc = _get("l3", build_l3, TOK)
    idn = np.eye(128, dtype=np.float32).astype(NPBF)
    ims = []
    for c in range(K3):
        ts = slice(c * TOK, (c + 1) * TOK)
        ims.append(dict(x=x2[ts], sbT=np.ascontiguousarray(sbT[:, ts]), go=np.ascontiguousarray(go[ts]), mem=mem2,
                        wl=p["wl"], wbg=p["wbg"], wbs=p["wbs"], wbm=p["wbm"], wkv=p["wkv"], wo=p["wo"], wgu=p["wgu"], wd=p["wd"],
                        an=_fm(p["attn_norm"]), fn=_fm(p["ffn_norm"]), mn=_fm(p["mem_norm"]), gon=_bc(p["gla_out_norm"], 4),
                        mqn=np.ascontiguousarray(p["mem_q_norm"].reshape(2, 128).T), mkn=np.ascontiguousarray(p["mem_k_norm"].reshape(2, 128).T),
                        idn=idn))
    res = run_bass_kernel_spmd(nc, ims, core_ids=list(range(K3))).results
    return np.concatenate([r["y"] for r in res], 0)


def split_w_in(w):
    gq, gk, gv, gr = w[:, 0:512], w[:, 512:1024], w[:, 1024:2048], w[:, 2048:3072]
    ga1 = w[:, 3072:3088]
    sq, sk, sv = w[:, 3088:4112], w[:, 4112:5136], w[:, 5136:6160]
    mq, gates = w[:, 6160:7184], w[:, 7184:13328]
    wh = np.ascontiguousarray(np.concatenate([sq, sk, sv, gq, gk, gv, ga1], 1))
    wl = np.ascontiguousarray(np.concatenate([gr, mq, gates], 1))
    return wh, wl


def kernel(x, mem, attn_norm, w_in, gla_w_a2, gla_b_a, gla_out_norm, w_br_gla,
           sb_q_norm, sb_k_norm, w_br_sb, mem_norm, w_mem_kv, mem_q_norm, mem_k_norm,
           w_br_mem, w_o, ffn_norm, w_gate_up, w_down):
    f = lambda a: np.asarray(a, np.float32)
    x2 = np.ascontiguousarray(f(x)[0])
    mem2 = np.ascontiguousarray(f(mem)[0])
    for l in range(w_in.shape[0]):
        wh, wl = split_w_in(f(w_in[l]))
        oh, oa = run_l1(x2, wh, f(attn_norm[l]), f(sb_q_norm[l]), f(sb_k_norm[l]))
        sbT, go = run_l2(oh, oa, f(gla_w_a2[l]), f(gla_b_a[l]))
        p = dict(wl=wl, wbg=f(w_br_gla[l]), wbs=f(w_br_sb[l]), wbm=f(w_br_mem[l]), wkv=f(w_mem_kv[l]), wo=f(w_o[l]),
                 wgu=f(w_gate_up[l]), wd=f(w_down[l]), attn_norm=f(attn_norm[l]), ffn_norm=f(ffn_norm[l]),
                 mem_norm=f(mem_norm[l]), gla_out_norm=f(gla_out_norm[l]), mem_q_norm=f(mem_q_norm[l]),
                 mem_k_norm=f(mem_k_norm[l]))
        x2 = run_l3(x2, sbT, go, mem2, p)
    return x2[None].astype(np.float32)
```
